# Optimizing a Trainium2 kernel written in Bass

```python
import math
import jax
import jax.numpy as jnp
from jax import lax
import numpy as np

D_MODEL = 2048
BATCH = 8
SEQ = 2048
DEPTH = 1

CTX_LEN = 256
GRID_W = 64
D_MIX = D_MODEL
D_GDN = D_MIX // 2
D_RWKV = D_MIX - D_GDN
GDN_HEAD = 128
GDN_HEADS = D_GDN // GDN_HEAD
RWKV_HEAD = 64
RWKV_HEADS = D_RWKV // RWKV_HEAD
CONV_K = 5
CHUNK = 64
LORA_W = 96
LORA_A = 96
EPS = 1e-6
GN_EPS = 64e-5

GDN_COLS = 4 * D_GDN + 4 * GDN_HEADS
RWKV_SHIFT_COLS = 3 * D_RWKV + 2 * LORA_W + 2 * LORA_A
RWKV_COLS = RWKV_SHIFT_COLS + D_RWKV
IN_COLS = GDN_COLS + RWKV_COLS

kernel_name = "hybrid_gdn_rwkv7_prefix_dit_layer"


def rmsnorm(x, g):
    xf = x.astype(jnp.float32)
    y = xf * lax.rsqrt(jnp.mean(xf * xf, axis=-1, keepdims=True) + EPS)
    return (y * g.astype(jnp.float32)).astype(x.dtype)


def l2norm(t):
    return t * lax.rsqrt(jnp.sum(t * t, axis=-1, keepdims=True) + EPS)


def flip_seq(t, d):
    return t if d == 0 else t[:, ::-1]


def centred_dwconv(x, w):
    return lax.conv_general_dilated(
        x, w[:, None, :].astype(x.dtype), window_strides=(1,),
        padding=[(CONV_K // 2, CONV_K // 2)],
        dimension_numbers=('NWC', 'WIO', 'NWC'), feature_group_count=x.shape[-1])


def q_shift_grid(x):
    b, n, ch = x.shape
    rows = n // GRID_W
    q = ch // 4
    p = jnp.pad(x.reshape(b, rows, GRID_W, ch), ((0, 0), (1, 1), (1, 1), (0, 0)))
    left = p[:, 1:-1, :-2, :q]
    right = p[:, 1:-1, 2:, q:2 * q]
    up = p[:, :-2, 1:-1, 2 * q:3 * q]
    down = p[:, 2:, 1:-1, 3 * q:]
    return jnp.concatenate([left, right, up, down], axis=-1).reshape(b, n, ch)


def shift_1d(x):
    h = x.shape[-1] // 2
    p = jnp.pad(x, ((0, 0), (1, 1), (0, 0)))
    return jnp.concatenate([p[:, :-2, :h], p[:, 2:, h:]], axis=-1)


def gated_delta_chunked(q, k, v, beta, g, s0):
    b, n, h, dk = q.shape
    dv = v.shape[-1]
    nc = n // CHUNK

    def chunk(t):
        t = jnp.moveaxis(t, 2, 1)
        return t.reshape((b, h, nc, CHUNK) + t.shape[3:])

    q, k, v, beta, g = (chunk(t) for t in (q, k, v, beta, g))
    g = jnp.cumsum(g, axis=-1)
    idx = jnp.arange(CHUNK)
    tril = idx[:, None] >= idx[None, :]
    strict = idx[:, None] > idx[None, :]
    decay = jnp.exp(jnp.where(tril, g[..., :, None] - g[..., None, :], -jnp.inf))
    kb = k * beta[..., None]
    lmat = jnp.where(strict, jnp.einsum('bhncd,bhnsd->bhncs', kb, k) * decay, 0.0)
    eye = jnp.eye(CHUNK, dtype=jnp.float32)
    tmat = lax.linalg.triangular_solve(eye + lmat, jnp.broadcast_to(eye, lmat.shape),
                                       left_side=True, lower=True, unit_diagonal=True)
    u = tmat @ (v * beta[..., None])
    wk = tmat @ (kb * jnp.exp(g)[..., None])
    attn = jnp.where(tril, jnp.einsum('bhncd,bhnsd->bhncs', q, k) * decay, 0.0)
    g_last = g[..., -1]
    k_tail = k * jnp.exp(g_last[..., None] - g)[..., None]
    q_head = q * jnp.exp(g)[..., None]

    def step(state, inp):
        u_i, wk_i, attn_i, q_i, k_i, gl_i = inp
        v_new = u_i - wk_i @ state
        o_i = q_i @ state + attn_i @ v_new
        state = state * jnp.exp(gl_i)[..., None, None] + jnp.einsum('bhcd,bhce->bhde', k_i, v_new)
        return state, o_i

    xs = tuple(jnp.moveaxis(t, 2, 0) for t in (u, wk, attn, q_head, k_tail, g_last))
    state, o = lax.scan(step, s0, xs)
    o = jnp.moveaxis(o, 0, 2).reshape(b, h, n, dv)
    return jnp.moveaxis(o, 1, 2), state


def gdn_prep(u, conv_w, a_log, dt_bias):
    b, n, _ = u.shape
    qkv = jax.nn.silu(centred_dwconv(u[..., :3 * D_GDN], conv_w)).astype(jnp.float32)
    qkv = qkv.reshape(b, n, 3, GDN_HEADS, GDN_HEAD)
    q = l2norm(qkv[:, :, 0]) * (GDN_HEAD ** -0.5)
    k = l2norm(qkv[:, :, 1])
    v = qkv[:, :, 2]
    z = u[..., 3 * D_GDN:4 * D_GDN]
    gates = u[..., 4 * D_GDN:].astype(jnp.float32).reshape(b, n, 2, 2, GDN_HEADS)
    g = -jnp.exp(a_log) * jax.nn.softplus(gates[:, :, 0] + dt_bias)
    beta = jax.nn.sigmoid(gates[:, :, 1])
    return q, k, v, z, g, beta


def gdn_output(o, z, norm_g):
    b, n = o.shape[:2]
    y = rmsnorm(o, norm_g).reshape(b, n, D_GDN) * jax.nn.silu(z.astype(jnp.float32))
    return y.astype(z.dtype)


def gdn_mixer(u_ctx, u_lat, conv_w, a_log, dt_bias, norm_g):
    streams = (gdn_prep(u_ctx, conv_w, a_log, dt_bias), gdn_prep(u_lat, conv_w, a_log, dt_bias))
    b = u_lat.shape[0]
    outs = [jnp.zeros_like(s[2]) for s in streams]
    for d in range(2):
        state = jnp.zeros((b, GDN_HEADS, GDN_HEAD, GDN_HEAD), jnp.float32)
        for i, (q, k, v, z, g, beta) in enumerate(streams):
            o, state = gated_delta_chunked(flip_seq(q, d), flip_seq(k, d), flip_seq(v, d),
                                           flip_seq(beta[:, :, d], d), flip_seq(g[:, :, d], d), state)
            outs[i] = outs[i] + flip_seq(o, d)
    return (gdn_output(outs[0], streams[0][3], norm_g), gdn_output(outs[1], streams[1][3], norm_g))


def rwkv_heads(t):
    return t.reshape(t.shape[:-1] + (RWKV_HEADS, RWKV_HEAD))


def rwkv_prep(u, shifted, mu, w0, w_up, a0, a_up, k_k, k_a):
    b, n, _ = u.shape
    xs = u[..., :RWKV_SHIFT_COLS]
    xs = (xs + mu * (shifted - xs)).astype(jnp.float32)
    r = xs[..., :D_RWKV]
    k = xs[..., D_RWKV:2 * D_RWKV]
    v = xs[..., 2 * D_RWKV:3 * D_RWKV]
    lo = 3 * D_RWKV
    wd = xs[..., lo:lo + 2 * LORA_W].reshape(b, n, 2, LORA_W)
    ad = xs[..., lo + 2 * LORA_W:].reshape(b, n, 2, LORA_A)
    w_log = -jax.nn.softplus(-(w0 + jnp.einsum('bndr,drc->bndc', jnp.tanh(wd), w_up))) - 0.5
    decay = jnp.exp(-jnp.exp(w_log))
    a = jax.nn.sigmoid(a0 + jnp.einsum('bndr,drc->bndc', ad, a_up))
    kk = l2norm(rwkv_heads(k * k_k))
    k_dir = k[:, :, None] * (1.0 + (a - 1.0) * k_a)
    z = u[..., RWKV_SHIFT_COLS:]
    return rwkv_heads(r), rwkv_heads(v), kk, rwkv_heads(decay), rwkv_heads(a), rwkv_heads(k_dir), z


def rwkv7_scan(r, w, k, v, kk, b_vec, s0):
    def step(state, inp):
        r_t, w_t, k_t, v_t, kk_t, b_t = inp
        sa = -jnp.einsum('bhvk,bhk->bhv', state, kk_t)
        state = (state * w_t[:, :, None, :] + sa[..., None] * b_t[:, :, None, :]
                 + v_t[..., None] * k_t[:, :, None, :])
        return state, jnp.einsum('bhvk,bhk->bhv', state, r_t)

    xs = tuple(jnp.moveaxis(t, 1, 0) for t in (r, w, k, v, kk, b_vec))
    state, o = lax.scan(step, s0, xs)
    return jnp.moveaxis(o, 0, 1), state


def rwkv_output(o, bonus, z, gn_g, gn_b):
    b, n = o.shape[:2]
    mu = jnp.mean(o, axis=-1, keepdims=True)
    var = jnp.mean(jnp.square(o - mu), axis=-1, keepdims=True)
    y = ((o - mu) * lax.rsqrt(var + GN_EPS)).reshape(b, n, D_RWKV) * gn_g + gn_b
    y = (y + bonus.reshape(b, n, D_RWKV)) * jax.nn.silu(z.astype(jnp.float32))
    return y.astype(z.dtype)


def rwkv_mixer(u_ctx, u_lat, mu, w0, w_up, a0, a_up, k_k, k_a, r_k, gn_g, gn_b):
    streams = (
        rwkv_prep(u_ctx, shift_1d(u_ctx[..., :RWKV_SHIFT_COLS]), mu, w0, w_up, a0, a_up, k_k, k_a),
        rwkv_prep(u_lat, q_shift_grid(u_lat[..., :RWKV_SHIFT_COLS]), mu, w0, w_up, a0, a_up, k_k, k_a),
    )
    b = u_lat.shape[0]
    outs = [jnp.zeros_like(s[0]) for s in streams]
    bonus = [jnp.zeros_like(s[0]) for s in streams]
    for d in range(2):
        state = jnp.zeros((b, RWKV_HEADS, RWKV_HEAD, RWKV_HEAD), jnp.float32)
        for i, (r, v, kk, decay, a, k_dir, z) in enumerate(streams):
            k_d = k_dir[:, :, d]
            o, state = rwkv7_scan(flip_seq(r, d), flip_seq(decay[:, :, d], d), flip_seq(k_d, d),
                                  flip_seq(v, d), flip_seq(kk, d), flip_seq(kk * a[:, :, d], d), state)
            outs[i] = outs[i] + flip_seq(o, d)
            bonus[i] = bonus[i] + jnp.sum(r * k_d * r_k, axis=-1, keepdims=True) * v
    return (rwkv_output(outs[0], bonus[0], streams[0][6], gn_g, gn_b),
            rwkv_output(outs[1], bonus[1], streams[1][6], gn_g, gn_b))


def setup_inputs(seed: int = 0) -> dict:
    key = jax.random.key(seed)
    ks = jax.random.split(key, 24)
    f32 = jnp.float32

    def nrm(k, shape, s):
        return s * jax.random.normal(k, shape, f32)

    dt = jnp.exp(jax.random.uniform(ks[9], (DEPTH, 2, GDN_HEADS), f32, math.log(1e-3), math.log(1e-1)))
    return {
        "x": nrm(ks[0], (BATCH, SEQ, D_MODEL), 1.0),
        "c": nrm(ks[1], (BATCH, D_MODEL), 1.0),
        "ctx": nrm(ks[2], (BATCH, CTX_LEN, D_MODEL), 1.0),
        "c_ctx": nrm(ks[3], (D_MODEL,), 1.0),
        "w_ada": nrm(ks[4], (DEPTH, D_MODEL, 3 * D_MODEL), 0.5 * D_MODEL ** -0.5),
        "b_ada": nrm(ks[5], (DEPTH, 3 * D_MODEL), 0.02),
        "norm_g": 1.0 + nrm(ks[6], (DEPTH, D_MODEL), 0.02),
        "w_in": nrm(ks[7], (DEPTH, D_MODEL, IN_COLS), D_MODEL ** -0.5),
        "gdn_conv_w": nrm(ks[8], (DEPTH, CONV_K, 3 * D_GDN), CONV_K ** -0.5),
        "gdn_a_log": jnp.log(jax.random.uniform(ks[10], (DEPTH, 2, GDN_HEADS), f32, 1.0, 16.0)),
        "gdn_dt_bias": dt + jnp.log(-jnp.expm1(-dt)),
        "gdn_norm_g": 1.0 + nrm(ks[11], (DEPTH, GDN_HEAD), 0.02),
        "rwkv_mu": jax.random.uniform(ks[12], (DEPTH, RWKV_SHIFT_COLS), f32),
        "rwkv_w0": jax.random.uniform(ks[13], (DEPTH, 2, D_RWKV), f32, -4.0, 1.0),
        "rwkv_w_up": nrm(ks[14], (DEPTH, 2, LORA_W, D_RWKV), 0.5 * LORA_W ** -0.5),
        "rwkv_a0": nrm(ks[15], (DEPTH, 2, D_RWKV), 0.5),
        "rwkv_a_up": nrm(ks[16], (DEPTH, 2, LORA_A, D_RWKV), 0.5 * LORA_A ** -0.5),
        "rwkv_k_k": 0.85 + nrm(ks[17], (DEPTH, D_RWKV), 0.05),
        "rwkv_k_a": 1.0 + nrm(ks[18], (DEPTH, D_RWKV), 0.05),
        "rwkv_r_k": nrm(ks[19], (DEPTH, RWKV_HEADS, RWKV_HEAD), 0.1),
        "rwkv_gn_g": 1.0 + nrm(ks[20], (DEPTH, D_RWKV), 0.02),
        "rwkv_gn_b": nrm(ks[21], (DEPTH, D_RWKV), 0.02),
        "w_out": nrm(ks[22], (DEPTH, D_MIX, D_MODEL), D_MIX ** -0.5),
        "final_norm_g": 1.0 + nrm(ks[23], (D_MODEL,), 0.02),
    }


def reference(x, c, ctx, c_ctx, w_ada, b_ada, norm_g, w_in, gdn_conv_w, gdn_a_log, gdn_dt_bias,
              gdn_norm_g, rwkv_mu, rwkv_w0, rwkv_w_up, rwkv_a0, rwkv_a_up, rwkv_k_k, rwkv_k_a,
              rwkv_r_k, rwkv_gn_g, rwkv_gn_b, w_out, final_norm_g):
    for layer in range(DEPTH):
        shift, scale, gate = jnp.split(jax.nn.silu(c) @ w_ada[layer] + b_ada[layer], 3, axis=-1)
        shift_c, scale_c, gate_c = jnp.split(jax.nn.silu(c_ctx) @ w_ada[layer] + b_ada[layer], 3, axis=-1)
        h = rmsnorm(x, norm_g[layer]) * (1.0 + scale[:, None]) + shift[:, None]
        hc = rmsnorm(ctx, norm_g[layer]) * (1.0 + scale_c) + shift_c
        u = h @ w_in[layer]
        uc = hc @ w_in[layer]
        ya_c, ya = gdn_mixer(uc[..., :GDN_COLS], u[..., :GDN_COLS], gdn_conv_w[layer],
                             gdn_a_log[layer], gdn_dt_bias[layer], gdn_norm_g[layer])
        yb_c, yb = rwkv_mixer(uc[..., GDN_COLS:], u[..., GDN_COLS:], rwkv_mu[layer], rwkv_w0[layer],
                              rwkv_w_up[layer], rwkv_a0[layer], rwkv_a_up[layer], rwkv_k_k[layer],
                              rwkv_k_a[layer], rwkv_r_k[layer], rwkv_gn_g[layer], rwkv_gn_b[layer])
        if layer + 1 < DEPTH:
            ctx = ctx + gate_c * (jnp.concatenate([ya_c, yb_c], axis=-1) @ w_out[layer])
        x = x + gate[:, None] * (jnp.concatenate([ya, yb], axis=-1) @ w_out[layer])
    return rmsnorm(x, final_norm_g)
```

```python
import numpy as np
from contextlib import ExitStack
import concourse.bass as bass
import concourse.mybir as mybir
from concourse.bass_utils import run_bass_kernel_spmd

F32 = mybir.dt.float32
BF16 = mybir.dt.bfloat16
AF = mybir.ActivationFunctionType
ALU = mybir.AluOpType
AX = mybir.AxisListType

D = 2048
SEQ = 2048
CTX = 256
T = SEQ + CTX
C = 64
NCH = T // C
NCH_CTX = CTX // C
IN_COLS = 8608
GDN_COLS = 4128
RW0 = GDN_COLS
LO0 = RW0 + 3072
ZR0 = LO0 + 384
EPS = 1e-6
GN_EPS = 64e-5
TOK_BLOCKS = [(0, 512), (512, 512), (1024, 512), (1536, 512), (2048, 256)]
ARENA_WORDS = 51200
import os as _os0
POOL_ENG = _os0.environ.get('K_POOL', 'gpsimd')
NEU_MODE = _os0.environ.get('K_NEU', 'F32')
F32R = mybir.dt.float32r


def mv(ap):
    return ap.bitcast(F32R) if NEU_MODE == 'F32R' else ap


def neu_alloc(A, words, parts=64):
    if NEU_MODE == 'BF16':
        return A.alloc(words // 2, parts=parts, dt=BF16)
    return A.alloc(words, parts=parts)


class Buf:
    __slots__ = ("name", "w", "r")

    def __init__(self, name=""):
        self.name = name
        self.w = None
        self.r = []


class Tracker:
    ENGS = ("tensor", "vector", "scalar", "gpsimd", "sync")

    def __init__(self, nc, es, n_dma_sems=10):
        self.nc = nc
        self.prog = {e: [] for e in self.ENGS}
        self.sems = {}
        self.val = {}
        self.waited = {e: {} for e in self.ENGS}
        self.es = es
        self.cur = {}
        self.gen = {}
        for e in self.ENGS:
            self.gen[e] = 0
            k = e + "#0"
            self.cur[e] = k
            self.sems[k] = es.enter_context(nc.semaphore("s_" + e + "_0"))
            self.val[k] = 0
        self.dma_keys = {}
        self.dma_rr = {}
        for q in ("sync", "gpsimd", "scalar"):
            ks = []
            for i in range(n_dma_sems):
                k = "d_%s_%d" % (q, i)
                self.sems[k] = es.enter_context(nc.semaphore(k))
                self.val[k] = 0
                ks.append(k)
            self.dma_keys[q] = ks
            self.dma_rr[q] = 0
        self.n_inst = 0
        import os as _os
        self.maxops = int(_os.environ["K_MAXOPS"]) if _os.environ.get("K_MAXOPS") else None

    def _wait(self, eng, key, value):
        if value <= 0:
            return
        if self.waited[eng].get(key, 0) >= value:
            return
        self.waited[eng][key] = value
        sem = self.sems[key]
        self.prog[eng].append(lambda e, sem=sem, value=value: e.wait_ge(sem, value))

    SEM_MAX = 12000

    def _deps(self, eng, reads, writes):
        deps = []
        for b in reads:
            if b.w is not None:
                deps.append(b.w)
        strict = (eng == "gpsimd")
        for b in writes:
            if b.w is not None and (strict or b.w[0].split("#")[0] != eng):
                deps.append(b.w)
            for d in b.r:
                if strict or d[0].split("#")[0] != eng:
                    deps.append(d)
        return deps

    def op(self, eng, fn, reads=(), writes=()):
        if self.maxops is not None and self.n_inst >= self.maxops:
            return None
        for (k, v) in self._deps(eng, reads, writes):
            self._wait(eng, k, v)
        ck = self.cur[eng]
        if self.val[ck] >= self.SEM_MAX:
            self.gen[eng] += 1
            ck = "%s#%d" % (eng, self.gen[eng])
            self.cur[eng] = ck
            self.sems[ck] = self.es.enter_context(self.nc.semaphore("s_%s_%d" % (eng, self.gen[eng])))
            self.val[ck] = 0
        self.val[ck] += 1
        v = self.val[ck]
        sem = self.sems[ck]
        self.prog[eng].append(lambda e, fn=fn, sem=sem: fn(e).then_inc(sem, 1))
        ev = (ck, v)
        for b in reads:
            b.r.append(ev)
        for b in writes:
            b.w = ev
            b.r = []
        self.n_inst += 1
        return ev

    def dma(self, q, out, in_, reads=(), writes=(), **kw):
        if self.maxops is not None and self.n_inst >= self.maxops:
            return None
        ks = self.dma_keys[q]
        k = ks[self.dma_rr[q] % len(ks)]
        self.dma_rr[q] += 1
        self._wait(q, k, self.val[k])
        for (kk, v) in self._deps(q, reads, writes):
            self._wait(q, kk, v)
        self.val[k] += 16
        v = self.val[k]
        sem = self.sems[k]
        self.prog[q].append(
            lambda e, out=out, in_=in_, sem=sem, kw=kw: e.dma_start(out=out, in_=in_, **kw).then_inc(sem, 16))
        ev = (k, v)
        for b in reads:
            b.r.append(ev)
        for b in writes:
            b.w = ev
            b.r = []
        self.n_inst += 1
        return ev

    def barrier(self):
        for e in self.ENGS:
            for k, v in list(self.val.items()):
                if k.split("#")[0] != e:
                    self._wait(e, k, v)

    def run(self):
        nc = self.nc
        print("TRACKER n_inst", self.n_inst, {k: v for k, v in self.val.items() if "#" in k})
        self.barrier()
        with nc.Block() as block:
            @block.sync
            def _(e):
                for th in self.prog["sync"]:
                    th(e)

            @block.tensor
            def _(e):
                for th in self.prog["tensor"]:
                    th(e)

            @block.vector
            def _(e):
                for th in self.prog["vector"]:
                    th(e)

            @block.scalar
            def _(e):
                for th in self.prog["scalar"]:
                    th(e)

            @block.gpsimd
            def _(e):
                for th in self.prog["gpsimd"]:
                    th(e)


class Arena:
    def __init__(self, ap):
        self.ap = ap
        self.base = 0
        self.ptr = 0

    def alloc(self, words, parts=128, dt=F32, shape=None):
        words = (words + 7) // 8 * 8
        assert self.ptr + words <= ARENA_WORDS, ("arena overflow", self.ptr, words)
        a = self.ap[0:parts, self.ptr:self.ptr + words]
        self.ptr += words
        if dt == BF16:
            a = a.bitcast(BF16)
        return a

    def mark(self):
        return self.ptr

    def reset(self, m):
        self.ptr = m


def col_tiles():
    blocks = []
    for i in range(8):
        blocks.append((i * 512, 512, 128))
    blocks.append((4096, 32, 32))
    for i in range(6):
        blocks.append((RW0 + i * 512, 512, 128))
    blocks.append((LO0, 384, 96))
    for i in range(2):
        blocks.append((ZR0 + i * 512, 512, 128))
    return blocks


def build(stop_after=None, dbg=(), feed_uT=False):
    nc = bass.Bass("TRN2", target_bir_lowering=False)
    ins = {}

    def din(name, shape, dt=F32):
        ins[name] = nc.dram_tensor(name, list(shape), dt, kind="ExternalInput").ap()
        return ins[name]

    def scr(name, shape, dt=F32):
        kind = "ExternalOutput" if name in dbg else "Internal"
        return nc.dram_tensor(name, list(shape), dt, kind=kind).ap()

    x_in = din("x", [SEQ, D])
    x_res_in = x_in
    if feed_uT:
        gate_dbg = din("gate_dbg", [D])
    if not feed_uT:
        ctx_in = din("ctx", [CTX, D])
        cc_in = din("cc", [128, 16, 2])
        w_ada = din("w_ada", [D, 3 * D])
        b_ada2 = din("b_ada2", [2, 3 * D])
        normg_T = din("normg_T", [128, 16])
        w_in = din("w_in", [D, IN_COLS])
    ident_in = din("ident", [128, 128])
    out_ap = nc.dram_tensor("out", [SEQ, D], F32, kind="ExternalOutput").ap()

    uT = din("uT_in", [IN_COLS, T]) if feed_uT else scr("uT", [IN_COLS, T])
    conv_wT = din("conv_wT", [128, 24, 5])
    gpar_in = din("gpar", [32, 8])
    gmask_in = din("gmask", [64, 6, 64])
    qT_s = scr("qT_s", [NCH, 128, 8, 64])
    kT_s = scr("kT_s", [NCH, 128, 8, 64])
    ktok_s = scr("ktok_s", [NCH, 64, 8, 128])
    vtok_s = scr("vtok_s", [NCH, 64, 8, 128])
    o_s = scr("o_s", [2, SEQ, D])
    muT_in = din("muT", [128, 28])
    smask_in = din("smask", [128, 28, 6])
    rpar_in = din("rpar", [128, 56])
    bones_in = din("bones", [128, 128])
    w_up_in = din("w_up", [2, 96, 1024])
    a_up_in = din("a_up", [2, 96, 1024])
    rmask_in = din("rmask", [128, 3, 128])
    w_out_in = din("w_out", [D, D])
    gng_in = din("gng", [128])
    gn_g_in = din("gn_g", [1024])
    gn_b_in = din("gn_b", [1024])
    fg_in = din("fg", [D])
    rw_fm = scr("rw_fm", [2, NCH, 64, 16, 5, 64])
    rw_tm = scr("rw_tm", [2, NCH, 64, 16, 3, 64])
    v_tm = scr("v_tm", [NCH, 64, 16, 64])
    bonus_s = scr("bonus_s", [1024, SEQ])
    dbgA = scr("dbgA", [128, 16, 4])

    es = ExitStack()
    with es:
        tr = Tracker(nc, es)
        arena_t = es.enter_context(nc.sbuf_tensor("arena", [128, ARENA_WORDS], F32))
        psum_t = es.enter_context(nc.psum_tensor("psum", [128, 8, 512], F32))
        A = Arena(arena_t)
        PS = [psum_t[:, i, :] for i in range(8)]
        PSB = [Buf("ps%d" % i) for i in range(8)]

        ident = A.alloc(128); b_ident = Buf()
        tr.dma("sync", ident, ident_in, writes=[b_ident])
        ones = A.alloc(128); b_ones = Buf()
        tr.op("vector", lambda e: e.memset(ones, 1.0), writes=[b_ones])
        epsc = A.alloc(8); b_epsc = Buf()
        tr.op("vector", lambda e: e.memset(epsc[:, 0:1], EPS), writes=[b_epsc])
        tr.op("vector", lambda e: e.memset(epsc[:, 1:2], GN_EPS), writes=[b_epsc])
        gate_bc = A.alloc(D); b_gate = Buf()
        ABT = A.alloc(64); b_ABT = Buf()
        ABTv = ABT.rearrange("p (f s k) -> p f s k", f=16, s=2)
        persist_mark = A.mark()

        if feed_uT:
            tr.dma("sync", gate_bc, gate_dbg.partition_broadcast(128), writes=[b_gate])
        if not feed_uT:
            cc = A.alloc(32); b_cc = Buf()
            sc = A.alloc(32); b_sc = Buf()
            tr.dma("sync", cc, cc_in.rearrange("p a b -> p (a b)"), writes=[b_cc])
            tr.op("scalar", lambda e: e.activation(sc, cc, AF.Silu), reads=[b_cc], writes=[b_sc])
            scv = sc.rearrange("p (a b) -> p a b", b=2)
            mod_rows = A.alloc(3 * D, parts=2); b_mod = Buf()
            bada = A.alloc(3 * D, parts=2); b_bada = Buf()
            tr.dma("sync", bada, b_ada2, writes=[b_bada])
            wbuf = [A.alloc(3072), A.alloc(3072)]
            b_wbuf = [Buf(), Buf()]
            for half in range(2):
                for kt in range(16):
                    wb = wbuf[kt % 2]
                    tr.dma("sync", wb, w_ada[kt * 128:(kt + 1) * 128, half * 3072:(half + 1) * 3072],
                           writes=[b_wbuf[kt % 2]])
                    for cb in range(6):
                        tr.op("tensor",
                              lambda e, cb=cb, kt=kt, wb=wb: e.matmul(PS[cb][0:2, :], scv[:, kt, :],
                                                                     wb[:, cb * 512:(cb + 1) * 512],
                                                                     start=(kt == 0), stop=(kt == 15)),
                              reads=[b_sc, b_wbuf[kt % 2]], writes=[PSB[cb]])
                for cb in range(6):
                    c0 = half * 3072 + cb * 512
                    tr.op("vector",
                          lambda e, cb=cb, c0=c0: e.tensor_tensor(mod_rows[:, c0:c0 + 512], PS[cb][0:2, :],
                                                                  bada[:, c0:c0 + 512], ALU.add),
                          reads=[PSB[cb], b_bada], writes=[b_mod])
            for j in range(32):
                tr.op("tensor",
                      lambda e, j=j: e.matmul(PS[6][:, j * 2:(j + 1) * 2], mod_rows[0:2, j * 128:(j + 1) * 128],
                                              ident[0:2, 0:2], start=True, stop=True),
                      reads=[b_mod, b_ident], writes=[PSB[6]])
            modT = A.alloc(64); b_modT = Buf()
            tr.op("vector", lambda e: e.tensor_copy(modT, PS[6][:, 0:64]), reads=[PSB[6]], writes=[b_modT])
            modTv = modT.rearrange("p (j s) -> p j s", s=2)
            gT = A.alloc(16); b_gT = Buf()
            tr.dma("sync", gT, normg_T, writes=[b_gT])
            tmpA = A.alloc(32); b_tmpA = Buf()
            tmpAv = tmpA.rearrange("p (j s) -> p j s", s=2)
            tr.op("vector", lambda e: e.tensor_scalar(tmpA, modT[:, 32:64], 1.0, None, ALU.add),
                  reads=[b_modT], writes=[b_tmpA])
            tr.op("vector",
                  lambda e: e.tensor_tensor(ABTv[:, :, :, 0], tmpAv, gT.unsqueeze(2).broadcast_to([128, 16, 2]), ALU.mult),
                  reads=[b_tmpA, b_gT], writes=[b_ABT])
            tr.op("vector", lambda e: e.tensor_copy(ABTv[:, :, :, 1], modTv[:, 0:16, :]),
                  reads=[b_modT], writes=[b_ABT])
            for nb in range(4):
                tr.op("tensor",
                      lambda e, nb=nb: e.matmul(PS[7][:, :], ones[0:1, 0:128],
                                                mod_rows[0:1, 4096 + nb * 512:4096 + (nb + 1) * 512],
                                                start=True, stop=True),
                      reads=[b_mod, b_ones], writes=[PSB[7]])
                tr.op("vector", lambda e, nb=nb: e.tensor_copy(gate_bc[:, nb * 512:(nb + 1) * 512], PS[7][:, :]),
                      reads=[PSB[7]], writes=[b_gate])
            if "dbgA" in dbg:
                tr.dma("sync", dbgA.rearrange("p a b -> p (a b)"), ABT, reads=[b_ABT])
            tr.barrier()
            A.reset(persist_mark)
            if stop_after == "A":
                tr.run()
                return nc

            hT = A.alloc(16 * T // 2, dt=BF16)
            hTv = hT.rearrange("p (f t) -> p f t", f=16)
            b_hT = Buf()
            phaseB_mark = A.mark()
            xb = [A.alloc(D), A.alloc(D)]; b_xb = [Buf(), Buf()]
            xn = [A.alloc(D), A.alloc(D)]; b_xn = [Buf(), Buf()]
            ss = A.alloc(24); b_ss = Buf()
            rstd = A.alloc(24); b_rstd = Buf()
            tr.op("vector", lambda e: e.memset(ss, 0.0), writes=[b_ss])
            for tt in range(18):
                s = 1 if tt < 2 else 0
                src = ctx_in[tt * 128:(tt + 1) * 128, :] if tt < 2 else x_in[(tt - 2) * 128:(tt - 1) * 128, :]
                p = tt % 2
                tr.dma("sync", xb[p], src, writes=[b_xb[p]])
                tr.op("scalar",
                      lambda e, p=p, tt=tt: e.activation(xn[p], xb[p], AF.Square, accum_out=ss[:, tt:tt + 1]),
                      reads=[b_xb[p]], writes=[b_xn[p], b_ss])
                tr.op("scalar",
                      lambda e, tt=tt: e.activation(ss[:, tt:tt + 1], ss[:, tt:tt + 1], AF.Sqrt, bias=epsc[:, 0:1], scale=1.0 / D),
                      reads=[b_ss, b_epsc], writes=[b_ss])
                tr.op("vector",
                      lambda e, tt=tt: e.reciprocal(rstd[:, tt:tt + 1], ss[:, tt:tt + 1]),
                      reads=[b_ss], writes=[b_rstd])
                tr.op("vector",
                      lambda e, p=p, tt=tt: e.tensor_scalar(xn[p], xb[p], rstd[:, tt:tt + 1], None, ALU.mult),
                      reads=[b_xb[p], b_rstd], writes=[b_xn[p]])
                for f in range(16):
                    bank = (p * 4) + f // 4
                    q = f % 4
                    tr.op("tensor",
                          lambda e, bank=bank, q=q, f=f, p=p: e.transpose(PS[bank][:, q * 128:(q + 1) * 128],
                                                                            xn[p][:, f * 128:(f + 1) * 128], ident),
                          reads=[b_xn[p], b_ident], writes=[PSB[bank]])
                for f in range(16):
                    bank = (p * 4) + f // 4
                    q = f % 4
                    dst = hTv[:, f, tt * 128:(tt + 1) * 128]
                    if f % 2 == 0:
                        tr.op("vector",
                              lambda e, bank=bank, q=q, f=f, s=s, dst=dst: e.tensor_scalar(
                                  dst, PS[bank][:, q * 128:(q + 1) * 128], ABTv[:, f, s, 0:1], ABTv[:, f, s, 1:2],
                                  ALU.mult, ALU.add),
                              reads=[PSB[bank], b_ABT], writes=[b_hT])
                    else:
                        tr.op("scalar",
                              lambda e, bank=bank, q=q, f=f, s=s, dst=dst: e.activation(
                                  dst, PS[bank][:, q * 128:(q + 1) * 128], AF.Identity,
                                  bias=ABTv[:, f, s, 1:2], scale=ABTv[:, f, s, 0:1]),
                              reads=[PSB[bank], b_ABT], writes=[b_hT])
            tr.barrier()
            A.reset(phaseB_mark)
            if stop_after == "B":
                if "uT" in dbg:
                    pass

            wst = [A.alloc(16 * 512), A.alloc(16 * 512)]; b_wst = [Buf(), Buf()]
            wbf = [A.alloc(16 * 512 // 2, dt=BF16), A.alloc(16 * 512 // 2, dt=BF16)]; b_wbf = [Buf(), Buf()]
            rowbuf = [A.alloc(T), A.alloc(T)]; b_row = [Buf(), Buf()]
            blocks = col_tiles()
            if stop_after == "C1":
                blocks = blocks[:1] + blocks[8:9]
            tcount = 0
            ecount = 0
            for bi, (c0, ncols, tsz) in enumerate(blocks):
                p = bi % 2
                wsv = wst[p][:, 0:16 * ncols].rearrange("p (k c) -> p k c", k=16)
                wbv = wbf[p][:, 0:16 * ncols].rearrange("p (k c) -> p k c", k=16)
                tr.dma("sync", wsv, w_in[:, c0:c0 + ncols].rearrange("(k p) c -> p k c", p=128), writes=[b_wst[p]])
                tr.op("gpsimd", lambda e, wsv=wsv, wbv=wbv: e.tensor_copy(wbv[:, 0:8, :], wsv[:, 0:8, :]),
                      reads=[b_wst[p]], writes=[b_wbf[p]])
                tr.op("vector", lambda e, wsv=wsv, wbv=wbv: e.tensor_copy(wbv[:, 8:16, :], wsv[:, 8:16, :]),
                      reads=[b_wst[p]], writes=[b_wbf[p]])
                for ti in range(ncols // tsz):
                    rb = rowbuf[tcount % 2]; brb = b_row[tcount % 2]
                    tcount += 1
                    for (t0, tn) in TOK_BLOCKS:
                        bank = ecount % 8
                        ecount += 1
                        for kt in range(16):
                            tr.op("tensor",
                                  lambda e, bank=bank, kt=kt, ti=ti, tsz=tsz, t0=t0, tn=tn, wbv=wbv: e.matmul(
                                      PS[bank][0:tsz, 0:tn], wbv[:, kt, ti * tsz:(ti + 1) * tsz],
                                      hTv[:, kt, t0:t0 + tn], start=(kt == 0), stop=(kt == 15)),
                                  reads=[b_wbf[p], b_hT], writes=[PSB[bank]])
                        if ecount % 2 == 0:
                            tr.op("vector",
                                  lambda e, bank=bank, tsz=tsz, t0=t0, tn=tn, rb=rb: e.tensor_copy(
                                      rb[0:tsz, t0:t0 + tn], PS[bank][0:tsz, 0:tn]),
                                  reads=[PSB[bank]], writes=[brb])
                        else:
                            tr.op("scalar",
                                  lambda e, bank=bank, tsz=tsz, t0=t0, tn=tn, rb=rb: e.copy(
                                      rb[0:tsz, t0:t0 + tn], PS[bank][0:tsz, 0:tn]),
                                  reads=[PSB[bank]], writes=[brb])
                    r0 = c0 + ti * tsz
                    tr.dma("gpsimd", uT[r0:r0 + tsz, :], rb[0:tsz, :], reads=[brb])
            tr.barrier()
            A.reset(persist_mark)
            if stop_after in ("C", "C1"):
                tr.run()
                return nc


        NEG = -30000.0
        gmask = A.alloc(6 * 64, parts=64); b_gmask = Buf()
        tr.dma("sync", gmask, gmask_in.rearrange("p a b -> p (a b)"), writes=[b_gmask])
        gmv = gmask.rearrange("p (a b) -> p a b", a=6)
        gcum = A.alloc(576, parts=64); b_gcum = Buf()
        beta = A.alloc(576, parts=64); b_beta = Buf()
        nbeta = A.alloc(576, parts=64); b_nbeta = Buf()
        bgs = A.alloc(576, parts=64); b_bgs = Buf()
        kts = A.alloc(576, parts=64); b_kts = Buf()
        gcv = gcum.rearrange("p (n c) -> p n c", c=16)
        betav = beta.rearrange("p (n c) -> p n c", c=16)
        nbetav = nbeta.rearrange("p (n c) -> p n c", c=16)
        bgv = bgs.rearrange("p (n c) -> p n c", c=16)
        ktsv = kts.rearrange("p (n c) -> p n c", c=16)
        gdn_mark = A.mark()

        gt = A.alloc(T, parts=32); b_gt = Buf()
        gpar = A.alloc(8, parts=32); b_gpar = Buf()
        tr.dma("sync", gt, uT[4096:4128, :], writes=[b_gt])
        tr.dma("sync", gpar, gpar_in, writes=[b_gpar])
        xa = A.alloc(T, parts=32); b_xa = Buf()
        t1g = A.alloc(T, parts=32); b_t1g = Buf()
        t2g = A.alloc(T, parts=32); b_t2g = Buf()
        sg = A.alloc(T, parts=32); b_sg = Buf()
        coef = A.alloc(8, parts=32); b_coef = Buf()
        tr.op("vector", lambda e: e.tensor_scalar(xa, gt, gpar[:, 0:1], None, ALU.add), reads=[b_gt, b_gpar], writes=[b_xa])
        tr.op("scalar", lambda e: e.activation(t1g, xa, AF.Abs), reads=[b_xa], writes=[b_t1g])
        tr.op("scalar", lambda e: e.activation(t1g, t1g, AF.Exp, scale=-1.0), reads=[b_t1g], writes=[b_t1g])
        tr.op("scalar", lambda e: e.activation(t1g, t1g, AF.Ln, bias=gpar[:, 4:5], scale=1.0), reads=[b_t1g, b_gpar], writes=[b_t1g])
        tr.op("vector", lambda e: e.tensor_scalar(t2g, xa, 0.0, None, ALU.max), reads=[b_xa], writes=[b_t2g])
        tr.op("vector", lambda e: e.tensor_tensor(t2g, t2g, t1g, ALU.add), reads=[b_t2g, b_t1g], writes=[b_t2g])
        tr.op("scalar", lambda e: e.activation(coef[:, 0:1], gpar[:, 1:2], AF.Exp), reads=[b_gpar], writes=[b_coef])
        tr.op("vector", lambda e: e.tensor_scalar(coef[:, 1:2], coef[:, 0:1], gpar[:, 2:3], -1.0, ALU.mult, ALU.mult),
              reads=[b_coef, b_gpar], writes=[b_coef])
        tr.op("vector", lambda e: e.tensor_scalar(t2g, t2g, coef[:, 1:2], None, ALU.mult), reads=[b_t2g, b_coef], writes=[b_t2g])
        tr.op("scalar", lambda e: e.activation(sg, gt, AF.Sigmoid), reads=[b_gt], writes=[b_sg])
        tr.op("vector", lambda e: e.scalar_tensor_tensor(t2g, sg, gpar[:, 3:4], t2g, ALU.mult, ALU.add),
              reads=[b_sg, b_gpar, b_t2g], writes=[b_t2g])
        gtok = A.alloc(36 * 32, parts=64); b_gtok = Buf()
        gtv = gtok.rearrange("p (n c) -> p n c", c=32)
        for n in range(NCH):
            bank = n // 16
            q = n % 16
            tr.op("tensor",
                  lambda e, n=n, bank=bank, q=q: e.transpose(PS[bank][0:64, q * 32:(q + 1) * 32],
                                                            t2g[0:32, n * 64:(n + 1) * 64], ident[0:32, 0:32]),
                  reads=[b_t2g, b_ident], writes=[PSB[bank]])
        for bank in range(3):
            nn = 16 if bank < 2 else 4
            tr.op("vector",
                  lambda e, bank=bank, nn=nn: e.tensor_copy(gtok[:, bank * 512:bank * 512 + nn * 32], PS[bank][0:64, 0:nn * 32]),
                  reads=[PSB[bank]], writes=[b_gtok])
        tmpg = A.alloc(288, parts=64); b_tmpg = Buf()
        for d in range(2):
            ps_c = PS[3 + d][0:64, 0:288].rearrange("p (n h) -> p n h", h=8)
            tr.op("tensor",
                  lambda e, d=d, ps_c=ps_c: e.matmul(ps_c, gmv[:, 4 + d, :], gtv[:, :, 8 * d:8 * d + 8], start=True, stop=True),
                  reads=[b_gmask, b_gtok], writes=[PSB[3 + d]])
            tr.op("vector", lambda e, d=d, ps_c=ps_c: e.tensor_copy(gcv[:, :, 8 * d:8 * d + 8], ps_c),
                  reads=[PSB[3 + d]], writes=[b_gcum])
            ps_l = PS[5 + d][0:64, 0:288].rearrange("p (n h) -> p n h", h=8)
            tr.op("tensor",
                  lambda e, d=d, ps_l=ps_l: e.matmul(ps_l, ones[0:64, 0:64], gtv[:, :, 8 * d:8 * d + 8], start=True, stop=True),
                  reads=[b_ones, b_gtok], writes=[PSB[5 + d]])
            tr.op("vector",
                  lambda e, d=d, ps_l=ps_l: e.tensor_tensor(ktsv[:, :, 8 * d:8 * d + 8], ps_l, gcv[:, :, 8 * d:8 * d + 8], ALU.subtract),
                  reads=[PSB[5 + d], b_gcum], writes=[b_kts])
        tr.op("scalar", lambda e: e.activation(kts, kts, AF.Exp), reads=[b_kts], writes=[b_kts])
        tr.op("vector", lambda e: e.tensor_copy(betav, gtv[:, :, 16:32]), reads=[b_gtok], writes=[b_beta])
        tr.op("vector", lambda e: e.tensor_scalar(nbetav, gtv[:, :, 16:32], -1.0, None, ALU.mult), reads=[b_gtok], writes=[b_nbeta])
        tr.op("scalar", lambda e: e.activation(bgs, gcum, AF.Exp), reads=[b_gcum], writes=[b_bgs])
        tr.op("vector", lambda e: e.tensor_tensor(bgs, bgs, beta, ALU.mult), reads=[b_bgs, b_beta], writes=[b_bgs])
        if "dbgG" in dbg:
            dbgG = scr("dbgG", [64, 4, 576])
            tr.dma("sync", dbgG[:, 0, :], gcum, reads=[b_gcum])
            tr.dma("sync", dbgG[:, 1, :], beta, reads=[b_beta])
            tr.dma("sync", dbgG[:, 2, :], bgs, reads=[b_bgs])
            tr.dma("sync", dbgG[:, 3, :], kts, reads=[b_kts])
        tr.barrier()
        A.reset(gdn_mark)

        cw = A.alloc(120); b_cw = Buf()
        tr.dma("sync", cw, conv_wT.rearrange("p a b -> p (a b)"), writes=[b_cw])
        cwv = cw.rearrange("p (a b) -> p a b", b=5)
        bufA = [[A.alloc(T) for _ in range(3)] for _ in range(2)]
        bufB = [[A.alloc(T) for _ in range(3)] for _ in range(2)]
        b_bufA = [[Buf() for _ in range(3)] for _ in range(2)]
        b_bufB = [[Buf() for _ in range(3)] for _ in range(2)]
        tokb = [A.alloc(NCH * 128, parts=64) for _ in range(2)]
        b_tokb = [Buf(), Buf()]
        segs = [(0, CTX), (CTX, T)]
        n_heads_d = 8 if stop_after != "D1" else 1
        if _os0.environ.get("K_SKIP_GDN"):
            n_heads_d = 0
        pbank = 0
        for h in range(n_heads_d):
            st = h % 2
            for qi in range(3):
                raw = bufA[st][qi]; acc = bufB[st][qi]
                b_raw = b_bufA[st][qi]; b_acc = b_bufB[st][qi]
                ct = qi * 8 + h
                r0 = qi * 1024 + h * 128
                tr.dma("sync", raw, uT[r0:r0 + 128, :], writes=[b_raw])
                tr.op("vector", lambda e, raw=raw, acc=acc, ct=ct: e.tensor_scalar(acc, raw, cwv[:, ct, 2:3], None, ALU.mult),
                      reads=[b_raw, b_cw], writes=[b_acc])
                for j in (0, 1, 3, 4):
                    sh = j - 2
                    for (a, b) in segs:
                        t0 = max(a, a - sh); t1 = min(b, b - sh)
                        tr.op("vector",
                              lambda e, raw=raw, acc=acc, ct=ct, j=j, sh=sh, t0=t0, t1=t1: e.scalar_tensor_tensor(
                                  acc[:, t0:t1], raw[:, t0 + sh:t1 + sh], cwv[:, ct, j:j + 1], acc[:, t0:t1], ALU.mult, ALU.add),
                              reads=[b_raw, b_cw, b_acc], writes=[b_acc])
                tr.op("scalar", lambda e, raw=raw, acc=acc: e.activation(raw, acc, AF.Silu), reads=[b_acc], writes=[b_raw])
                if qi < 2:
                    tr.op("gpsimd", lambda e, raw=raw, acc=acc: e.tensor_tensor(acc, raw, raw, ALU.mult), reads=[b_raw], writes=[b_acc])
                    for (t0, tn) in TOK_BLOCKS:
                        bank = pbank % 8; pbank += 1
                        tr.op("tensor",
                              lambda e, bank=bank, acc=acc, t0=t0, tn=tn: e.matmul(PS[bank][:, 0:tn], ones[:, 0:128], acc[:, t0:t0 + tn],
                                                                                   start=True, stop=True),
                              reads=[b_acc, b_ones], writes=[PSB[bank]])
                        tr.op("scalar",
                              lambda e, bank=bank, acc=acc, t0=t0, tn=tn: e.activation(acc[:, t0:t0 + tn], PS[bank][:, 0:tn], AF.Sqrt,
                                                                                       bias=epsc[:, 0:1], scale=1.0),
                              reads=[PSB[bank], b_epsc, b_acc], writes=[b_acc])
                    tr.op("vector", lambda e, acc=acc: e.reciprocal(acc, acc), reads=[b_acc], writes=[b_acc])
                    scl = (128.0 ** -0.5) if qi == 0 else 1.0
                    tr.op("vector", lambda e, raw=raw, acc=acc, scl=scl: e.scalar_tensor_tensor(raw, raw, scl, acc, ALU.mult, ALU.mult),
                          reads=[b_raw, b_acc], writes=[b_raw])
                    dst = qT_s if qi == 0 else kT_s
                    tr.dma("gpsimd", dst[:, :, h, :].rearrange("c p t -> p c t"), raw.rearrange("p (c t) -> p c t", t=64),
                           reads=[b_raw])
                if qi >= 1:
                    tb = tokb[qi - 1]; b_tb = b_tokb[qi - 1]
                    for n in range(NCH):
                        bank = pbank % 8
                        q4 = n % 4
                        tr.op("tensor",
                              lambda e, bank=bank, q4=q4, raw=raw, n=n: e.transpose(PS[bank][0:64, q4 * 128:(q4 + 1) * 128],
                                                                                   raw[:, n * 64:(n + 1) * 64], ident),
                              reads=[b_raw, b_ident], writes=[PSB[bank]])
                        if q4 == 3:
                            n0 = n - 3
                            if (n // 4) % 2 == 0:
                                tr.op("vector",
                                      lambda e, bank=bank, tb=tb, n0=n0: e.tensor_copy(tb[:, n0 * 128:(n0 + 4) * 128], PS[bank][0:64, :]),
                                      reads=[PSB[bank]], writes=[b_tb])
                            else:
                                tr.op("scalar",
                                      lambda e, bank=bank, tb=tb, n0=n0: e.copy(tb[:, n0 * 128:(n0 + 4) * 128], PS[bank][0:64, :]),
                                      reads=[PSB[bank]], writes=[b_tb])
                            pbank += 1
                    dst = ktok_s if qi == 1 else vtok_s
                    tr.dma("gpsimd", dst[:, :, h, :].rearrange("c p f -> p c f"), tb.rearrange("p (c f) -> p c f", f=128),
                           reads=[b_tb])
        tr.barrier()
        A.reset(gdn_mark)
        if stop_after in ("D", "D1"):
            tr.run()
            return nc

        _sw = int(_os0.environ.get("K_SWAP", "0"))

        def hb(slot, i):
            return psum_t[:, slot * 2 + i // 2, (i % 2) * 256:(i % 2) * 256 + 256]

        def hbank(slot, i):
            return psum_t[:, slot * 2 + i // 2, :]

        HBB = []
        for sl in range(4):
            bk = [Buf("hbk%d_%d" % (sl, i)) for i in range(2)]
            HBB.append([bk[i // 2] for i in range(4)])
        gorder = [list(range(NCH)), [3, 2, 1, 0] + list(range(NCH - 1, 3, -1))]

        def v3(ap, a):
            return ap.rearrange("p (a b) -> p a b", a=a)

        GL = []
        for sl in range(4):
            L = {}
            L["in"] = []
            for par in range(1):
                L["in"].append(dict(q=A.alloc(256), k=A.alloc(256), kt=A.alloc(512, parts=64), vt=A.alloc(512, parts=64),
                                    bq=Buf(), bk=Buf(), bkt=Buf(), bvt=Buf()))
            for nm in ("dg", "a0", "t1", "P0f", "AT"):
                L[nm] = A.alloc(256, parts=64); L["b_" + nm] = Buf()
            for nm in ("P0", "P1", "PT0", "PT1", "Y"):
                L[nm] = neu_alloc(A, 256); L["b_" + nm] = Buf()
            for nm in ("args", "DD", "ktail", "X", "vn", "osb"):
                L[nm] = A.alloc(512, parts=64); L["b_" + nm] = Buf()
            for nm in ("vb", "Kbg"):
                L[nm] = A.alloc(256, parts=64, dt=BF16); L["b_" + nm] = Buf()
            L["Yb"] = A.alloc(128, parts=64, dt=BF16); L["b_Yb"] = Buf()
            L["qb"] = A.alloc(128, dt=BF16); L["b_qb"] = Buf()
            L["kb"] = A.alloc(128, dt=BF16); L["b_kb"] = Buf()
            for nm in ("eg",):
                L[nm] = A.alloc(256); L["b_" + nm] = Buf()
            for nm in ("qh", "Wk"):
                L[nm] = A.alloc(128, dt=BF16); L["b_" + nm] = Buf()
            L["ATb"] = A.alloc(128, parts=64, dt=BF16)
            L["vnb"] = A.alloc(256, parts=64, dt=BF16)
            L["ktb"] = A.alloc(256, parts=64, dt=BF16)
            L["S"] = A.alloc(512); L["b_S"] = [Buf()]
            L["Sb"] = A.alloc(256, dt=BF16); L["b_Sb"] = [Buf()]
            GL.append(L)
            tr.op("vector", lambda e, L=L: e.memset(L["S"], 0.0), writes=L["b_S"])
            tr.op("vector", lambda e, L=L: e.memset(L["Sb"], 0.0), writes=L["b_Sb"])
        ident64_bc = ident[0:64, 0:64].unsqueeze(1).broadcast_to([64, 4, 64])
        pass_ctr = [0, 0, 0, 0]

        def neumann(sl, L, HB, ia, ib, ic):
            Y3 = v3(L["Y"], 4)

            def sq(Pc3, PTc3, bPc, bPTc, want_T):
                for h in range(4):
                    tr.op("tensor", lambda e, h=h: e.matmul(hb(sl, ia)[0:64, h * 64:(h + 1) * 64], mv(PTc3[:, h, :]), mv(Pc3[:, h, :]), start=True, stop=True),
                          reads=[bPc, bPTc], writes=[HB[ia]])
                if want_T:
                    for h in range(4):
                        tr.op("tensor", lambda e, h=h: e.matmul(hb(sl, ib)[0:64, h * 64:(h + 1) * 64], mv(Pc3[:, h, :]), mv(PTc3[:, h, :]), start=True, stop=True),
                              reads=[bPc, bPTc], writes=[HB[ib]])

            def evac(nxt, want_T):
                Pn, PTn = L["P%d" % nxt], L["PT%d" % nxt]
                tr.op("scalar", lambda e: e.copy(mv(Pn), hb(sl, ia)[0:64, :]), reads=[HB[ia]], writes=[L["b_P%d" % nxt]])
                if want_T:
                    tr.op("scalar", lambda e: e.copy(mv(PTn), hb(sl, ib)[0:64, :]), reads=[HB[ib]], writes=[L["b_PT%d" % nxt]])

            cur = 0
            sq(v3(L["P0"], 4), v3(L["PT0"], 4), L["b_P0"], L["b_PT0"], True)
            yield
            evac(1, True)
            cur = 1
            for r in range(1, 6):
                Pc3, PTc3 = v3(L["P%d" % cur], 4), v3(L["PT%d" % cur], 4)
                bPc, bPTc = L["b_P%d" % cur], L["b_PT%d" % cur]
                for h in range(4):
                    tr.op("tensor", lambda e, h=h, Pc3=Pc3: e.matmul(hb(sl, ic)[0:64, h * 64:(h + 1) * 64], mv(Pc3[:, h, :]), mv(Y3[:, h, :]), start=True, stop=True),
                          reads=[bPc, L["b_Y"]], writes=[HB[ic]])
                if r < 5:
                    sq(Pc3, PTc3, bPc, bPTc, r < 4)
                yield
                tr.op("vector", lambda e: e.tensor_tensor(mv(L["Y"]), L["Y"], hb(sl, ic)[0:64, :], ALU.add),
                      reads=[L["b_Y"], HB[ic]], writes=[L["b_Y"]])
                if r < 5:
                    evac(1 - cur, r < 4)
                    cur = 1 - cur

        def gdn_pass(s, d, half, sl):
            L = GL[sl]
            HB = HBB[sl]
            n = gorder[d][s]
            lat = n >= NCH_CTX
            last = 63 if d == 0 else 0
            hs0 = 4 * half
            col0 = 8 * d + 4 * half
            par = 0
            pass_ctr[sl] += 1
            I = L["in"][par]
            qc, kc, ktc, vtc = I["q"], I["k"], I["kt"], I["vt"]
            qc3, kc3 = v3(qc, 4), v3(kc, 4)
            ktc3, vtc3 = v3(ktc, 4), v3(vtc, 4)
            tr.dma("sync", qc3, qT_s[n, :, hs0:hs0 + 4, :], writes=[I["bq"]])
            tr.dma("sync", kc3, kT_s[n, :, hs0:hs0 + 4, :], writes=[I["bk"]])
            tr.dma("sync", ktc3, ktok_s[n, :, hs0:hs0 + 4, :], writes=[I["bkt"]])
            tr.dma("sync", vtc3, vtok_s[n, :, hs0:hs0 + 4, :], writes=[I["bvt"]])
            gc_bc = gcv[:, n, col0:col0 + 4].unsqueeze(2).broadcast_to([64, 4, 64])
            for h in range(4):
                tr.op("tensor", lambda e, h=h: e.matmul(hb(sl, 0)[0:64, h * 64:(h + 1) * 64], kc3[:, h, :], kc3[:, h, :],
                                                        start=True, stop=True),
                      reads=[I["bk"]], writes=[HB[0]])
            tr.op(POOL_ENG, lambda e: e.tensor_copy(L["qb"], qc), reads=[I["bq"]], writes=[L["b_qb"]])
            tr.op(POOL_ENG, lambda e: e.tensor_copy(L["kb"], kc), reads=[I["bk"]], writes=[L["b_kb"]])
            qb3, kb3 = v3(L["qb"], 4), v3(L["kb"], 4)
            for h in range(4):
                tr.op("tensor", lambda e, h=h: e.matmul(hb(sl, 1)[0:64, h * 64:(h + 1) * 64], kb3[:, h, :], qb3[:, h, :],
                                                        start=True, stop=True),
                      reads=[L["b_kb"], L["b_qb"]], writes=[HB[1]])
            tr.op(POOL_ENG, lambda e: e.tensor_tensor(v3(L["dg"], 4), ident64_bc, gc_bc, ALU.mult),
                  reads=[b_ident, b_gcum], writes=[L["b_dg"]])
            tr.op("tensor", lambda e: e.matmul(hb(sl, 2), ones[0:64, 0:128], L["dg"], start=True, stop=True),
                  reads=[b_ones, L["b_dg"]], writes=[HB[2]])
            yield
            tr.op("vector", lambda e: e.tensor_copy(L["eg"], hb(sl, 2)), reads=[HB[2]], writes=[L["b_eg"]])
            tr.op("scalar", lambda e: e.activation(L["eg"], L["eg"], AF.Exp), reads=[L["b_eg"]], writes=[L["b_eg"]])
            tr.op("vector", lambda e: e.tensor_tensor(v3(L["a0"], 4), v3(hb(sl, 2)[0:64, :], 4), gc_bc, ALU.subtract),
                  reads=[HB[2], b_gcum], writes=[L["b_a0"]])
            tr.op("vector",
                  lambda e: e.scalar_tensor_tensor(v3(L["args"][:, 0:256], 4), v3(L["a0"], 4), -1.0,
                                                   gmv[:, 2 * d, :].unsqueeze(1).broadcast_to([64, 4, 64]), ALU.mult, ALU.add),
                  reads=[L["b_a0"], b_gmask], writes=[L["b_args"]])
            tr.op(POOL_ENG,
                  lambda e: e.tensor_tensor(v3(L["args"][:, 256:512], 4), v3(L["a0"], 4),
                                            gmv[:, 2 * d + 1, :].unsqueeze(1).broadcast_to([64, 4, 64]), ALU.add),
                  reads=[L["b_a0"], b_gmask], writes=[L["b_args"]])
            tr.op("scalar", lambda e: e.activation(L["DD"], L["args"], AF.Exp), reads=[L["b_args"]], writes=[L["b_DD"]])
            tr.op(POOL_ENG, lambda e: e.tensor_tensor(L["qh"], qc, L["eg"], ALU.mult), reads=[I["bq"], L["b_eg"]], writes=[L["b_qh"]])
            tr.op("vector",
                  lambda e: e.tensor_tensor(v3(L["t1"], 4), v3(hb(sl, 0)[0:64, :], 4),
                                            nbetav[:, n, col0:col0 + 4].unsqueeze(2).broadcast_to([64, 4, 64]), ALU.mult),
                  reads=[HB[0], b_nbeta], writes=[L["b_t1"]])
            tr.op(POOL_ENG, lambda e: e.tensor_tensor(L["P0f"], L["t1"], L["DD"][:, 0:256], ALU.mult),
                  reads=[L["b_t1"], L["b_DD"]], writes=[L["b_P0f"]])
            tr.op(POOL_ENG, lambda e: e.tensor_tensor(mv(L["P0"]), L["t1"], L["DD"][:, 0:256], ALU.mult),
                  reads=[L["b_t1"], L["b_DD"]], writes=[L["b_P0"]])
            tr.op("vector", lambda e: e.tensor_tensor(L["ATb"], hb(sl, 1)[0:64, :], L["DD"][:, 256:512], ALU.mult),
                  reads=[HB[1], L["b_DD"]], writes=[L["b_AT"]])
            P03 = v3(L["P0f"], 4)
            for h in range(4):
                tr.op("tensor", lambda e, h=h: e.transpose(hb(sl, 3)[0:64, h * 64:(h + 1) * 64], P03[:, h, :], ident[0:64, 0:64]),
                      reads=[L["b_P0f"], b_ident], writes=[HB[3]])
            yield
            tr.op("vector", lambda e: e.tensor_copy(mv(L["PT0"]), hb(sl, 3)[0:64, :]), reads=[HB[3]], writes=[L["b_PT0"]])
            tr.op("vector", lambda e: e.tensor_tensor(mv(v3(L["Y"], 4)), v3(hb(sl, 3)[0:64, :], 4), ident64_bc, ALU.add),
                  reads=[HB[3], b_ident], writes=[L["b_Y"]])
            tr.op(POOL_ENG,
                  lambda e: e.tensor_tensor(mv(v3(L["vb"], 4)), vtc3, betav[:, n, col0:col0 + 4].unsqueeze(2).broadcast_to([64, 4, 128]), ALU.mult),
                  reads=[I["bvt"], b_beta], writes=[L["b_vb"]])
            tr.op(POOL_ENG,
                  lambda e: e.tensor_tensor(mv(v3(L["Kbg"], 4)), ktc3, bgv[:, n, col0:col0 + 4].unsqueeze(2).broadcast_to([64, 4, 128]), ALU.mult),
                  reads=[I["bkt"], b_bgs], writes=[L["b_Kbg"]])
            tr.op(POOL_ENG,
                  lambda e: e.tensor_tensor(v3(L["ktb"], 4), ktc3, ktsv[:, n, col0:col0 + 4].unsqueeze(2).broadcast_to([64, 4, 128]), ALU.mult),
                  reads=[I["bkt"], b_kts], writes=[L["b_ktail"]])
            Y3 = v3(L["Y"], 4)
            yield from neumann(sl, L, HB, 0, 1, 2)
            vb3, Kbg3, kta3 = v3(L["vb"], 4), v3(L["Kbg"], 4), v3(L["ktb"], 4)
            tr.op(POOL_ENG, lambda e: e.tensor_copy(L["Yb"], L["Y"]), reads=[L["b_Y"]], writes=[L["b_Yb"]])
            Yb3 = v3(L["Yb"], 4)
            for h in range(4):
                tr.op("tensor", lambda e, h=h: e.matmul(hbank(sl, 0)[0:64, h * 128:(h + 1) * 128], Yb3[:, h, :], vb3[:, h, :],
                                                        start=True, stop=True),
                      reads=[L["b_Yb"], L["b_vb"]], writes=[HB[0], HB[1]])
            for h in range(4):
                tr.op("tensor", lambda e, h=h: e.matmul(hb(sl, 2)[:, h * 64:(h + 1) * 64], Kbg3[:, h, :], Yb3[:, h, :],
                                                        start=True, stop=True),
                      reads=[L["b_Yb"], L["b_Kbg"]], writes=[HB[2]])
            yield
            tr.op("vector", lambda e: e.tensor_copy(L["X"], hbank(sl, 0)[0:64, :]), reads=[HB[0], HB[1]], writes=[L["b_X"]])
            tr.op("vector", lambda e: e.tensor_scalar(L["Wk"], hb(sl, 2), -1.0, None, ALU.mult), reads=[HB[2]], writes=[L["b_Wk"]])
            S = L["S"]
            bS = L["b_S"][0]
            S3 = v3(S, 4)
            Sb = L["Sb"]
            bSb = L["b_Sb"][0]
            Sb3 = v3(Sb, 4)
            Wk3, qh3, AT3, vn3 = v3(L["Wk"], 4), v3(L["qh"], 4), v3(L["ATb"], 4), v3(L["vnb"], 4)
            for h in range(4):
                tr.op("tensor", lambda e, h=h: e.matmul(hbank(sl, 2)[0:64, h * 128:(h + 1) * 128], Wk3[:, h, :], Sb3[:, h, :],
                                                        start=True, stop=True),
                      reads=[L["b_Wk"], bSb], writes=[HB[2], HB[3]])
            yield
            tr.op("vector", lambda e: e.tensor_tensor(L["vnb"], hbank(sl, 2)[0:64, :], L["X"], ALU.add),
                  reads=[HB[2], HB[3], L["b_X"]], writes=[L["b_vn"]])
            if lat:
                for h in range(4):
                    tr.op("tensor", lambda e, h=h: e.matmul(hbank(sl, 0)[0:64, h * 128:(h + 1) * 128], qh3[:, h, :], Sb3[:, h, :],
                                                            start=True, stop=False),
                          reads=[L["b_qh"], bSb], writes=[HB[0], HB[1]])
                    tr.op("tensor", lambda e, h=h: e.matmul(hbank(sl, 0)[0:64, h * 128:(h + 1) * 128], AT3[:, h, :], vn3[:, h, :],
                                                            start=False, stop=True),
                          reads=[L["b_AT"], L["b_vn"]], writes=[HB[0], HB[1]])
            for h in range(4):
                tr.op("tensor", lambda e, h=h: e.matmul(hbank(sl, 2)[:, h * 128:(h + 1) * 128], kta3[:, h, :], vn3[:, h, :],
                                                        start=True, stop=True),
                      reads=[L["b_ktail"], L["b_vn"]], writes=[HB[2], HB[3]])
            yield
            egl = v3(L["eg"], 4)[:, :, last:last + 1].broadcast_to([128, 4, 128])
            tr.op(POOL_ENG, lambda e: e.tensor_tensor(S3, S3, egl, ALU.mult), reads=[bS, L["b_eg"]], writes=[bS])
            tr.op("vector", lambda e: e.tensor_tensor(S, S, hbank(sl, 2), ALU.add), reads=[bS, HB[2], HB[3]], writes=[bS])
            tr.op(POOL_ENG, lambda e: e.tensor_copy(Sb, S), reads=[bS], writes=[bSb])
            if lat:
                tr.op("vector", lambda e: e.tensor_copy(L["osb"], hbank(sl, 0)[0:64, :]), reads=[HB[0], HB[1]], writes=[L["b_osb"]])
                tok0 = (n - NCH_CTX) * 64
                tr.dma("gpsimd", o_s[d, tok0:tok0 + 64, hs0 * 128:(hs0 + 4) * 128], L["osb"], reads=[L["b_osb"]])
            yield

        def run_slots(streams, max_yields=None):
            active = list(streams)
            cnt = 0
            while active:
                for it in list(active):
                    if max_yields is not None and cnt >= max_yields:
                        return
                    cnt += 1
                    try:
                        next(it)
                    except StopIteration:
                        active.remove(it)

        def gdn_stream(d, half, nsteps):
            for s in range(nsteps):
                yield from gdn_pass(s, d, half, 2 * d + half)

        nsteps_g = NCH if stop_after != "E1" else 6
        if _os0.environ.get("K_SKIP_GDN"):
            nsteps_g = 0
        import os as _os
        _my = _os.environ.get("K_MAXY")
        run_slots([gdn_stream(0, 0, nsteps_g), gdn_stream(1, 0, nsteps_g), gdn_stream(0, 1, nsteps_g), gdn_stream(1, 1, nsteps_g)],
                  int(_my) if _my else None)
        tr.barrier()
        A.reset(gdn_mark)
        if stop_after in ("E", "E1"):
            tr.run()
            return nc

        rw_mark = A.mark()
        NEG_E = -0.6065306597126334
        muT = A.alloc(32); b_muT = Buf()
        smk = A.alloc(28 * 6); b_smk = Buf()
        c0t = A.alloc(32); b_c0 = Buf()
        cmt = A.alloc(28 * 6); b_cm = Buf()
        rpar = A.alloc(56); b_rpar = Buf()
        bones = A.alloc(128); b_bones = Buf()
        tr.dma("sync", muT[:, 0:28], muT_in, writes=[b_muT])
        tr.dma("sync", smk, smask_in.rearrange("p a b -> p (a b)"), writes=[b_smk])
        tr.dma("sync", rpar, rpar_in, writes=[b_rpar])
        tr.dma("sync", bones, bones_in, writes=[b_bones])
        tr.op("vector", lambda e: e.tensor_scalar(c0t[:, 0:28], muT[:, 0:28], -1.0, 1.0, ALU.mult, ALU.add), reads=[b_muT], writes=[b_c0])
        cmv = cmt.rearrange("p (a b) -> p a b", b=6)
        tr.op("vector",
              lambda e: e.tensor_tensor(cmv, smk.rearrange("p (a b) -> p a b", b=6), muT[:, 0:28].unsqueeze(2).broadcast_to([128, 28, 6]), ALU.mult),
              reads=[b_smk, b_muT], writes=[b_cm])
        oneska = A.alloc(8); b_oneska = Buf()
        tr.op("vector", lambda e: e.tensor_scalar(oneska, rpar[:, 40:48], -1.0, 1.0, ALU.mult, ALU.add), reads=[b_rpar], writes=[b_oneska])
        rmask = A.alloc(T); b_rmask = Buf()
        tr.op("gpsimd", lambda e: e.memset(rmask, 1.0), writes=[b_rmask])
        tr.op("gpsimd", lambda e: e.memset(rmask.rearrange("p (n t) -> p n t", t=64)[:, :, 0:1], 0.0), writes=[b_rmask])
        loT = [A.alloc(T, parts=96) for _ in range(4)]; b_loT = [Buf() for _ in range(4)]
        W = [A.alloc(T) for _ in range(13)]; bW = [Buf() for _ in range(13)]
        stg = [A.alloc(18 * 128, parts=64)] * 2; b_stg = [Buf()] * 2

        def lerp(dst, src, bd, bs, ti, parts, ch0, ch1):
            tr.op("vector", lambda e: e.tensor_scalar(dst, src, c0t[0:parts, ti:ti + 1], None, ALU.mult),
                  reads=[bs, b_c0], writes=[bd])
            qs = set(range(ch0 // 864, (ch1 - 1) // 864 + 1))
            hs = set(range(ch0 // 1728, (ch1 - 1) // 1728 + 1))
            dl, sl_ = dst[:, CTX:T], src[:, CTX:T]
            dl3 = dl.rearrange("p (r c) -> p r c", c=64)
            sl3 = sl_.rearrange("p (r c) -> p r c", c=64)
            terms = []
            if 0 in qs:
                terms.append((dl3[:, :, 1:64], sl3[:, :, 0:63], 0))
            if 1 in qs:
                terms.append((dl3[:, :, 0:63], sl3[:, :, 1:64], 1))
            if 2 in qs:
                terms.append((dl[:, 64:SEQ], sl_[:, 0:SEQ - 64], 2))
            if 3 in qs:
                terms.append((dl[:, 0:SEQ - 64], sl_[:, 64:SEQ], 3))
            if 0 in hs:
                terms.append((dst[:, 1:CTX], src[:, 0:CTX - 1], 4))
            if 1 in hs:
                terms.append((dst[:, 0:CTX - 1], src[:, 1:CTX], 5))
            for (o_, i_, ci) in terms:
                tr.op("vector",
                      lambda e, o_=o_, i_=i_, ci=ci: e.scalar_tensor_tensor(o_, i_, cmv[0:parts, ti, ci:ci + 1], o_, ALU.mult, ALU.add),
                      reads=[bs, b_cm, bd], writes=[bd])

        for m in range(4):
            raw = W[m][0:96, :]
            tr.dma("sync", raw, uT[LO0 + 96 * m:LO0 + 96 * (m + 1), :], writes=[bW[m]])
            lerp(loT[m], raw, b_loT[m], bW[m], 24 + m, 96, 3072 + 96 * m, 3072 + 96 * (m + 1))
            if m < 2:
                tr.op("scalar", lambda e, m=m: e.activation(loT[m], loT[m], AF.Tanh), reads=[b_loT[m]], writes=[b_loT[m]])
        lw = [A.alloc(128, parts=96) for _ in range(4)]; b_lw = [Buf() for _ in range(4)]

        n_tiles_f = 8 if stop_after != "F1" else 1
        if _os0.environ.get("K_FT"):
            n_tiles_f = int(_os0.environ["K_FT"])
        lbank = 0
        tbank = 0
        for j in range(n_tiles_f):
            Wr, Wk_, Wv, Wkk, Wbs = W[0], W[1], W[2], W[3], W[4]
            for qi, dstW in enumerate((0, 1, 2)):
                raw = W[5 + qi]
                r0 = RW0 + qi * 1024 + 128 * j
                tr.dma("sync", raw, uT[r0:r0 + 128, :], writes=[bW[5 + qi]])
                ch0 = qi * 1024 + 128 * j
                lerp(W[dstW], raw, bW[dstW], bW[5 + qi], qi * 8 + j, 128, ch0, ch0 + 128)
            for d in range(2):
                tr.dma("sync", lw[d], w_up_in[d, :, 128 * j:128 * (j + 1)], writes=[b_lw[d]])
                tr.dma("sync", lw[2 + d], a_up_in[d, :, 128 * j:128 * (j + 1)], writes=[b_lw[2 + d]])
            tr.op("vector", lambda e, j=j: e.tensor_scalar(Wkk, Wk_, rpar[:, 32 + j:33 + j], None, ALU.mult),
                  reads=[bW[1], b_rpar], writes=[bW[3]])
            tr.op("gpsimd", lambda e: e.tensor_tensor(W[5], Wkk, Wkk, ALU.mult), reads=[bW[3]], writes=[bW[5]])
            for bi_, (t0, tn) in enumerate(TOK_BLOCKS):
                bank = 4 + bi_ % 2
                tr.op("tensor", lambda e, bank=bank, t0=t0, tn=tn: e.matmul(PS[bank][:, 0:tn], bones, W[5][:, t0:t0 + tn], start=True, stop=True),
                      reads=[bW[5], b_bones], writes=[PSB[bank]])
                tr.op("scalar", lambda e, bank=bank, t0=t0, tn=tn: e.activation(W[6][:, t0:t0 + tn], PS[bank][:, 0:tn], AF.Sqrt, bias=epsc[:, 0:1], scale=1.0),
                      reads=[PSB[bank], b_epsc], writes=[bW[6]])
            tr.op("vector", lambda e: e.reciprocal(W[6], W[6]), reads=[bW[6]], writes=[bW[6]])
            tr.op("gpsimd", lambda e: e.tensor_tensor(Wkk, Wkk, W[6], ALU.mult), reads=[bW[3], bW[6]], writes=[bW[3]])
            def transposes_out(srcW, bsrc, dst_fn, nm):
                nonlocal tbank
                for hf in range(2):
                    sg_ = stg[tbank % 2]; bsg = b_stg[tbank % 2]
                    for c4 in range(5):
                        cs = list(range(hf * 18 + c4 * 4, min(hf * 18 + c4 * 4 + 4, hf * 18 + 18)))
                        if not cs:
                            continue
                        bank = 6 + (c4 % 2)
                        for qi_, n_ in enumerate(cs):
                            tr.op("tensor", lambda e, bank=bank, qi_=qi_, n_=n_: e.transpose(PS[bank][0:64, qi_ * 128:(qi_ + 1) * 128],
                                                                                          srcW[:, n_ * 64:(n_ + 1) * 64], ident),
                                  reads=[bsrc, b_ident], writes=[PSB[bank]])
                        l0 = (cs[0] - hf * 18) * 128
                        tr.op("vector", lambda e, bank=bank, l0=l0, ncs=len(cs), sg_=sg_: e.tensor_copy(sg_[:, l0:l0 + ncs * 128], PS[bank][0:64, 0:ncs * 128]),
                              reads=[PSB[bank]], writes=[bsg])
                    for hh_ in range(2):
                        tr.dma("gpsimd", dst_fn(hf, hh_), sg_.rearrange("p (n h k) -> p n h k", n=18, h=2)[:, :, hh_, :], reads=[bsg])
                    tbank += 1

            transposes_out(Wv, bW[2], lambda hf, hh_: v_tm[hf * 18:(hf + 1) * 18, :, 2 * j + hh_, :].rearrange("n p k -> p n k"), "v")
            for d in range(2):
                Wld, Wcum, Wal, WE, Wb, Wkd, Wo1, Wo2 = W[5], W[6], W[7], W[8], W[9], W[10], W[11], W[12]
                bld, bcum, bal, bE, bb_, bkd, bo1, bo2 = bW[5], bW[6], bW[7], bW[8], bW[9], bW[10], bW[11], bW[12]
                for (t0, tn) in TOK_BLOCKS:
                    bank = lbank % 3; lbank += 1
                    tr.op("tensor", lambda e, bank=bank, t0=t0, tn=tn, d=d: e.matmul(PS[bank][:, 0:tn], lw[d], loT[d][:, t0:t0 + tn], start=True, stop=True),
                          reads=[b_lw[d], b_loT[d]], writes=[PSB[bank]])
                    tr.op("scalar", lambda e, bank=bank, t0=t0, tn=tn, d=d, j=j: e.activation(Wld[:, t0:t0 + tn], PS[bank][:, 0:tn], AF.Sigmoid,
                                                                                           bias=rpar[:, d * 8 + j:d * 8 + j + 1], scale=1.0),
                          reads=[PSB[bank], b_rpar], writes=[bld])
                    bank = lbank % 3; lbank += 1
                    tr.op("tensor", lambda e, bank=bank, t0=t0, tn=tn, d=d: e.matmul(PS[bank][:, 0:tn], lw[2 + d], loT[2 + d][:, t0:t0 + tn], start=True, stop=True),
                          reads=[b_lw[2 + d], b_loT[2 + d]], writes=[PSB[bank]])
                    tr.op("scalar", lambda e, bank=bank, t0=t0, tn=tn, d=d, j=j: e.activation(Wal[:, t0:t0 + tn], PS[bank][:, 0:tn], AF.Sigmoid,
                                                                                           bias=rpar[:, 16 + d * 8 + j:16 + d * 8 + j + 1], scale=1.0),
                          reads=[PSB[bank], b_rpar], writes=[bal])
                tr.op("gpsimd", lambda e: e.tensor_scalar(Wld, Wld, NEG_E, None, ALU.mult), reads=[bld], writes=[bld])
                tr.op("vector", lambda e: e.tensor_tensor_scan(Wcum, rmask, Wld, 0.0, ALU.mult, ALU.add), reads=[b_rmask, bld], writes=[bcum])
                cum3 = Wcum.rearrange("p (n t) -> p n t", t=64)
                if d == 1:
                    tot_bc = cum3[:, :, 63:64].broadcast_to([128, NCH, 64])
                    tr.op("vector", lambda e: e.tensor_tensor(WE.rearrange("p (n t) -> p n t", t=64), tot_bc, cum3, ALU.subtract), reads=[bcum], writes=[bE])
                    tr.op("gpsimd", lambda e: e.tensor_tensor(Wcum, WE, Wld, ALU.add), reads=[bE, bld], writes=[bcum])
                last = 63 if d == 0 else 0
                cumC_bc = cum3[:, :, last:last + 1].broadcast_to([128, NCH, 64])
                tr.op("gpsimd", lambda e: e.tensor_tensor(Wb, Wkk, Wal, ALU.mult), reads=[bW[3], bal], writes=[bb_])
                tr.op("vector", lambda e, j=j: e.tensor_scalar(Wal, Wal, rpar[:, 40 + j:41 + j], oneska[:, j:j + 1], ALU.mult, ALU.add),
                      reads=[bal, b_rpar, b_oneska], writes=[bal])
                tr.op("gpsimd", lambda e: e.tensor_tensor(Wkd, Wk_, Wal, ALU.mult), reads=[bW[1], bal], writes=[bkd])
                if d == 0:
                    tr.op("vector", lambda e, j=j: e.scalar_tensor_tensor(Wbs, Wr, rpar[:, 48 + j:49 + j], Wkd, ALU.mult, ALU.mult),
                          reads=[bW[0], b_rpar, bkd], writes=[bW[4]])
                else:
                    tr.op("vector", lambda e, j=j: e.scalar_tensor_tensor(Wal, Wr, rpar[:, 48 + j:49 + j], Wkd, ALU.mult, ALU.mult),
                          reads=[bW[0], b_rpar, bkd], writes=[bal])
                    tr.op("gpsimd", lambda e: e.tensor_tensor(Wbs, Wbs, Wal, ALU.add), reads=[bW[4], bal], writes=[bW[4]])

                def fm_out(srcW, bsrc, var, d=d, j=j):
                    for hh in range(2):
                        tr.dma("gpsimd", rw_fm[d, :, :, 2 * j + hh, var, :].rearrange("n p t -> p n t"),
                               srcW[hh * 64:(hh + 1) * 64, :].rearrange("p (n t) -> p n t", t=64), reads=[bsrc])

                tr.op("scalar", lambda e: e.activation(WE, Wcum, AF.Exp), reads=[bcum], writes=[bE])
                tr.op("gpsimd", lambda e: e.tensor_tensor(Wo1, Wr, WE, ALU.mult), reads=[bW[0], bE], writes=[bo1])
                fm_out(Wo1, bo1, 1)
                tr.op("gpsimd", lambda e: e.tensor_tensor(Wld, Wcum, Wld, ALU.subtract), reads=[bcum, bld], writes=[bld])
                tr.op("scalar", lambda e: e.activation(WE, Wld, AF.Exp), reads=[bld], writes=[bE])
                tr.op("vector", lambda e: e.scalar_tensor_tensor(Wo2, Wkk, -1.0, WE, ALU.mult, ALU.mult), reads=[bW[3], bE], writes=[bo2])
                fm_out(Wo2, bo2, 0)
                transposes_out(Wo2, bo2, lambda hf, hh_, d=d: rw_tm[d, hf * 18:(hf + 1) * 18, :, 2 * j + hh_, 0, :].rearrange("n p k -> p n k"), "A")
                tr.op("scalar", lambda e: e.activation(WE, Wcum, AF.Exp, scale=-1.0), reads=[bcum], writes=[bE])
                tr.op("gpsimd", lambda e: e.tensor_tensor(Wo1, Wb, WE, ALU.mult), reads=[bb_, bE], writes=[bo1])
                fm_out(Wo1, bo1, 2)
                tr.op("vector", lambda e: e.tensor_tensor(Wo2, Wkd, WE, ALU.mult), reads=[bkd, bE], writes=[bo2])
                fm_out(Wo2, bo2, 3)
                tr.op("vector", lambda e, cumC_bc=cumC_bc: e.tensor_tensor(Wld.rearrange("p (n t) -> p n t", t=64), cumC_bc, cum3, ALU.subtract), reads=[bcum], writes=[bld])
                tr.op("scalar", lambda e: e.activation(WE, Wld, AF.Exp), reads=[bld], writes=[bE])
                tr.op("gpsimd", lambda e: e.tensor_tensor(Wo1, Wb, WE, ALU.mult), reads=[bb_, bE], writes=[bo1])
                transposes_out(Wo1, bo1, lambda hf, hh_, d=d: rw_tm[d, hf * 18:(hf + 1) * 18, :, 2 * j + hh_, 1, :].rearrange("n p k -> p n k"), "B")
                tr.op("vector", lambda e: e.tensor_tensor(Wo2, Wkd, WE, ALU.mult), reads=[bkd, bE], writes=[bo2])
                transposes_out(Wo2, bo2, lambda hf, hh_, d=d: rw_tm[d, hf * 18:(hf + 1) * 18, :, 2 * j + hh_, 2, :].rearrange("n p k -> p n k"), "K")
                tr.op("scalar", lambda e, cumC_bc=cumC_bc: e.activation(WE.rearrange("p (n t) -> p n t", t=64), cumC_bc, AF.Exp), reads=[bcum], writes=[bE])
                fm_out(WE, bE, 4)
            for bi_, (t0, tn) in enumerate(TOK_BLOCKS):
                if t0 + tn <= CTX:
                    continue
                tr.op("tensor", lambda e, t0=t0, tn=tn: e.matmul(PS[3][:, 0:tn], bones, Wbs[:, t0:t0 + tn], start=True, stop=True),
                      reads=[bW[4], b_bones], writes=[PSB[3]])
                tr.op("vector", lambda e, t0=t0, tn=tn: e.tensor_tensor(W[11][:, t0:t0 + tn], PS[3][:, 0:tn], Wv[:, t0:t0 + tn], ALU.mult),
                      reads=[PSB[3], bW[2]], writes=[bW[11]])
            tr.dma("gpsimd", bonus_s[128 * j:128 * (j + 1), :], W[11][:, CTX:T], reads=[bW[11]])
        tr.barrier()
        A.reset(rw_mark)
        if stop_after in ("F", "F1"):
            tr.run()
            return nc

        rmk = A.alloc(3 * 128); b_rmk = Buf()
        tr.dma("sync", rmk, rmask_in.rearrange("p a b -> p (a b)"), writes=[b_rmk])
        rmkv = rmk.rearrange("p (a b) -> p a b", a=3)
        RL = []
        for sl in range(4):
            L = {}
            L["in"] = []
            for par in range(2):
                L["in"].append(dict(fm=A.alloc(4 * 5 * 64, parts=64), At=A.alloc(256, parts=64), Kt=A.alloc(256), X=A.alloc(256),
                                    Vt=A.alloc(256, parts=64), bfm=Buf(), bAt=Buf(), bKt=Buf(), bX=Buf(), bVt=Buf()))
            L["Mt"] = A.alloc(512); L["b_Mt"] = Buf()
            for nm in ("osb", "AakT"):
                L[nm] = A.alloc(256, parts=64); L["b_" + nm] = Buf()
            for nm in ("P0", "P1", "PT0", "PT1", "Y"):
                L[nm] = neu_alloc(A, 256); L["b_" + nm] = Buf()
            for nm in ("M1", "Atb", "Yb", "AakTb", "Vtb"):
                L[nm] = A.alloc(128, parts=64, dt=BF16); L["b_" + nm] = Buf()
            L["Wk"] = A.alloc(256, parts=64, dt=BF16); L["b_Wk"] = Buf()
            L["vn"] = A.alloc(128, dt=BF16); L["b_vn"] = Buf()
            L["Rb"] = A.alloc(128, parts=64, dt=BF16); L["b_Rb"] = Buf()
            L["ATb"] = A.alloc(128, dt=BF16); L["b_ATb"] = Buf()
            L["Ktb"] = A.alloc(128, dt=BF16); L["b_Ktb"] = Buf()
            L["S"] = A.alloc(512, parts=64); L["b_S"] = [Buf() for _ in range(2)]
            L["Sb"] = A.alloc(256, parts=64, dt=BF16); L["b_Sb"] = [Buf() for _ in range(2)]
            RL.append(L)
            tr.op("vector", lambda e, L=L: e.memset(L["S"], 0.0), writes=L["b_S"])
            tr.op("vector", lambda e, L=L: e.memset(L["Sb"], 0.0), writes=L["b_Sb"])
            tr.op("vector", lambda e, L=L: e.memset(L["Wk"], 0.0), writes=[L["b_Wk"]])
        rpass_ctr = [0, 0, 0, 0]

        def rwkv_pass(s, d, hq, sl):
            L = RL[sl]
            HB = HBB[sl]
            n = gorder[d][s]
            lat = n >= NCH_CTX
            h0 = 4 * hq
            par = rpass_ctr[sl] % 2
            rpass_ctr[sl] += 1
            I = L["in"][par]
            fm = I["fm"].rearrange("p (h v t) -> p h v t", h=4, v=5)
            At3, Kt3, X3 = v3(I["At"], 4), v3(I["Kt"], 4), v3(I["X"], 4)
            tr.dma("sync", fm, rw_fm[d, n, :, h0:h0 + 4, :, :], writes=[I["bfm"]])
            tr.dma("sync", At3, rw_tm[d, n, :, h0:h0 + 4, 0, :], writes=[I["bAt"]])
            tr.dma("sync", Kt3[0:64], rw_tm[d, n, :, h0:h0 + 4, 1, :], writes=[I["bKt"]])
            tr.dma("sync", Kt3[64:128], rw_tm[d, n, :, h0:h0 + 4, 2, :], writes=[I["bKt"]])
            tr.dma("sync", X3[64:128], v_tm[n, :, h0:h0 + 4, :], writes=[I["bX"]])
            Vt3 = v3(I["Vt"], 4)
            tr.dma("sync", Vt3, v_tm[n, :, h0:h0 + 4, :], writes=[I["bVt"]])
            for h in range(4):
                tr.op("tensor", lambda e, h=h: e.matmul(hbank(sl, 0)[:, h * 128:(h + 1) * 128],
                                                        fm[:, h, 2:4, :].rearrange("p v t -> p (v t)"),
                                                        fm[:, h, 0:2, :].rearrange("p v t -> p (v t)"), start=True, stop=True),
                      reads=[I["bfm"]], writes=[HB[0], HB[1]])
            for h in range(4):
                tr.op("tensor", lambda e, h=h: e.matmul(hb(sl, 2)[0:64, h * 64:(h + 1) * 64], fm[:, h, 0, :], fm[:, h, 2, :], start=True, stop=True),
                      reads=[I["bfm"]], writes=[HB[2]])
            for h in range(4):
                tr.op("tensor", lambda e, h=h: e.matmul(hb(sl, 3)[0:64, h * 64:(h + 1) * 64], fm[:, h, 3, :], fm[:, h, 0, :], start=True, stop=True),
                      reads=[I["bfm"]], writes=[HB[3]])
            yield
            Mt4 = v3(L["Mt"], 4)
            tr.op(POOL_ENG, lambda e: e.tensor_copy(L["Vtb"], I["Vt"]), reads=[I["bVt"]], writes=[L["b_Vtb"]])
            tr.op("vector", lambda e: e.tensor_tensor(v3(L["AakTb"], 4), v3(hb(sl, 3)[0:64, :], 4),
                                                      rmkv[0:64, d, 0:64].unsqueeze(1).broadcast_to([64, 4, 64]), ALU.mult),
                  reads=[HB[3], b_rmk], writes=[L["b_AakTb"]])
            tr.op("vector", lambda e: e.tensor_tensor(Mt4, v3(hbank(sl, 0), 4), rmkv[:, d, :].unsqueeze(1).broadcast_to([128, 4, 128]), ALU.mult),
                  reads=[HB[0], HB[1], b_rmk], writes=[L["b_Mt"]])
            tr.op("vector", lambda e: e.tensor_tensor(mv(v3(L["P0"], 4)), v3(hb(sl, 2)[0:64, :], 4),
                                                      rmkv[0:64, 2, 64 * d:64 * d + 64].unsqueeze(1).broadcast_to([64, 4, 64]), ALU.mult),
                  reads=[HB[2], b_rmk], writes=[L["b_P0"]])
            tr.op(POOL_ENG, lambda e: e.tensor_copy(mv(v3(L["PT0"], 4)), Mt4[0:64, :, 0:64]), reads=[L["b_Mt"]], writes=[L["b_PT0"]])
            tr.op(POOL_ENG, lambda e: e.tensor_tensor(mv(v3(L["Y"], 4)), Mt4[0:64, :, 0:64], ident64_bc, ALU.add),
                  reads=[L["b_Mt"], b_ident], writes=[L["b_Y"]])
            Y3 = v3(L["Y"], 4)
            yield from neumann(sl, L, HB, 0, 1, 2)
            Atb3 = v3(L["Atb"], 4)
            tr.op(POOL_ENG, lambda e: e.tensor_copy(mv(L["Atb"]), I["At"]), reads=[I["bAt"]], writes=[L["b_Atb"]])
            tr.op(POOL_ENG, lambda e: e.tensor_copy(L["Yb"], L["Y"]), reads=[L["b_Y"]], writes=[L["b_Yb"]])
            Yb3 = v3(L["Yb"], 4)
            for h in range(4):
                tr.op("tensor", lambda e, h=h: e.matmul(hb(sl, 3)[0:64, h * 64:(h + 1) * 64], Atb3[:, h, :], Yb3[:, h, :], start=True, stop=True),
                      reads=[L["b_Atb"], L["b_Yb"]], writes=[HB[3]])
            for h in range(4):
                tr.op("tensor", lambda e, h=h: e.matmul(hb(sl, 0)[0:64, h * 64:(h + 1) * 64], v3(L["AakTb"], 4)[:, h, :], v3(L["Vtb"], 4)[:, h, :], start=True, stop=True),
                      reads=[L["b_AakTb"], L["b_Vtb"]], writes=[HB[0]])
            yield
            Wk4 = L["Wk"].rearrange("p (h c) -> p h c", h=4)
            tr.op("vector", lambda e: e.tensor_copy(Wk4[:, :, 0:64], v3(hb(sl, 3)[0:64, :], 4)),
                  reads=[HB[3]], writes=[L["b_Wk"]])
            tr.op("vector", lambda e: e.tensor_copy(mv(L["M1"]), hb(sl, 0)[0:64, :]), reads=[HB[0]], writes=[L["b_M1"]])
            M13 = v3(L["M1"], 4)
            for h in range(4):
                tr.op("tensor", lambda e, h=h: e.matmul(hb(sl, 1)[0:64, h * 64:(h + 1) * 64], Yb3[:, h, :], M13[:, h, :], start=True, stop=True),
                      reads=[L["b_Yb"], L["b_M1"]], writes=[HB[1]])
            yield
            tr.op("vector", lambda e: e.tensor_copy(I["X"][0:64, :], hb(sl, 1)[0:64, :]), reads=[HB[1]], writes=[I["bX"]])
            S = L["S"][:, (hq // 2) * 256:(hq // 2 + 1) * 256]
            bS = L["b_S"][hq // 2]
            S3 = v3(S, 4)
            Sb = L["Sb"][:, (hq // 2) * 256:(hq // 2 + 1) * 256]
            bSb = L["b_Sb"][hq // 2]
            Sb3 = v3(Sb, 4)
            vn3 = v3(L["vn"], 4)
            Rb3, ATb3, Ktb3 = v3(L["Rb"], 4), v3(L["ATb"], 4), v3(L["Ktb"], 4)
            tr.op(POOL_ENG, lambda e: e.tensor_copy(Rb3, fm[:, :, 1, :]), reads=[I["bfm"]], writes=[L["b_Rb"]])
            tr.op(POOL_ENG, lambda e: e.tensor_copy(ATb3, Mt4[:, :, 64:128]), reads=[L["b_Mt"]], writes=[L["b_ATb"]])
            tr.op(POOL_ENG, lambda e: e.tensor_copy(L["Ktb"], I["Kt"]), reads=[I["bKt"]], writes=[L["b_Ktb"]])
            for h in range(4):
                tr.op("tensor", lambda e, h=h: e.matmul(hb(sl, 2)[:, h * 64:(h + 1) * 64], Wk4[:, h, :], Sb3[:, h, :], start=True, stop=True),
                      reads=[L["b_Wk"], bSb], writes=[HB[2]])
            yield
            tr.op("vector", lambda e: e.tensor_tensor(L["vn"], hb(sl, 2), I["X"], ALU.add), reads=[HB[2], I["bX"]], writes=[L["b_vn"]])
            if lat:
                for h in range(4):
                    tr.op("tensor", lambda e, h=h: e.matmul(hb(sl, 3)[0:64, h * 64:(h + 1) * 64], Rb3[:, h, :], Sb3[:, h, :], start=True, stop=False),
                          reads=[L["b_Rb"], bSb], writes=[HB[3]])
                    tr.op("tensor", lambda e, h=h: e.matmul(hb(sl, 3)[0:64, h * 64:(h + 1) * 64], ATb3[:, h, :], vn3[:, h, :], start=False, stop=True),
                          reads=[L["b_ATb"], L["b_vn"]], writes=[HB[3]])
            for h in range(4):
                tr.op("tensor", lambda e, h=h: e.matmul(hb(sl, 0)[0:64, h * 64:(h + 1) * 64], Ktb3[:, h, :], vn3[:, h, :], start=True, stop=True),
                      reads=[L["b_Ktb"], L["b_vn"]], writes=[HB[0]])
            yield
            tr.op(POOL_ENG, lambda e: e.tensor_tensor(S3, S3, fm[:, :, 4, :], ALU.mult), reads=[bS, I["bfm"]], writes=[bS])
            tr.op("vector", lambda e: e.tensor_tensor(S, S, hb(sl, 0)[0:64, :], ALU.add), reads=[bS, HB[0]], writes=[bS])
            tr.op(POOL_ENG, lambda e: e.tensor_copy(Sb, S), reads=[bS], writes=[bSb])
            if lat:
                tr.op("vector", lambda e: e.tensor_copy(L["osb"], hb(sl, 3)[0:64, :]), reads=[HB[3]], writes=[L["b_osb"]])
                tok0 = (n - NCH_CTX) * 64
                tr.dma("gpsimd", o_s[d, tok0:tok0 + 64, 1024 + h0 * 64:1024 + (h0 + 4) * 64], L["osb"], reads=[L["b_osb"]])
            yield

        def rwkv_stream(d, par_, nsteps):
            for s in range(nsteps):
                for hq in (par_, par_ + 2)[:int(_os0.environ.get("K_HQ", "2"))]:
                    yield from rwkv_pass(s, d, hq, 2 * d + par_)

        nsteps_r = NCH if stop_after != "G1" else 6
        run_slots([rwkv_stream(0, 0, nsteps_r), rwkv_stream(1, 0, nsteps_r), rwkv_stream(0, 1, nsteps_r), rwkv_stream(1, 1, nsteps_r)])
        tr.barrier()
        A.reset(gdn_mark)
        if stop_after in ("G", "G1"):
            tr.run()
            return nc

        wout_bf = A.alloc(16 * D // 2, dt=BF16); b_wout = Buf()
        woutv = wout_bf.rearrange("p (k c) -> p k c", k=16)
        wstg = [A.alloc(2 * D), A.alloc(2 * D)]; b_wstg = [Buf(), Buf()]
        for g in range(8):
            p = g % 2
            tr.dma("sync", wstg[p].rearrange("p (k c) -> p k c", k=2),
                   w_out_in[g * 256:(g + 1) * 256, :].rearrange("(k p) c -> p k c", p=128), writes=[b_wstg[p]])
            eng = "vector" if g % 2 == 0 else POOL_ENG
            tr.op(eng, lambda e, g=g, p=p: e.tensor_copy(woutv[:, 2 * g:2 * g + 2, :], wstg[p].rearrange("p (k c) -> p k c", k=2)),
                  reads=[b_wstg[p]], writes=[b_wout])
        gng_bc = A.alloc(128); b_gng = Buf()
        gn_g_bc = A.alloc(1024); gn_b_bc = A.alloc(1024); fg_bc = A.alloc(D); b_gnc = Buf()
        tr.dma("sync", gng_bc, gng_in.partition_broadcast(128), writes=[b_gng])
        tr.dma("sync", gn_g_bc, gn_g_in.partition_broadcast(128), writes=[b_gnc])
        tr.dma("sync", gn_b_bc, gn_b_in.partition_broadcast(128), writes=[b_gnc])
        tr.dma("sync", fg_bc, fg_in.partition_broadcast(128), writes=[b_gnc])
        of_ = A.alloc(D); ob_ = A.alloc(D); tmpH = A.alloc(D); zT = A.alloc(D); bon = A.alloc(1024)
        xt_ = A.alloc(D); res = A.alloc(D)
        b_of, b_ob, b_tmpH, b_zT, b_bon, b_xt, b_res = Buf(), Buf(), Buf(), Buf(), Buf(), Buf(), Buf()
        YT = A.alloc(16 * 128 // 2, dt=BF16); b_YT = Buf()
        YTv = YT.rearrange("p (k t) -> p k t", k=16)
        st = A.alloc(64); b_st = Buf()
        n_tt = 16 if stop_after != "H1" else 1
        for tt in range(n_tt):
            tok0 = tt * 128
            tr.dma("sync", of_, o_s[0, tok0:tok0 + 128, :], writes=[b_of])
            tr.dma("sync", ob_, o_s[1, tok0:tok0 + 128, :], writes=[b_ob])
            tr.dma("sync", zT[:, 0:1024].rearrange("p (k t) -> p k t", k=8),
                   uT[3072:4096, CTX + tok0:CTX + tok0 + 128].rearrange("(k p) t -> p k t", p=128), writes=[b_zT])
            tr.dma("sync", zT[:, 1024:2048].rearrange("p (k t) -> p k t", k=8),
                   uT[ZR0:ZR0 + 1024, CTX + tok0:CTX + tok0 + 128].rearrange("(k p) t -> p k t", p=128), writes=[b_zT])
            tr.dma("sync", bon.rearrange("p (k t) -> p k t", k=8),
                   bonus_s[:, tok0:tok0 + 128].rearrange("(k p) t -> p k t", p=128), writes=[b_bon])
            tr.dma("sync", xt_, x_res_in[tok0:tok0 + 128, :], writes=[b_xt])
            tr.op(POOL_ENG, lambda e: e.tensor_tensor(of_, of_, ob_, ALU.add), reads=[b_of, b_ob], writes=[b_of])
            tr.op("scalar", lambda e: e.activation(zT, zT, AF.Silu), reads=[b_zT], writes=[b_zT])
            og = of_[:, 0:1024].rearrange("p (h c) -> p h c", h=8)
            tg = tmpH[:, 0:1024].rearrange("p (h c) -> p h c", h=8)
            tr.op("vector", lambda e: e.tensor_tensor(tmpH[:, 0:1024], of_[:, 0:1024], of_[:, 0:1024], ALU.mult), reads=[b_of], writes=[b_tmpH])
            tr.op("vector", lambda e: e.tensor_reduce(st[:, 0:8], tg, AX.X, ALU.add), reads=[b_tmpH], writes=[b_st])
            tr.op("scalar", lambda e: e.activation(st[:, 0:8], st[:, 0:8], AF.Sqrt, bias=epsc[:, 0:1], scale=1.0 / 128), reads=[b_st, b_epsc], writes=[b_st])
            tr.op("vector", lambda e: e.reciprocal(st[:, 0:8], st[:, 0:8]), reads=[b_st], writes=[b_st])
            tr.op("vector", lambda e: e.tensor_tensor(tg, og, st[:, 0:8].unsqueeze(2).broadcast_to([128, 8, 128]), ALU.mult), reads=[b_of, b_st], writes=[b_tmpH])
            tr.op(POOL_ENG, lambda e: e.tensor_tensor(tg, tg, gng_bc.unsqueeze(1).broadcast_to([128, 8, 128]), ALU.mult), reads=[b_tmpH, b_gng], writes=[b_tmpH])
            orr = of_[:, 1024:2048].rearrange("p (h c) -> p h c", h=16)
            trr = tmpH[:, 1024:2048].rearrange("p (h c) -> p h c", h=16)
            tr.op("vector", lambda e: e.tensor_reduce(st[:, 16:32], orr, AX.X, ALU.add), reads=[b_of], writes=[b_st])
            tr.op("vector", lambda e: e.tensor_scalar(st[:, 16:32], st[:, 16:32], -1.0 / 64, None, ALU.mult), reads=[b_st], writes=[b_st])
            tr.op("vector", lambda e: e.tensor_tensor(orr, orr, st[:, 16:32].unsqueeze(2).broadcast_to([128, 16, 64]), ALU.add), reads=[b_of, b_st], writes=[b_of])
            tr.op("vector", lambda e: e.tensor_tensor(trr, orr, orr, ALU.mult), reads=[b_of], writes=[b_tmpH])
            tr.op("vector", lambda e: e.tensor_reduce(st[:, 32:48], trr, AX.X, ALU.add), reads=[b_tmpH], writes=[b_st])
            tr.op("scalar", lambda e: e.activation(st[:, 32:48], st[:, 32:48], AF.Sqrt, bias=epsc[:, 1:2], scale=1.0 / 64), reads=[b_st, b_epsc], writes=[b_st])
            tr.op("vector", lambda e: e.reciprocal(st[:, 32:48], st[:, 32:48]), reads=[b_st], writes=[b_st])
            tr.op("vector", lambda e: e.tensor_tensor(trr, orr, st[:, 32:48].unsqueeze(2).broadcast_to([128, 16, 64]), ALU.mult), reads=[b_of, b_st], writes=[b_tmpH])
            tr.op(POOL_ENG, lambda e: e.tensor_tensor(tmpH[:, 1024:2048], tmpH[:, 1024:2048], gn_g_bc, ALU.mult), reads=[b_tmpH, b_gnc], writes=[b_tmpH])
            tr.op(POOL_ENG, lambda e: e.tensor_tensor(tmpH[:, 1024:2048], tmpH[:, 1024:2048], gn_b_bc, ALU.add), reads=[b_tmpH, b_gnc], writes=[b_tmpH])
            for k in range(16):
                bank = k // 4
                q = k % 4
                tr.op("tensor", lambda e, bank=bank, q=q, k=k: e.transpose(PS[bank][:, q * 128:(q + 1) * 128], tmpH[:, k * 128:(k + 1) * 128], ident),
                      reads=[b_tmpH, b_ident], writes=[PSB[bank]])
            for bank in range(4):
                zsl = zT[:, bank * 512:(bank + 1) * 512]
                dst = YT[:, bank * 512:(bank + 1) * 512]
                if bank < 2:
                    tr.op("vector", lambda e, bank=bank, zsl=zsl, dst=dst: e.tensor_tensor(dst, PS[bank][:, :], zsl, ALU.mult),
                          reads=[PSB[bank], b_zT], writes=[b_YT])
                else:
                    bsl = bon[:, (bank - 2) * 512:(bank - 1) * 512]
                    tr.op("vector", lambda e, bank=bank, bsl=bsl: e.tensor_tensor(res[:, 0:512], PS[bank][:, :], bsl, ALU.add),
                          reads=[PSB[bank], b_bon], writes=[b_res])
                    tr.op(POOL_ENG, lambda e, zsl=zsl, dst=dst: e.tensor_tensor(dst, res[:, 0:512], zsl, ALU.mult),
                          reads=[b_res, b_zT], writes=[b_YT])
            for cb in range(4):
                bank = 4 + cb
                for kt in range(16):
                    tr.op("tensor", lambda e, bank=bank, kt=kt, cb=cb: e.matmul(PS[bank][:, :], YTv[:, kt, :], woutv[:, kt, cb * 512:(cb + 1) * 512],
                                                                              start=(kt == 0), stop=(kt == 15)),
                          reads=[b_YT, b_wout], writes=[PSB[bank]])
                tr.op("vector", lambda e, bank=bank, cb=cb: e.tensor_tensor(res[:, cb * 512:(cb + 1) * 512], PS[bank][:, :],
                                                                           gate_bc[:, cb * 512:(cb + 1) * 512], ALU.mult),
                      reads=[PSB[bank], b_gate], writes=[b_res])
            tr.op(POOL_ENG, lambda e: e.tensor_tensor(res, res, xt_, ALU.add), reads=[b_res, b_xt], writes=[b_res])
            tr.op("vector", lambda e: e.memset(st[:, 48:49], 0.0), writes=[b_st])
            tr.op("scalar", lambda e: e.activation(xt_, res, AF.Square, accum_out=st[:, 48:49]), reads=[b_res, b_st], writes=[b_xt, b_st])
            tr.op("scalar", lambda e: e.activation(st[:, 48:49], st[:, 48:49], AF.Sqrt, bias=epsc[:, 0:1], scale=1.0 / D), reads=[b_st, b_epsc], writes=[b_st])
            tr.op("vector", lambda e: e.reciprocal(st[:, 48:49], st[:, 48:49]), reads=[b_st], writes=[b_st])
            tr.op("vector", lambda e: e.scalar_tensor_tensor(res, res, st[:, 48:49], fg_bc, ALU.mult, ALU.mult), reads=[b_res, b_st, b_gnc], writes=[b_res])
            tr.dma("gpsimd", out_ap[tok0:tok0 + 128, :], res, reads=[b_res])
        tr.barrier()
        tr.run()
    return nc


def prep_inputs(inputs, b):
    f = lambda a: np.ascontiguousarray(a, dtype=np.float32)
    c = inputs["c"][b]
    cc = np.stack([c.reshape(16, 128).T, inputs["c_ctx"].reshape(16, 128).T], axis=-1)
    m = {
        "x": f(inputs["x"][b]),
        "ctx": f(inputs["ctx"][b]),
        "cc": f(cc),
        "w_ada": f(inputs["w_ada"][0]),
        "b_ada2": f(np.stack([inputs["b_ada"][0], inputs["b_ada"][0]], 0)),
        "normg_T": f(inputs["norm_g"][0].reshape(16, 128).T),
        "w_in": f(inputs["w_in"][0]),
        "ident": np.eye(128, dtype=np.float32),
    }
    cw = inputs["gdn_conv_w"][0]
    m["conv_wT"] = f(cw.reshape(5, 24, 128).transpose(2, 1, 0))
    gpar = np.zeros((32, 8), np.float32)
    gpar[0:16, 0] = inputs["gdn_dt_bias"][0].reshape(16)
    gpar[0:16, 1] = inputs["gdn_a_log"][0].reshape(16)
    gpar[0:16, 2] = 1.0
    gpar[16:32, 3] = 1.0
    gpar[:, 4] = 1.0
    m["gpar"] = gpar
    idx = np.arange(64)
    P = idx[:, None]; Fq = idx[None, :]
    NEG = -30000.0
    gm = np.zeros((64, 6, 64), np.float32)
    gm[:, 0, :] = np.where(P > Fq, 0.0, NEG)
    gm[:, 1, :] = np.where(Fq >= P, 0.0, NEG)
    gm[:, 2, :] = np.where(P < Fq, 0.0, NEG)
    gm[:, 3, :] = np.where(Fq <= P, 0.0, NEG)
    gm[:, 4, :] = (P <= Fq).astype(np.float32)
    gm[:, 5, :] = (P >= Fq).astype(np.float32)
    m["gmask"] = gm
    mu = inputs["rwkv_mu"][0]
    muT = np.zeros((128, 28), np.float32)
    muT[:, 0:24] = mu[0:3072].reshape(24, 128).T
    for g4 in range(4):
        muT[0:96, 24 + g4] = mu[3072 + 96 * g4:3072 + 96 * (g4 + 1)]
    m["muT"] = muT
    ch = np.zeros((128, 28), np.int64) - 1
    ch[:, 0:24] = np.arange(3072).reshape(24, 128).T
    for g4 in range(4):
        ch[0:96, 24 + g4] = 3072 + 96 * g4 + np.arange(96)
    sm = np.zeros((128, 28, 6), np.float32)
    qq = ch // 864
    hh = ch // 1728
    for ci in range(4):
        sm[:, :, ci] = ((qq == ci) & (ch >= 0)).astype(np.float32)
    sm[:, :, 4] = ((hh == 0) & (ch >= 0)).astype(np.float32)
    sm[:, :, 5] = ((hh == 1) & (ch >= 0)).astype(np.float32)
    m["smask"] = sm
    rp = np.zeros((128, 56), np.float32)
    rp[:, 0:16] = inputs["rwkv_w0"][0].reshape(2, 8, 128).transpose(2, 0, 1).reshape(128, 16)
    rp[:, 16:32] = inputs["rwkv_a0"][0].reshape(2, 8, 128).transpose(2, 0, 1).reshape(128, 16)
    rp[:, 32:40] = inputs["rwkv_k_k"][0].reshape(8, 128).T
    rp[:, 40:48] = inputs["rwkv_k_a"][0].reshape(8, 128).T
    rp[:, 48:56] = inputs["rwkv_r_k"][0].reshape(8, 128).T
    m["rpar"] = rp
    blk = np.arange(128) // 64
    m["bones"] = (blk[:, None] == blk[None, :]).astype(np.float32)
    m["w_up"] = f(inputs["rwkv_w_up"][0])
    m["a_up"] = f(inputs["rwkv_a_up"][0])
    s_ = np.arange(128)[:, None] % 64
    t_ = np.arange(128)[None, :] % 64
    is_incl = (np.arange(128)[None, :] >= 64)
    rm = np.zeros((128, 3, 128), np.float32)
    rm[:, 0, :] = np.where(is_incl, t_ >= s_, t_ > s_)
    rm[:, 1, :] = np.where(is_incl, t_ <= s_, t_ < s_)
    tt_ = np.arange(64)[:, None]; ss_ = np.arange(64)[None, :]
    rm[0:64, 2, 0:64] = (tt_ > ss_)
    rm[0:64, 2, 64:128] = (tt_ < ss_)
    m["rmask"] = rm
    m["w_out"] = f(inputs["w_out"][0])
    m["gng"] = f(inputs["gdn_norm_g"][0])
    m["gn_g"] = f(inputs["rwkv_gn_g"][0])
    m["gn_b"] = f(inputs["rwkv_gn_b"][0])
    m["fg"] = f(inputs["final_norm_g"])
    return m


def kernel(**inputs):
    inputs = {k: np.asarray(v) for k, v in inputs.items()}
    nc = build()
    in_maps = [prep_inputs(inputs, b) for b in range(8)]
    res = run_bass_kernel_spmd(nc, in_maps, core_ids=list(range(8)))
    out = np.stack([np.asarray(r["out"]) for r in res.results], axis=0)
    return out.astype(np.float32)
```

```python
import numpy as np
from contextlib import ExitStack
import concourse.bass as bass
import concourse.mybir as mybir
from concourse.bass_utils import run_bass_kernel_spmd

F32 = mybir.dt.float32
BF16 = mybir.dt.bfloat16
AF = mybir.ActivationFunctionType
ALU = mybir.AluOpType
AX = mybir.AxisListType

D = 2048
SEQ = 2048
CTX = 256
T = SEQ + CTX
C = 64
NCH = T // C
NCH_CTX = CTX // C
IN_COLS = 8608
GDN_COLS = 4128
RW0 = GDN_COLS
LO0 = RW0 + 3072
ZR0 = LO0 + 384
EPS = 1e-6
GN_EPS = 64e-5
TOK_BLOCKS = [(0, 512), (512, 512), (1024, 512), (1536, 512), (2048, 256)]
ARENA_WORDS = 51200
import os as _os0
POOL_ENG = _os0.environ.get('K_POOL', 'gpsimd')
NEU_MODE = _os0.environ.get('K_NEU', 'F32')
F32R = mybir.dt.float32r


def mv(ap):
    return ap.bitcast(F32R) if NEU_MODE == 'F32R' else ap


def neu_alloc(A, words, parts=64):
    if NEU_MODE == 'BF16':
        return A.alloc(words // 2, parts=parts, dt=BF16)
    return A.alloc(words, parts=parts)


class Buf:
    __slots__ = ("name", "w", "r")

    def __init__(self, name=""):
        self.name = name
        self.w = None
        self.r = []


class Tracker:
    ENGS = ("tensor", "vector", "scalar", "gpsimd", "sync")

    def __init__(self, nc, es, n_dma_sems=10):
        self.nc = nc
        self.prog = {e: [] for e in self.ENGS}
        self.sems = {}
        self.val = {}
        self.waited = {e: {} for e in self.ENGS}
        self.es = es
        self.cur = {}
        self.gen = {}
        for e in self.ENGS:
            self.gen[e] = 0
            k = e + "#0"
            self.cur[e] = k
            self.sems[k] = es.enter_context(nc.semaphore("s_" + e + "_0"))
            self.val[k] = 0
        self.dma_keys = {}
        self.dma_rr = {}
        for q in ("sync", "gpsimd", "scalar"):
            ks = []
            for i in range(n_dma_sems):
                k = "d_%s_%d" % (q, i)
                self.sems[k] = es.enter_context(nc.semaphore(k))
                self.val[k] = 0
                ks.append(k)
            self.dma_keys[q] = ks
            self.dma_rr[q] = 0
        self.n_inst = 0
        import os as _os
        self.maxops = int(_os.environ["K_MAXOPS"]) if _os.environ.get("K_MAXOPS") else None

    def _wait(self, eng, key, value):
        if value <= 0:
            return
        if self.waited[eng].get(key, 0) >= value:
            return
        self.waited[eng][key] = value
        sem = self.sems[key]
        self.prog[eng].append(lambda e, sem=sem, value=value: e.wait_ge(sem, value))

    SEM_MAX = 12000

    def _deps(self, eng, reads, writes):
        deps = []
        for b in reads:
            if b.w is not None:
                deps.append(b.w)
        strict = (eng == "gpsimd")
        for b in writes:
            if b.w is not None and (strict or b.w[0].split("#")[0] != eng):
                deps.append(b.w)
            for d in b.r:
                if strict or d[0].split("#")[0] != eng:
                    deps.append(d)
        return deps

    def op(self, eng, fn, reads=(), writes=()):
        if self.maxops is not None and self.n_inst >= self.maxops:
            return None
        for (k, v) in self._deps(eng, reads, writes):
            self._wait(eng, k, v)
        ck = self.cur[eng]
        if self.val[ck] >= self.SEM_MAX:
            self.gen[eng] += 1
            ck = "%s#%d" % (eng, self.gen[eng])
            self.cur[eng] = ck
            self.sems[ck] = self.es.enter_context(self.nc.semaphore("s_%s_%d" % (eng, self.gen[eng])))
            self.val[ck] = 0
        self.val[ck] += 1
        v = self.val[ck]
        sem = self.sems[ck]
        self.prog[eng].append(lambda e, fn=fn, sem=sem: fn(e).then_inc(sem, 1))
        ev = (ck, v)
        for b in reads:
            b.r.append(ev)
        for b in writes:
            b.w = ev
            b.r = []
        self.n_inst += 1
        return ev

    def dma(self, q, out, in_, reads=(), writes=(), **kw):
        if self.maxops is not None and self.n_inst >= self.maxops:
            return None
        ks = self.dma_keys[q]
        k = ks[self.dma_rr[q] % len(ks)]
        self.dma_rr[q] += 1
        self._wait(q, k, self.val[k])
        for (kk, v) in self._deps(q, reads, writes):
            self._wait(q, kk, v)
        self.val[k] += 16
        v = self.val[k]
        sem = self.sems[k]
        self.prog[q].append(
            lambda e, out=out, in_=in_, sem=sem, kw=kw: e.dma_start(out=out, in_=in_, **kw).then_inc(sem, 16))
        ev = (k, v)
        for b in reads:
            b.r.append(ev)
        for b in writes:
            b.w = ev
            b.r = []
        self.n_inst += 1
        return ev

    def barrier(self):
        for e in self.ENGS:
            for k, v in list(self.val.items()):
                if k.split("#")[0] != e:
                    self._wait(e, k, v)

    def run(self):
        nc = self.nc
        print("TRACKER n_inst", self.n_inst, {k: v for k, v in self.val.items() if "#" in k})
        self.barrier()
        with nc.Block() as block:
            @block.sync
            def _(e):
                for th in self.prog["sync"]:
                    th(e)

            @block.tensor
            def _(e):
                for th in self.prog["tensor"]:
                    th(e)

            @block.vector
            def _(e):
                for th in self.prog["vector"]:
                    th(e)

            @block.scalar
            def _(e):
                for th in self.prog["scalar"]:
                    th(e)

            @block.gpsimd
            def _(e):
                for th in self.prog["gpsimd"]:
                    th(e)


class Arena:
    def __init__(self, ap):
        self.ap = ap
        self.base = 0
        self.ptr = 0

    def alloc(self, words, parts=128, dt=F32, shape=None):
        words = (words + 7) // 8 * 8
        assert self.ptr + words <= ARENA_WORDS, ("arena overflow", self.ptr, words)
        a = self.ap[0:parts, self.ptr:self.ptr + words]
        self.ptr += words
        if dt == BF16:
            a = a.bitcast(BF16)
        return a

    def mark(self):
        return self.ptr

    def reset(self, m):
        self.ptr = m


def col_tiles():
    blocks = []
    for i in range(8):
        blocks.append((i * 512, 512, 128))
    blocks.append((4096, 32, 32))
    for i in range(6):
        blocks.append((RW0 + i * 512, 512, 128))
    blocks.append((LO0, 384, 96))
    for i in range(2):
        blocks.append((ZR0 + i * 512, 512, 128))
    return blocks


def build(stop_after=None, dbg=(), feed_uT=False):
    nc = bass.Bass("TRN2", target_bir_lowering=False)
    ins = {}

    def din(name, shape, dt=F32):
        ins[name] = nc.dram_tensor(name, list(shape), dt, kind="ExternalInput").ap()
        return ins[name]

    def scr(name, shape, dt=F32):
        kind = "ExternalOutput" if name in dbg else "Internal"
        return nc.dram_tensor(name, list(shape), dt, kind=kind).ap()

    x_in = din("x", [SEQ, D])
    x_res_in = x_in
    if feed_uT:
        gate_dbg = din("gate_dbg", [D])
    if not feed_uT:
        ctx_in = din("ctx", [CTX, D])
        cc_in = din("cc", [128, 16, 2])
        w_ada = din("w_ada", [D, 3 * D])
        b_ada2 = din("b_ada2", [2, 3 * D])
        normg_T = din("normg_T", [128, 16])
        w_in = din("w_in", [D, IN_COLS])
    ident_in = din("ident", [128, 128])
    out_ap = nc.dram_tensor("out", [SEQ, D], F32, kind="ExternalOutput").ap()

    uT = din("uT_in", [IN_COLS, T]) if feed_uT else scr("uT", [IN_COLS, T])
    conv_wT = din("conv_wT", [128, 24, 5])
    gpar_in = din("gpar", [32, 8])
    gmask_in = din("gmask", [64, 6, 64])
    qT_s = scr("qT_s", [NCH, 128, 8, 64])
    kT_s = scr("kT_s", [NCH, 128, 8, 64])
    ktok_s = scr("ktok_s", [NCH, 64, 8, 128])
    vtok_s = scr("vtok_s", [NCH, 64, 8, 128])
    o_s = scr("o_s", [2, SEQ, D])
    muT_in = din("muT", [128, 28])
    smask_in = din("smask", [128, 28, 6])
    rpar_in = din("rpar", [128, 56])
    bones_in = din("bones", [128, 128])
    w_up_in = din("w_up", [2, 96, 1024])
    a_up_in = din("a_up", [2, 96, 1024])
    rmask_in = din("rmask", [128, 3, 128])
    w_out_in = din("w_out", [D, D])
    gng_in = din("gng", [128])
    gn_g_in = din("gn_g", [1024])
    gn_b_in = din("gn_b", [1024])
    fg_in = din("fg", [D])
    rw_fm = scr("rw_fm", [2, NCH, 64, 16, 5, 64])
    rw_tm = scr("rw_tm", [2, NCH, 64, 16, 3, 64])
    v_tm = scr("v_tm", [NCH, 64, 16, 64])
    bonus_s = scr("bonus_s", [1024, SEQ])
    dbgA = scr("dbgA", [128, 16, 4])

    es = ExitStack()
    with es:
        tr = Tracker(nc, es)
        arena_t = es.enter_context(nc.sbuf_tensor("arena", [128, ARENA_WORDS], F32))
        psum_t = es.enter_context(nc.psum_tensor("psum", [128, 8, 512], F32))
        A = Arena(arena_t)
        PS = [psum_t[:, i, :] for i in range(8)]
        PSB = [Buf("ps%d" % i) for i in range(8)]

        ident = A.alloc(128); b_ident = Buf()
        tr.dma("sync", ident, ident_in, writes=[b_ident])
        ones = A.alloc(128); b_ones = Buf()
        tr.op("vector", lambda e: e.memset(ones, 1.0), writes=[b_ones])
        epsc = A.alloc(8); b_epsc = Buf()
        tr.op("vector", lambda e: e.memset(epsc[:, 0:1], EPS), writes=[b_epsc])
        tr.op("vector", lambda e: e.memset(epsc[:, 1:2], GN_EPS), writes=[b_epsc])
        gate_bc = A.alloc(D); b_gate = Buf()
        ABT = A.alloc(64); b_ABT = Buf()
        ABTv = ABT.rearrange("p (f s k) -> p f s k", f=16, s=2)
        persist_mark = A.mark()

        if feed_uT:
            tr.dma("sync", gate_bc, gate_dbg.partition_broadcast(128), writes=[b_gate])
        if not feed_uT:
            cc = A.alloc(32); b_cc = Buf()
            sc = A.alloc(32); b_sc = Buf()
            tr.dma("sync", cc, cc_in.rearrange("p a b -> p (a b)"), writes=[b_cc])
            tr.op("scalar", lambda e: e.activation(sc, cc, AF.Silu), reads=[b_cc], writes=[b_sc])
            scv = sc.rearrange("p (a b) -> p a b", b=2)
            mod_rows = A.alloc(3 * D, parts=2); b_mod = Buf()
            bada = A.alloc(3 * D, parts=2); b_bada = Buf()
            tr.dma("sync", bada, b_ada2, writes=[b_bada])
            wbuf = [A.alloc(3072), A.alloc(3072)]
            b_wbuf = [Buf(), Buf()]
            for half in range(2):
                for kt in range(16):
                    wb = wbuf[kt % 2]
                    tr.dma("sync", wb, w_ada[kt * 128:(kt + 1) * 128, half * 3072:(half + 1) * 3072],
                           writes=[b_wbuf[kt % 2]])
                    for cb in range(6):
                        tr.op("tensor",
                              lambda e, cb=cb, kt=kt, wb=wb: e.matmul(PS[cb][0:2, :], scv[:, kt, :],
                                                                     wb[:, cb * 512:(cb + 1) * 512],
                                                                     start=(kt == 0), stop=(kt == 15)),
                              reads=[b_sc, b_wbuf[kt % 2]], writes=[PSB[cb]])
                for cb in range(6):
                    c0 = half * 3072 + cb * 512
                    tr.op("vector",
                          lambda e, cb=cb, c0=c0: e.tensor_tensor(mod_rows[:, c0:c0 + 512], PS[cb][0:2, :],
                                                                  bada[:, c0:c0 + 512], ALU.add),
                          reads=[PSB[cb], b_bada], writes=[b_mod])
            for j in range(32):
                tr.op("tensor",
                      lambda e, j=j: e.matmul(PS[6][:, j * 2:(j + 1) * 2], mod_rows[0:2, j * 128:(j + 1) * 128],
                                              ident[0:2, 0:2], start=True, stop=True),
                      reads=[b_mod, b_ident], writes=[PSB[6]])
            modT = A.alloc(64); b_modT = Buf()
            tr.op("vector", lambda e: e.tensor_copy(modT, PS[6][:, 0:64]), reads=[PSB[6]], writes=[b_modT])
            modTv = modT.rearrange("p (j s) -> p j s", s=2)
            gT = A.alloc(16); b_gT = Buf()
            tr.dma("sync", gT, normg_T, writes=[b_gT])
            tmpA = A.alloc(32); b_tmpA = Buf()
            tmpAv = tmpA.rearrange("p (j s) -> p j s", s=2)
            tr.op("vector", lambda e: e.tensor_scalar(tmpA, modT[:, 32:64], 1.0, None, ALU.add),
                  reads=[b_modT], writes=[b_tmpA])
            tr.op("vector",
                  lambda e: e.tensor_tensor(ABTv[:, :, :, 0], tmpAv, gT.unsqueeze(2).broadcast_to([128, 16, 2]), ALU.mult),
                  reads=[b_tmpA, b_gT], writes=[b_ABT])
            tr.op("vector", lambda e: e.tensor_copy(ABTv[:, :, :, 1], modTv[:, 0:16, :]),
                  reads=[b_modT], writes=[b_ABT])
            for nb in range(4):
                tr.op("tensor",
                      lambda e, nb=nb: e.matmul(PS[7][:, :], ones[0:1, 0:128],
                                                mod_rows[0:1, 4096 + nb * 512:4096 + (nb + 1) * 512],
                                                start=True, stop=True),
                      reads=[b_mod, b_ones], writes=[PSB[7]])
                tr.op("vector", lambda e, nb=nb: e.tensor_copy(gate_bc[:, nb * 512:(nb + 1) * 512], PS[7][:, :]),
                      reads=[PSB[7]], writes=[b_gate])
            if "dbgA" in dbg:
                tr.dma("sync", dbgA.rearrange("p a b -> p (a b)"), ABT, reads=[b_ABT])
            tr.barrier()
            A.reset(persist_mark)
            if stop_after == "A":
                tr.run()
                return nc

            hT = A.alloc(16 * T // 2, dt=BF16)
            hTv = hT.rearrange("p (f t) -> p f t", f=16)
            b_hT = Buf()
            phaseB_mark = A.mark()
            xb = [A.alloc(D), A.alloc(D)]; b_xb = [Buf(), Buf()]
            xn = [A.alloc(D), A.alloc(D)]; b_xn = [Buf(), Buf()]
            ss = A.alloc(24); b_ss = Buf()
            rstd = A.alloc(24); b_rstd = Buf()
            tr.op("vector", lambda e: e.memset(ss, 0.0), writes=[b_ss])
            for tt in range(18):
                s = 1 if tt < 2 else 0
                src = ctx_in[tt * 128:(tt + 1) * 128, :] if tt < 2 else x_in[(tt - 2) * 128:(tt - 1) * 128, :]
                p = tt % 2
                tr.dma("sync", xb[p], src, writes=[b_xb[p]])
                tr.op("scalar",
                      lambda e, p=p, tt=tt: e.activation(xn[p], xb[p], AF.Square, accum_out=ss[:, tt:tt + 1]),
                      reads=[b_xb[p]], writes=[b_xn[p], b_ss])
                tr.op("scalar",
                      lambda e, tt=tt: e.activation(ss[:, tt:tt + 1], ss[:, tt:tt + 1], AF.Sqrt, bias=epsc[:, 0:1], scale=1.0 / D),
                      reads=[b_ss, b_epsc], writes=[b_ss])
                tr.op("vector",
                      lambda e, tt=tt: e.reciprocal(rstd[:, tt:tt + 1], ss[:, tt:tt + 1]),
                      reads=[b_ss], writes=[b_rstd])
                tr.op("vector",
                      lambda e, p=p, tt=tt: e.tensor_scalar(xn[p], xb[p], rstd[:, tt:tt + 1], None, ALU.mult),
                      reads=[b_xb[p], b_rstd], writes=[b_xn[p]])
                for f in range(16):
                    bank = (p * 4) + f // 4
                    q = f % 4
                    tr.op("tensor",
                          lambda e, bank=bank, q=q, f=f, p=p: e.transpose(PS[bank][:, q * 128:(q + 1) * 128],
                                                                            xn[p][:, f * 128:(f + 1) * 128], ident),
                          reads=[b_xn[p], b_ident], writes=[PSB[bank]])
                for f in range(16):
                    bank = (p * 4) + f // 4
                    q = f % 4
                    dst = hTv[:, f, tt * 128:(tt + 1) * 128]
                    if f % 2 == 0:
                        tr.op("vector",
                              lambda e, bank=bank, q=q, f=f, s=s, dst=dst: e.tensor_scalar(
                                  dst, PS[bank][:, q * 128:(q + 1) * 128], ABTv[:, f, s, 0:1], ABTv[:, f, s, 1:2],
                                  ALU.mult, ALU.add),
                              reads=[PSB[bank], b_ABT], writes=[b_hT])
                    else:
                        tr.op("scalar",
                              lambda e, bank=bank, q=q, f=f, s=s, dst=dst: e.activation(
                                  dst, PS[bank][:, q * 128:(q + 1) * 128], AF.Identity,
                                  bias=ABTv[:, f, s, 1:2], scale=ABTv[:, f, s, 0:1]),
                              reads=[PSB[bank], b_ABT], writes=[b_hT])
            tr.barrier()
            A.reset(phaseB_mark)
            if stop_after == "B":
                if "uT" in dbg:
                    pass

            wst = [A.alloc(16 * 512), A.alloc(16 * 512)]; b_wst = [Buf(), Buf()]
            wbf = [A.alloc(16 * 512 // 2, dt=BF16), A.alloc(16 * 512 // 2, dt=BF16)]; b_wbf = [Buf(), Buf()]
            rowbuf = [A.alloc(T), A.alloc(T)]; b_row = [Buf(), Buf()]
            blocks = col_tiles()
            if stop_after == "C1":
                blocks = blocks[:1] + blocks[8:9]
            tcount = 0
            ecount = 0
            for bi, (c0, ncols, tsz) in enumerate(blocks):
                p = bi % 2
                wsv = wst[p][:, 0:16 * ncols].rearrange("p (k c) -> p k c", k=16)
                wbv = wbf[p][:, 0:16 * ncols].rearrange("p (k c) -> p k c", k=16)
                tr.dma("sync", wsv, w_in[:, c0:c0 + ncols].rearrange("(k p) c -> p k c", p=128), writes=[b_wst[p]])
                tr.op("gpsimd", lambda e, wsv=wsv, wbv=wbv: e.tensor_copy(wbv[:, 0:8, :], wsv[:, 0:8, :]),
                      reads=[b_wst[p]], writes=[b_wbf[p]])
                tr.op("vector", lambda e, wsv=wsv, wbv=wbv: e.tensor_copy(wbv[:, 8:16, :], wsv[:, 8:16, :]),
                      reads=[b_wst[p]], writes=[b_wbf[p]])
                for ti in range(ncols // tsz):
                    rb = rowbuf[tcount % 2]; brb = b_row[tcount % 2]
                    tcount += 1
                    for (t0, tn) in TOK_BLOCKS:
                        bank = ecount % 8
                        ecount += 1
                        for kt in range(16):
                            tr.op("tensor",
                                  lambda e, bank=bank, kt=kt, ti=ti, tsz=tsz, t0=t0, tn=tn, wbv=wbv: e.matmul(
                                      PS[bank][0:tsz, 0:tn], wbv[:, kt, ti * tsz:(ti + 1) * tsz],
                                      hTv[:, kt, t0:t0 + tn], start=(kt == 0), stop=(kt == 15)),
                                  reads=[b_wbf[p], b_hT], writes=[PSB[bank]])
                        if ecount % 2 == 0:
                            tr.op("vector",
                                  lambda e, bank=bank, tsz=tsz, t0=t0, tn=tn, rb=rb: e.tensor_copy(
                                      rb[0:tsz, t0:t0 + tn], PS[bank][0:tsz, 0:tn]),
                                  reads=[PSB[bank]], writes=[brb])
                        else:
                            tr.op("scalar",
                                  lambda e, bank=bank, tsz=tsz, t0=t0, tn=tn, rb=rb: e.copy(
                                      rb[0:tsz, t0:t0 + tn], PS[bank][0:tsz, 0:tn]),
                                  reads=[PSB[bank]], writes=[brb])
                    r0 = c0 + ti * tsz
                    tr.dma("gpsimd", uT[r0:r0 + tsz, :], rb[0:tsz, :], reads=[brb])
            tr.barrier()
            A.reset(persist_mark)
            if stop_after in ("C", "C1"):
                tr.run()
                return nc


        NEG = -30000.0
        gmask = A.alloc(6 * 64, parts=64); b_gmask = Buf()
        tr.dma("sync", gmask, gmask_in.rearrange("p a b -> p (a b)"), writes=[b_gmask])
        gmv = gmask.rearrange("p (a b) -> p a b", a=6)
        gcum = A.alloc(576, parts=64); b_gcum = Buf()
        beta = A.alloc(576, parts=64); b_beta = Buf()
        nbeta = A.alloc(576, parts=64); b_nbeta = Buf()
        bgs = A.alloc(576, parts=64); b_bgs = Buf()
        kts = A.alloc(576, parts=64); b_kts = Buf()
        gcv = gcum.rearrange("p (n c) -> p n c", c=16)
        betav = beta.rearrange("p (n c) -> p n c", c=16)
        nbetav = nbeta.rearrange("p (n c) -> p n c", c=16)
        bgv = bgs.rearrange("p (n c) -> p n c", c=16)
        ktsv = kts.rearrange("p (n c) -> p n c", c=16)
        gdn_mark = A.mark()

        gt = A.alloc(T, parts=32); b_gt = Buf()
        gpar = A.alloc(8, parts=32); b_gpar = Buf()
        tr.dma("sync", gt, uT[4096:4128, :], writes=[b_gt])
        tr.dma("sync", gpar, gpar_in, writes=[b_gpar])
        xa = A.alloc(T, parts=32); b_xa = Buf()
        t1g = A.alloc(T, parts=32); b_t1g = Buf()
        t2g = A.alloc(T, parts=32); b_t2g = Buf()
        sg = A.alloc(T, parts=32); b_sg = Buf()
        coef = A.alloc(8, parts=32); b_coef = Buf()
        tr.op("vector", lambda e: e.tensor_scalar(xa, gt, gpar[:, 0:1], None, ALU.add), reads=[b_gt, b_gpar], writes=[b_xa])
        tr.op("scalar", lambda e: e.activation(t1g, xa, AF.Abs), reads=[b_xa], writes=[b_t1g])
        tr.op("scalar", lambda e: e.activation(t1g, t1g, AF.Exp, scale=-1.0), reads=[b_t1g], writes=[b_t1g])
        tr.op("scalar", lambda e: e.activation(t1g, t1g, AF.Ln, bias=gpar[:, 4:5], scale=1.0), reads=[b_t1g, b_gpar], writes=[b_t1g])
        tr.op("vector", lambda e: e.tensor_scalar(t2g, xa, 0.0, None, ALU.max), reads=[b_xa], writes=[b_t2g])
        tr.op("vector", lambda e: e.tensor_tensor(t2g, t2g, t1g, ALU.add), reads=[b_t2g, b_t1g], writes=[b_t2g])
        tr.op("scalar", lambda e: e.activation(coef[:, 0:1], gpar[:, 1:2], AF.Exp), reads=[b_gpar], writes=[b_coef])
        tr.op("vector", lambda e: e.tensor_scalar(coef[:, 1:2], coef[:, 0:1], gpar[:, 2:3], -1.0, ALU.mult, ALU.mult),
              reads=[b_coef, b_gpar], writes=[b_coef])
        tr.op("vector", lambda e: e.tensor_scalar(t2g, t2g, coef[:, 1:2], None, ALU.mult), reads=[b_t2g, b_coef], writes=[b_t2g])
        tr.op("scalar", lambda e: e.activation(sg, gt, AF.Sigmoid), reads=[b_gt], writes=[b_sg])
        tr.op("vector", lambda e: e.scalar_tensor_tensor(t2g, sg, gpar[:, 3:4], t2g, ALU.mult, ALU.add),
              reads=[b_sg, b_gpar, b_t2g], writes=[b_t2g])
        gtok = A.alloc(36 * 32, parts=64); b_gtok = Buf()
        gtv = gtok.rearrange("p (n c) -> p n c", c=32)
        for n in range(NCH):
            bank = n // 16
            q = n % 16
            tr.op("tensor",
                  lambda e, n=n, bank=bank, q=q: e.transpose(PS[bank][0:64, q * 32:(q + 1) * 32],
                                                            t2g[0:32, n * 64:(n + 1) * 64], ident[0:32, 0:32]),
                  reads=[b_t2g, b_ident], writes=[PSB[bank]])
        for bank in range(3):
            nn = 16 if bank < 2 else 4
            tr.op("vector",
                  lambda e, bank=bank, nn=nn: e.tensor_copy(gtok[:, bank * 512:bank * 512 + nn * 32], PS[bank][0:64, 0:nn * 32]),
                  reads=[PSB[bank]], writes=[b_gtok])
        tmpg = A.alloc(288, parts=64); b_tmpg = Buf()
        for d in range(2):
            ps_c = PS[3 + d][0:64, 0:288].rearrange("p (n h) -> p n h", h=8)
            tr.op("tensor",
                  lambda e, d=d, ps_c=ps_c: e.matmul(ps_c, gmv[:, 4 + d, :], gtv[:, :, 8 * d:8 * d + 8], start=True, stop=True),
                  reads=[b_gmask, b_gtok], writes=[PSB[3 + d]])
            tr.op("vector", lambda e, d=d, ps_c=ps_c: e.tensor_copy(gcv[:, :, 8 * d:8 * d + 8], ps_c),
                  reads=[PSB[3 + d]], writes=[b_gcum])
            ps_l = PS[5 + d][0:64, 0:288].rearrange("p (n h) -> p n h", h=8)
            tr.op("tensor",
                  lambda e, d=d, ps_l=ps_l: e.matmul(ps_l, ones[0:64, 0:64], gtv[:, :, 8 * d:8 * d + 8], start=True, stop=True),
                  reads=[b_ones, b_gtok], writes=[PSB[5 + d]])
            tr.op("vector",
                  lambda e, d=d, ps_l=ps_l: e.tensor_tensor(ktsv[:, :, 8 * d:8 * d + 8], ps_l, gcv[:, :, 8 * d:8 * d + 8], ALU.subtract),
                  reads=[PSB[5 + d], b_gcum], writes=[b_kts])
        tr.op("scalar", lambda e: e.activation(kts, kts, AF.Exp), reads=[b_kts], writes=[b_kts])
        tr.op("vector", lambda e: e.tensor_copy(betav, gtv[:, :, 16:32]), reads=[b_gtok], writes=[b_beta])
        tr.op("vector", lambda e: e.tensor_scalar(nbetav, gtv[:, :, 16:32], -1.0, None, ALU.mult), reads=[b_gtok], writes=[b_nbeta])
        tr.op("scalar", lambda e: e.activation(bgs, gcum, AF.Exp), reads=[b_gcum], writes=[b_bgs])
        tr.op("vector", lambda e: e.tensor_tensor(bgs, bgs, beta, ALU.mult), reads=[b_bgs, b_beta], writes=[b_bgs])
        if "dbgG" in dbg:
            dbgG = scr("dbgG", [64, 4, 576])
            tr.dma("sync", dbgG[:, 0, :], gcum, reads=[b_gcum])
            tr.dma("sync", dbgG[:, 1, :], beta, reads=[b_beta])
            tr.dma("sync", dbgG[:, 2, :], bgs, reads=[b_bgs])
            tr.dma("sync", dbgG[:, 3, :], kts, reads=[b_kts])
        tr.barrier()
        A.reset(gdn_mark)

        cw = A.alloc(120); b_cw = Buf()
        tr.dma("sync", cw, conv_wT.rearrange("p a b -> p (a b)"), writes=[b_cw])
        cwv = cw.rearrange("p (a b) -> p a b", b=5)
        bufA = [[A.alloc(T) for _ in range(3)] for _ in range(2)]
        bufB = [[A.alloc(T) for _ in range(3)] for _ in range(2)]
        b_bufA = [[Buf() for _ in range(3)] for _ in range(2)]
        b_bufB = [[Buf() for _ in range(3)] for _ in range(2)]
        tokb = [A.alloc(NCH * 128, parts=64) for _ in range(2)]
        b_tokb = [Buf(), Buf()]
        segs = [(0, CTX), (CTX, T)]
        n_heads_d = 8 if stop_after != "D1" else 1
        if _os0.environ.get("K_SKIP_GDN"):
            n_heads_d = 0
        pbank = 0
        for h in range(n_heads_d):
            st = h % 2
            for qi in range(3):
                raw = bufA[st][qi]; acc = bufB[st][qi]
                b_raw = b_bufA[st][qi]; b_acc = b_bufB[st][qi]
                ct = qi * 8 + h
                r0 = qi * 1024 + h * 128
                tr.dma("sync", raw, uT[r0:r0 + 128, :], writes=[b_raw])
                tr.op("vector", lambda e, raw=raw, acc=acc, ct=ct: e.tensor_scalar(acc, raw, cwv[:, ct, 2:3], None, ALU.mult),
                      reads=[b_raw, b_cw], writes=[b_acc])
                for j in (0, 1, 3, 4):
                    sh = j - 2
                    for (a, b) in segs:
                        t0 = max(a, a - sh); t1 = min(b, b - sh)
                        tr.op("vector",
                              lambda e, raw=raw, acc=acc, ct=ct, j=j, sh=sh, t0=t0, t1=t1: e.scalar_tensor_tensor(
                                  acc[:, t0:t1], raw[:, t0 + sh:t1 + sh], cwv[:, ct, j:j + 1], acc[:, t0:t1], ALU.mult, ALU.add),
                              reads=[b_raw, b_cw, b_acc], writes=[b_acc])
                tr.op("scalar", lambda e, raw=raw, acc=acc: e.activation(raw, acc, AF.Silu), reads=[b_acc], writes=[b_raw])
                if qi < 2:
                    tr.op("gpsimd", lambda e, raw=raw, acc=acc: e.tensor_tensor(acc, raw, raw, ALU.mult), reads=[b_raw], writes=[b_acc])
                    for (t0, tn) in TOK_BLOCKS:
                        bank = pbank % 8; pbank += 1
                        tr.op("tensor",
                              lambda e, bank=bank, acc=acc, t0=t0, tn=tn: e.matmul(PS[bank][:, 0:tn], ones[:, 0:128], acc[:, t0:t0 + tn],
                                                                                   start=True, stop=True),
                              reads=[b_acc, b_ones], writes=[PSB[bank]])
                        tr.op("scalar",
                              lambda e, bank=bank, acc=acc, t0=t0, tn=tn: e.activation(acc[:, t0:t0 + tn], PS[bank][:, 0:tn], AF.Sqrt,
                                                                                       bias=epsc[:, 0:1], scale=1.0),
                              reads=[PSB[bank], b_epsc, b_acc], writes=[b_acc])
                    tr.op("vector", lambda e, acc=acc: e.reciprocal(acc, acc), reads=[b_acc], writes=[b_acc])
                    scl = (128.0 ** -0.5) if qi == 0 else 1.0
                    tr.op("vector", lambda e, raw=raw, acc=acc, scl=scl: e.scalar_tensor_tensor(raw, raw, scl, acc, ALU.mult, ALU.mult),
                          reads=[b_raw, b_acc], writes=[b_raw])
                    dst = qT_s if qi == 0 else kT_s
                    tr.dma("gpsimd", dst[:, :, h, :].rearrange("c p t -> p c t"), raw.rearrange("p (c t) -> p c t", t=64),
                           reads=[b_raw])
                if qi >= 1:
                    tb = tokb[qi - 1]; b_tb = b_tokb[qi - 1]
                    for n in range(NCH):
                        bank = pbank % 8
                        q4 = n % 4
                        tr.op("tensor",
                              lambda e, bank=bank, q4=q4, raw=raw, n=n: e.transpose(PS[bank][0:64, q4 * 128:(q4 + 1) * 128],
                                                                                   raw[:, n * 64:(n + 1) * 64], ident),
                              reads=[b_raw, b_ident], writes=[PSB[bank]])
                        if q4 == 3:
                            n0 = n - 3
                            if (n // 4) % 2 == 0:
                                tr.op("vector",
                                      lambda e, bank=bank, tb=tb, n0=n0: e.tensor_copy(tb[:, n0 * 128:(n0 + 4) * 128], PS[bank][0:64, :]),
                                      reads=[PSB[bank]], writes=[b_tb])
                            else:
                                tr.op("scalar",
                                      lambda e, bank=bank, tb=tb, n0=n0: e.copy(tb[:, n0 * 128:(n0 + 4) * 128], PS[bank][0:64, :]),
                                      reads=[PSB[bank]], writes=[b_tb])
                            pbank += 1
                    dst = ktok_s if qi == 1 else vtok_s
                    tr.dma("gpsimd", dst[:, :, h, :].rearrange("c p f -> p c f"), tb.rearrange("p (c f) -> p c f", f=128),
                           reads=[b_tb])
        tr.barrier()
        A.reset(gdn_mark)
        if stop_after in ("D", "D1"):
            tr.run()
            return nc

        _sw = int(_os0.environ.get("K_SWAP", "0"))

        def hb(slot, i):
            return psum_t[:, slot * 2 + i // 2, (i % 2) * 256:(i % 2) * 256 + 256]

        def hbank(slot, i):
            return psum_t[:, slot * 2 + i // 2, :]

        HBB = []
        for sl in range(4):
            bk = [Buf("hbk%d_%d" % (sl, i)) for i in range(2)]
            HBB.append([bk[i // 2] for i in range(4)])
        gorder = [list(range(NCH)), [3, 2, 1, 0] + list(range(NCH - 1, 3, -1))]

        def v3(ap, a):
            return ap.rearrange("p (a b) -> p a b", a=a)

        GL = []
        for sl in range(4):
            L = {}
            L["in"] = []
            for par in range(1):
                L["in"].append(dict(q=A.alloc(256), k=A.alloc(256), kt=A.alloc(512, parts=64), vt=A.alloc(512, parts=64),
                                    bq=Buf(), bk=Buf(), bkt=Buf(), bvt=Buf()))
            for nm in ("dg", "a0", "t1", "P0f", "AT"):
                L[nm] = A.alloc(256, parts=64); L["b_" + nm] = Buf()
            for nm in ("P0", "P1", "PT0", "PT1", "Y"):
                L[nm] = neu_alloc(A, 256); L["b_" + nm] = Buf()
            for nm in ("args", "DD", "ktail", "X", "vn", "osb"):
                L[nm] = A.alloc(512, parts=64); L["b_" + nm] = Buf()
            for nm in ("vb", "Kbg"):
                L[nm] = A.alloc(256, parts=64, dt=BF16); L["b_" + nm] = Buf()
            L["Yb"] = A.alloc(128, parts=64, dt=BF16); L["b_Yb"] = Buf()
            for nm in ("eg",):
                L[nm] = A.alloc(256); L["b_" + nm] = Buf()
            for nm in ("qh", "Wk"):
                L[nm] = A.alloc(128, dt=BF16); L["b_" + nm] = Buf()
            L["ATb"] = A.alloc(128, parts=64, dt=BF16)
            L["vnb"] = A.alloc(256, parts=64, dt=BF16)
            L["ktb"] = A.alloc(256, parts=64, dt=BF16)
            L["S"] = A.alloc(512); L["b_S"] = [Buf()]
            L["Sb"] = A.alloc(256, dt=BF16); L["b_Sb"] = [Buf()]
            GL.append(L)
            tr.op("vector", lambda e, L=L: e.memset(L["S"], 0.0), writes=L["b_S"])
            tr.op("vector", lambda e, L=L: e.memset(L["Sb"], 0.0), writes=L["b_Sb"])
        ident64_bc = ident[0:64, 0:64].unsqueeze(1).broadcast_to([64, 4, 64])
        pass_ctr = [0, 0, 0, 0]

        def neumann(sl, L, HB, ia, ib, ic):
            Y3 = v3(L["Y"], 4)

            def sq(Pc3, PTc3, bPc, bPTc, want_T):
                for h in range(4):
                    tr.op("tensor", lambda e, h=h: e.matmul(hb(sl, ia)[0:64, h * 64:(h + 1) * 64], mv(PTc3[:, h, :]), mv(Pc3[:, h, :]), start=True, stop=True),
                          reads=[bPc, bPTc], writes=[HB[ia]])
                if want_T:
                    for h in range(4):
                        tr.op("tensor", lambda e, h=h: e.matmul(hb(sl, ib)[0:64, h * 64:(h + 1) * 64], mv(Pc3[:, h, :]), mv(PTc3[:, h, :]), start=True, stop=True),
                              reads=[bPc, bPTc], writes=[HB[ib]])

            def evac(nxt, want_T):
                Pn, PTn = L["P%d" % nxt], L["PT%d" % nxt]
                tr.op("scalar", lambda e: e.copy(mv(Pn), hb(sl, ia)[0:64, :]), reads=[HB[ia]], writes=[L["b_P%d" % nxt]])
                if want_T:
                    tr.op("scalar", lambda e: e.copy(mv(PTn), hb(sl, ib)[0:64, :]), reads=[HB[ib]], writes=[L["b_PT%d" % nxt]])

            cur = 0
            sq(v3(L["P0"], 4), v3(L["PT0"], 4), L["b_P0"], L["b_PT0"], True)
            yield
            evac(1, True)
            cur = 1
            for r in range(1, 6):
                Pc3, PTc3 = v3(L["P%d" % cur], 4), v3(L["PT%d" % cur], 4)
                bPc, bPTc = L["b_P%d" % cur], L["b_PT%d" % cur]
                for h in range(4):
                    tr.op("tensor", lambda e, h=h, Pc3=Pc3: e.matmul(hb(sl, ic)[0:64, h * 64:(h + 1) * 64], mv(Pc3[:, h, :]), mv(Y3[:, h, :]), start=True, stop=True),
                          reads=[bPc, L["b_Y"]], writes=[HB[ic]])
                if r < 5:
                    sq(Pc3, PTc3, bPc, bPTc, r < 4)
                yield
                tr.op("vector", lambda e: e.tensor_tensor(mv(L["Y"]), L["Y"], hb(sl, ic)[0:64, :], ALU.add),
                      reads=[L["b_Y"], HB[ic]], writes=[L["b_Y"]])
                if r < 5:
                    evac(1 - cur, r < 4)
                    cur = 1 - cur

        def gdn_pass(s, d, half, sl):
            L = GL[sl]
            HB = HBB[sl]
            n = gorder[d][s]
            lat = n >= NCH_CTX
            last = 63 if d == 0 else 0
            hs0 = 4 * half
            col0 = 8 * d + 4 * half
            par = 0
            pass_ctr[sl] += 1
            I = L["in"][par]
            qc, kc, ktc, vtc = I["q"], I["k"], I["kt"], I["vt"]
            qc3, kc3 = v3(qc, 4), v3(kc, 4)
            ktc3, vtc3 = v3(ktc, 4), v3(vtc, 4)
            tr.dma("sync", qc3, qT_s[n, :, hs0:hs0 + 4, :], writes=[I["bq"]])
            tr.dma("sync", kc3, kT_s[n, :, hs0:hs0 + 4, :], writes=[I["bk"]])
            tr.dma("sync", ktc3, ktok_s[n, :, hs0:hs0 + 4, :], writes=[I["bkt"]])
            tr.dma("sync", vtc3, vtok_s[n, :, hs0:hs0 + 4, :], writes=[I["bvt"]])
            gc_bc = gcv[:, n, col0:col0 + 4].unsqueeze(2).broadcast_to([64, 4, 64])
            for h in range(4):
                tr.op("tensor", lambda e, h=h: e.matmul(hb(sl, 0)[0:64, h * 64:(h + 1) * 64], kc3[:, h, :], kc3[:, h, :],
                                                        start=True, stop=True),
                      reads=[I["bk"]], writes=[HB[0]])
            for h in range(4):
                tr.op("tensor", lambda e, h=h: e.matmul(hb(sl, 1)[0:64, h * 64:(h + 1) * 64], kc3[:, h, :], qc3[:, h, :],
                                                        start=True, stop=True),
                      reads=[I["bk"], I["bq"]], writes=[HB[1]])
            tr.op(POOL_ENG, lambda e: e.tensor_tensor(v3(L["dg"], 4), ident64_bc, gc_bc, ALU.mult),
                  reads=[b_ident, b_gcum], writes=[L["b_dg"]])
            tr.op("tensor", lambda e: e.matmul(hb(sl, 2), ones[0:64, 0:128], L["dg"], start=True, stop=True),
                  reads=[b_ones, L["b_dg"]], writes=[HB[2]])
            yield
            tr.op("vector", lambda e: e.tensor_copy(L["eg"], hb(sl, 2)), reads=[HB[2]], writes=[L["b_eg"]])
            tr.op("scalar", lambda e: e.activation(L["eg"], L["eg"], AF.Exp), reads=[L["b_eg"]], writes=[L["b_eg"]])
            tr.op("vector", lambda e: e.tensor_tensor(v3(L["a0"], 4), v3(hb(sl, 2)[0:64, :], 4), gc_bc, ALU.subtract),
                  reads=[HB[2], b_gcum], writes=[L["b_a0"]])
            tr.op("vector",
                  lambda e: e.scalar_tensor_tensor(v3(L["args"][:, 0:256], 4), v3(L["a0"], 4), -1.0,
                                                   gmv[:, 2 * d, :].unsqueeze(1).broadcast_to([64, 4, 64]), ALU.mult, ALU.add),
                  reads=[L["b_a0"], b_gmask], writes=[L["b_args"]])
            tr.op(POOL_ENG,
                  lambda e: e.tensor_tensor(v3(L["args"][:, 256:512], 4), v3(L["a0"], 4),
                                            gmv[:, 2 * d + 1, :].unsqueeze(1).broadcast_to([64, 4, 64]), ALU.add),
                  reads=[L["b_a0"], b_gmask], writes=[L["b_args"]])
            tr.op("scalar", lambda e: e.activation(L["DD"], L["args"], AF.Exp), reads=[L["b_args"]], writes=[L["b_DD"]])
            tr.op(POOL_ENG, lambda e: e.tensor_tensor(L["qh"], qc, L["eg"], ALU.mult), reads=[I["bq"], L["b_eg"]], writes=[L["b_qh"]])
            tr.op("vector",
                  lambda e: e.tensor_tensor(v3(L["t1"], 4), v3(hb(sl, 0)[0:64, :], 4),
                                            nbetav[:, n, col0:col0 + 4].unsqueeze(2).broadcast_to([64, 4, 64]), ALU.mult),
                  reads=[HB[0], b_nbeta], writes=[L["b_t1"]])
            tr.op(POOL_ENG, lambda e: e.tensor_tensor(L["P0f"], L["t1"], L["DD"][:, 0:256], ALU.mult),
                  reads=[L["b_t1"], L["b_DD"]], writes=[L["b_P0f"]])
            tr.op(POOL_ENG, lambda e: e.tensor_tensor(mv(L["P0"]), L["t1"], L["DD"][:, 0:256], ALU.mult),
                  reads=[L["b_t1"], L["b_DD"]], writes=[L["b_P0"]])
            tr.op("vector", lambda e: e.tensor_tensor(L["ATb"], hb(sl, 1)[0:64, :], L["DD"][:, 256:512], ALU.mult),
                  reads=[HB[1], L["b_DD"]], writes=[L["b_AT"]])
            P03 = v3(L["P0f"], 4)
            for h in range(4):
                tr.op("tensor", lambda e, h=h: e.transpose(hb(sl, 3)[0:64, h * 64:(h + 1) * 64], P03[:, h, :], ident[0:64, 0:64]),
                      reads=[L["b_P0f"], b_ident], writes=[HB[3]])
            yield
            tr.op("vector", lambda e: e.tensor_copy(mv(L["PT0"]), hb(sl, 3)[0:64, :]), reads=[HB[3]], writes=[L["b_PT0"]])
            tr.op("vector", lambda e: e.tensor_tensor(mv(v3(L["Y"], 4)), v3(hb(sl, 3)[0:64, :], 4), ident64_bc, ALU.add),
                  reads=[HB[3], b_ident], writes=[L["b_Y"]])
            tr.op(POOL_ENG,
                  lambda e: e.tensor_tensor(mv(v3(L["vb"], 4)), vtc3, betav[:, n, col0:col0 + 4].unsqueeze(2).broadcast_to([64, 4, 128]), ALU.mult),
                  reads=[I["bvt"], b_beta], writes=[L["b_vb"]])
            tr.op(POOL_ENG,
                  lambda e: e.tensor_tensor(mv(v3(L["Kbg"], 4)), ktc3, bgv[:, n, col0:col0 + 4].unsqueeze(2).broadcast_to([64, 4, 128]), ALU.mult),
                  reads=[I["bkt"], b_bgs], writes=[L["b_Kbg"]])
            tr.op(POOL_ENG,
                  lambda e: e.tensor_tensor(v3(L["ktb"], 4), ktc3, ktsv[:, n, col0:col0 + 4].unsqueeze(2).broadcast_to([64, 4, 128]), ALU.mult),
                  reads=[I["bkt"], b_kts], writes=[L["b_ktail"]])
            Y3 = v3(L["Y"], 4)
            yield from neumann(sl, L, HB, 0, 1, 2)
            vb3, Kbg3, kta3 = v3(L["vb"], 4), v3(L["Kbg"], 4), v3(L["ktb"], 4)
            tr.op("scalar", lambda e: e.copy(L["Yb"], L["Y"]), reads=[L["b_Y"]], writes=[L["b_Yb"]])
            Yb3 = v3(L["Yb"], 4)
            for h in range(4):
                tr.op("tensor", lambda e, h=h: e.matmul(hbank(sl, 0)[0:64, h * 128:(h + 1) * 128], Yb3[:, h, :], vb3[:, h, :],
                                                        start=True, stop=True),
                      reads=[L["b_Yb"], L["b_vb"]], writes=[HB[0], HB[1]])
            for h in range(4):
                tr.op("tensor", lambda e, h=h: e.matmul(hb(sl, 2)[:, h * 64:(h + 1) * 64], Kbg3[:, h, :], Yb3[:, h, :],
                                                        start=True, stop=True),
                      reads=[L["b_Yb"], L["b_Kbg"]], writes=[HB[2]])
            yield
            tr.op("vector", lambda e: e.tensor_copy(L["X"], hbank(sl, 0)[0:64, :]), reads=[HB[0], HB[1]], writes=[L["b_X"]])
            tr.op("vector", lambda e: e.tensor_scalar(L["Wk"], hb(sl, 2), -1.0, None, ALU.mult), reads=[HB[2]], writes=[L["b_Wk"]])
            S = L["S"]
            bS = L["b_S"][0]
            S3 = v3(S, 4)
            Sb = L["Sb"]
            bSb = L["b_Sb"][0]
            Sb3 = v3(Sb, 4)
            Wk3, qh3, AT3, vn3 = v3(L["Wk"], 4), v3(L["qh"], 4), v3(L["ATb"], 4), v3(L["vnb"], 4)
            for h in range(4):
                tr.op("tensor", lambda e, h=h: e.matmul(hbank(sl, 2)[0:64, h * 128:(h + 1) * 128], Wk3[:, h, :], Sb3[:, h, :],
                                                        start=True, stop=True),
                      reads=[L["b_Wk"], bSb], writes=[HB[2], HB[3]])
            yield
            tr.op("vector", lambda e: e.tensor_tensor(L["vnb"], hbank(sl, 2)[0:64, :], L["X"], ALU.add),
                  reads=[HB[2], HB[3], L["b_X"]], writes=[L["b_vn"]])
            if lat:
                for h in range(4):
                    tr.op("tensor", lambda e, h=h: e.matmul(hbank(sl, 0)[0:64, h * 128:(h + 1) * 128], qh3[:, h, :], Sb3[:, h, :],
                                                            start=True, stop=False),
                          reads=[L["b_qh"], bSb], writes=[HB[0], HB[1]])
                    tr.op("tensor", lambda e, h=h: e.matmul(hbank(sl, 0)[0:64, h * 128:(h + 1) * 128], AT3[:, h, :], vn3[:, h, :],
                                                            start=False, stop=True),
                          reads=[L["b_AT"], L["b_vn"]], writes=[HB[0], HB[1]])
            for h in range(4):
                tr.op("tensor", lambda e, h=h: e.matmul(hbank(sl, 2)[:, h * 128:(h + 1) * 128], kta3[:, h, :], vn3[:, h, :],
                                                        start=True, stop=True),
                      reads=[L["b_ktail"], L["b_vn"]], writes=[HB[2], HB[3]])
            yield
            egl = v3(L["eg"], 4)[:, :, last:last + 1].broadcast_to([128, 4, 128])
            tr.op(POOL_ENG, lambda e: e.tensor_tensor(S3, S3, egl, ALU.mult), reads=[bS, L["b_eg"]], writes=[bS])
            tr.op("vector", lambda e: e.tensor_tensor(S, S, hbank(sl, 2), ALU.add), reads=[bS, HB[2], HB[3]], writes=[bS])
            tr.op("scalar", lambda e: e.copy(Sb, S), reads=[bS], writes=[bSb])
            if lat:
                tr.op("vector", lambda e: e.tensor_copy(L["osb"], hbank(sl, 0)[0:64, :]), reads=[HB[0], HB[1]], writes=[L["b_osb"]])
                tok0 = (n - NCH_CTX) * 64
                tr.dma("gpsimd", o_s[d, tok0:tok0 + 64, hs0 * 128:(hs0 + 4) * 128], L["osb"], reads=[L["b_osb"]])
            yield

        def run_slots(streams, max_yields=None):
            active = list(streams)
            cnt = 0
            while active:
                for it in list(active):
                    if max_yields is not None and cnt >= max_yields:
                        return
                    cnt += 1
                    try:
                        next(it)
                    except StopIteration:
                        active.remove(it)

        def gdn_stream(d, half, nsteps):
            for s in range(nsteps):
                yield from gdn_pass(s, d, half, 2 * d + half)

        nsteps_g = NCH if stop_after != "E1" else 6
        if _os0.environ.get("K_SKIP_GDN"):
            nsteps_g = 0
        import os as _os
        _my = _os.environ.get("K_MAXY")
        run_slots([gdn_stream(0, 0, nsteps_g), gdn_stream(1, 0, nsteps_g), gdn_stream(0, 1, nsteps_g), gdn_stream(1, 1, nsteps_g)],
                  int(_my) if _my else None)
        tr.barrier()
        A.reset(gdn_mark)
        if stop_after in ("E", "E1"):
            tr.run()
            return nc

        rw_mark = A.mark()
        NEG_E = -0.6065306597126334
        muT = A.alloc(32); b_muT = Buf()
        smk = A.alloc(28 * 6); b_smk = Buf()
        c0t = A.alloc(32); b_c0 = Buf()
        cmt = A.alloc(28 * 6); b_cm = Buf()
        rpar = A.alloc(56); b_rpar = Buf()
        bones = A.alloc(128); b_bones = Buf()
        tr.dma("sync", muT[:, 0:28], muT_in, writes=[b_muT])
        tr.dma("sync", smk, smask_in.rearrange("p a b -> p (a b)"), writes=[b_smk])
        tr.dma("sync", rpar, rpar_in, writes=[b_rpar])
        tr.dma("sync", bones, bones_in, writes=[b_bones])
        tr.op("vector", lambda e: e.tensor_scalar(c0t[:, 0:28], muT[:, 0:28], -1.0, 1.0, ALU.mult, ALU.add), reads=[b_muT], writes=[b_c0])
        cmv = cmt.rearrange("p (a b) -> p a b", b=6)
        tr.op("vector",
              lambda e: e.tensor_tensor(cmv, smk.rearrange("p (a b) -> p a b", b=6), muT[:, 0:28].unsqueeze(2).broadcast_to([128, 28, 6]), ALU.mult),
              reads=[b_smk, b_muT], writes=[b_cm])
        oneska = A.alloc(8); b_oneska = Buf()
        tr.op("vector", lambda e: e.tensor_scalar(oneska, rpar[:, 40:48], -1.0, 1.0, ALU.mult, ALU.add), reads=[b_rpar], writes=[b_oneska])
        rmask = A.alloc(T); b_rmask = Buf()
        tr.op("gpsimd", lambda e: e.memset(rmask, 1.0), writes=[b_rmask])
        tr.op("gpsimd", lambda e: e.memset(rmask.rearrange("p (n t) -> p n t", t=64)[:, :, 0:1], 0.0), writes=[b_rmask])
        loT = [A.alloc(T, parts=96) for _ in range(4)]; b_loT = [Buf() for _ in range(4)]
        W = [A.alloc(T) for _ in range(13)]; bW = [Buf() for _ in range(13)]
        stg = [A.alloc(18 * 128, parts=64)] * 2; b_stg = [Buf()] * 2

        def lerp(dst, src, bd, bs, ti, parts, ch0, ch1):
            tr.op("vector", lambda e: e.tensor_scalar(dst, src, c0t[0:parts, ti:ti + 1], None, ALU.mult),
                  reads=[bs, b_c0], writes=[bd])
            qs = set(range(ch0 // 864, (ch1 - 1) // 864 + 1))
            hs = set(range(ch0 // 1728, (ch1 - 1) // 1728 + 1))
            dl, sl_ = dst[:, CTX:T], src[:, CTX:T]
            dl3 = dl.rearrange("p (r c) -> p r c", c=64)
            sl3 = sl_.rearrange("p (r c) -> p r c", c=64)
            terms = []
            if 0 in qs:
                terms.append((dl3[:, :, 1:64], sl3[:, :, 0:63], 0))
            if 1 in qs:
                terms.append((dl3[:, :, 0:63], sl3[:, :, 1:64], 1))
            if 2 in qs:
                terms.append((dl[:, 64:SEQ], sl_[:, 0:SEQ - 64], 2))
            if 3 in qs:
                terms.append((dl[:, 0:SEQ - 64], sl_[:, 64:SEQ], 3))
            if 0 in hs:
                terms.append((dst[:, 1:CTX], src[:, 0:CTX - 1], 4))
            if 1 in hs:
                terms.append((dst[:, 0:CTX - 1], src[:, 1:CTX], 5))
            for (o_, i_, ci) in terms:
                tr.op("vector",
                      lambda e, o_=o_, i_=i_, ci=ci: e.scalar_tensor_tensor(o_, i_, cmv[0:parts, ti, ci:ci + 1], o_, ALU.mult, ALU.add),
                      reads=[bs, b_cm, bd], writes=[bd])

        for m in range(4):
            raw = W[m][0:96, :]
            tr.dma("sync", raw, uT[LO0 + 96 * m:LO0 + 96 * (m + 1), :], writes=[bW[m]])
            lerp(loT[m], raw, b_loT[m], bW[m], 24 + m, 96, 3072 + 96 * m, 3072 + 96 * (m + 1))
            if m < 2:
                tr.op("scalar", lambda e, m=m: e.activation(loT[m], loT[m], AF.Tanh), reads=[b_loT[m]], writes=[b_loT[m]])
        lw = [A.alloc(128, parts=96) for _ in range(4)]; b_lw = [Buf() for _ in range(4)]

        n_tiles_f = 8 if stop_after != "F1" else 1
        if _os0.environ.get("K_FT"):
            n_tiles_f = int(_os0.environ["K_FT"])
        lbank = 0
        tbank = 0
        for j in range(n_tiles_f):
            Wr, Wk_, Wv, Wkk, Wbs = W[0], W[1], W[2], W[3], W[4]
            for qi, dstW in enumerate((0, 1, 2)):
                raw = W[5 + qi]
                r0 = RW0 + qi * 1024 + 128 * j
                tr.dma("sync", raw, uT[r0:r0 + 128, :], writes=[bW[5 + qi]])
                ch0 = qi * 1024 + 128 * j
                lerp(W[dstW], raw, bW[dstW], bW[5 + qi], qi * 8 + j, 128, ch0, ch0 + 128)
            for d in range(2):
                tr.dma("sync", lw[d], w_up_in[d, :, 128 * j:128 * (j + 1)], writes=[b_lw[d]])
                tr.dma("sync", lw[2 + d], a_up_in[d, :, 128 * j:128 * (j + 1)], writes=[b_lw[2 + d]])
            tr.op("vector", lambda e, j=j: e.tensor_scalar(Wkk, Wk_, rpar[:, 32 + j:33 + j], None, ALU.mult),
                  reads=[bW[1], b_rpar], writes=[bW[3]])
            tr.op("gpsimd", lambda e: e.tensor_tensor(W[5], Wkk, Wkk, ALU.mult), reads=[bW[3]], writes=[bW[5]])
            for bi_, (t0, tn) in enumerate(TOK_BLOCKS):
                bank = 4 + bi_ % 2
                tr.op("tensor", lambda e, bank=bank, t0=t0, tn=tn: e.matmul(PS[bank][:, 0:tn], bones, W[5][:, t0:t0 + tn], start=True, stop=True),
                      reads=[bW[5], b_bones], writes=[PSB[bank]])
                tr.op("scalar", lambda e, bank=bank, t0=t0, tn=tn: e.activation(W[6][:, t0:t0 + tn], PS[bank][:, 0:tn], AF.Sqrt, bias=epsc[:, 0:1], scale=1.0),
                      reads=[PSB[bank], b_epsc], writes=[bW[6]])
            tr.op("vector", lambda e: e.reciprocal(W[6], W[6]), reads=[bW[6]], writes=[bW[6]])
            tr.op("gpsimd", lambda e: e.tensor_tensor(Wkk, Wkk, W[6], ALU.mult), reads=[bW[3], bW[6]], writes=[bW[3]])
            def transposes_out(srcW, bsrc, dst_fn, nm):
                nonlocal tbank
                for hf in range(2):
                    sg_ = stg[tbank % 2]; bsg = b_stg[tbank % 2]
                    for c4 in range(5):
                        cs = list(range(hf * 18 + c4 * 4, min(hf * 18 + c4 * 4 + 4, hf * 18 + 18)))
                        if not cs:
                            continue
                        bank = 6 + (c4 % 2)
                        for qi_, n_ in enumerate(cs):
                            tr.op("tensor", lambda e, bank=bank, qi_=qi_, n_=n_: e.transpose(PS[bank][0:64, qi_ * 128:(qi_ + 1) * 128],
                                                                                          srcW[:, n_ * 64:(n_ + 1) * 64], ident),
                                  reads=[bsrc, b_ident], writes=[PSB[bank]])
                        l0 = (cs[0] - hf * 18) * 128
                        tr.op("vector", lambda e, bank=bank, l0=l0, ncs=len(cs), sg_=sg_: e.tensor_copy(sg_[:, l0:l0 + ncs * 128], PS[bank][0:64, 0:ncs * 128]),
                              reads=[PSB[bank]], writes=[bsg])
                    for hh_ in range(2):
                        tr.dma("gpsimd", dst_fn(hf, hh_), sg_.rearrange("p (n h k) -> p n h k", n=18, h=2)[:, :, hh_, :], reads=[bsg])
                    tbank += 1

            transposes_out(Wv, bW[2], lambda hf, hh_: v_tm[hf * 18:(hf + 1) * 18, :, 2 * j + hh_, :].rearrange("n p k -> p n k"), "v")
            for d in range(2):
                Wld, Wcum, Wal, WE, Wb, Wkd, Wo1, Wo2 = W[5], W[6], W[7], W[8], W[9], W[10], W[11], W[12]
                bld, bcum, bal, bE, bb_, bkd, bo1, bo2 = bW[5], bW[6], bW[7], bW[8], bW[9], bW[10], bW[11], bW[12]
                for (t0, tn) in TOK_BLOCKS:
                    bank = lbank % 3; lbank += 1
                    tr.op("tensor", lambda e, bank=bank, t0=t0, tn=tn, d=d: e.matmul(PS[bank][:, 0:tn], lw[d], loT[d][:, t0:t0 + tn], start=True, stop=True),
                          reads=[b_lw[d], b_loT[d]], writes=[PSB[bank]])
                    tr.op("scalar", lambda e, bank=bank, t0=t0, tn=tn, d=d, j=j: e.activation(Wld[:, t0:t0 + tn], PS[bank][:, 0:tn], AF.Sigmoid,
                                                                                           bias=rpar[:, d * 8 + j:d * 8 + j + 1], scale=1.0),
                          reads=[PSB[bank], b_rpar], writes=[bld])
                    bank = lbank % 3; lbank += 1
                    tr.op("tensor", lambda e, bank=bank, t0=t0, tn=tn, d=d: e.matmul(PS[bank][:, 0:tn], lw[2 + d], loT[2 + d][:, t0:t0 + tn], start=True, stop=True),
                          reads=[b_lw[2 + d], b_loT[2 + d]], writes=[PSB[bank]])
                    tr.op("scalar", lambda e, bank=bank, t0=t0, tn=tn, d=d, j=j: e.activation(Wal[:, t0:t0 + tn], PS[bank][:, 0:tn], AF.Sigmoid,
                                                                                           bias=rpar[:, 16 + d * 8 + j:16 + d * 8 + j + 1], scale=1.0),
                          reads=[PSB[bank], b_rpar], writes=[bal])
                tr.op("gpsimd", lambda e: e.tensor_scalar(Wld, Wld, NEG_E, None, ALU.mult), reads=[bld], writes=[bld])
                tr.op("vector", lambda e: e.tensor_tensor_scan(Wcum, rmask, Wld, 0.0, ALU.mult, ALU.add), reads=[b_rmask, bld], writes=[bcum])
                cum3 = Wcum.rearrange("p (n t) -> p n t", t=64)
                if d == 1:
                    tot_bc = cum3[:, :, 63:64].broadcast_to([128, NCH, 64])
                    tr.op("vector", lambda e: e.tensor_tensor(WE.rearrange("p (n t) -> p n t", t=64), tot_bc, cum3, ALU.subtract), reads=[bcum], writes=[bE])
                    tr.op("gpsimd", lambda e: e.tensor_tensor(Wcum, WE, Wld, ALU.add), reads=[bE, bld], writes=[bcum])
                last = 63 if d == 0 else 0
                cumC_bc = cum3[:, :, last:last + 1].broadcast_to([128, NCH, 64])
                tr.op("gpsimd", lambda e: e.tensor_tensor(Wb, Wkk, Wal, ALU.mult), reads=[bW[3], bal], writes=[bb_])
                tr.op("vector", lambda e, j=j: e.tensor_scalar(Wal, Wal, rpar[:, 40 + j:41 + j], oneska[:, j:j + 1], ALU.mult, ALU.add),
                      reads=[bal, b_rpar, b_oneska], writes=[bal])
                tr.op("gpsimd", lambda e: e.tensor_tensor(Wkd, Wk_, Wal, ALU.mult), reads=[bW[1], bal], writes=[bkd])
                if d == 0:
                    tr.op("vector", lambda e, j=j: e.scalar_tensor_tensor(Wbs, Wr, rpar[:, 48 + j:49 + j], Wkd, ALU.mult, ALU.mult),
                          reads=[bW[0], b_rpar, bkd], writes=[bW[4]])
                else:
                    tr.op("vector", lambda e, j=j: e.scalar_tensor_tensor(Wal, Wr, rpar[:, 48 + j:49 + j], Wkd, ALU.mult, ALU.mult),
                          reads=[bW[0], b_rpar, bkd], writes=[bal])
                    tr.op("gpsimd", lambda e: e.tensor_tensor(Wbs, Wbs, Wal, ALU.add), reads=[bW[4], bal], writes=[bW[4]])

                def fm_out(srcW, bsrc, var, d=d, j=j):
                    for hh in range(2):
                        tr.dma("gpsimd", rw_fm[d, :, :, 2 * j + hh, var, :].rearrange("n p t -> p n t"),
                               srcW[hh * 64:(hh + 1) * 64, :].rearrange("p (n t) -> p n t", t=64), reads=[bsrc])

                tr.op("scalar", lambda e: e.activation(WE, Wcum, AF.Exp), reads=[bcum], writes=[bE])
                tr.op("gpsimd", lambda e: e.tensor_tensor(Wo1, Wr, WE, ALU.mult), reads=[bW[0], bE], writes=[bo1])
                fm_out(Wo1, bo1, 1)
                tr.op("gpsimd", lambda e: e.tensor_tensor(Wld, Wcum, Wld, ALU.subtract), reads=[bcum, bld], writes=[bld])
                tr.op("scalar", lambda e: e.activation(WE, Wld, AF.Exp), reads=[bld], writes=[bE])
                tr.op("vector", lambda e: e.scalar_tensor_tensor(Wo2, Wkk, -1.0, WE, ALU.mult, ALU.mult), reads=[bW[3], bE], writes=[bo2])
                fm_out(Wo2, bo2, 0)
                transposes_out(Wo2, bo2, lambda hf, hh_, d=d: rw_tm[d, hf * 18:(hf + 1) * 18, :, 2 * j + hh_, 0, :].rearrange("n p k -> p n k"), "A")
                tr.op("scalar", lambda e: e.activation(WE, Wcum, AF.Exp, scale=-1.0), reads=[bcum], writes=[bE])
                tr.op("gpsimd", lambda e: e.tensor_tensor(Wo1, Wb, WE, ALU.mult), reads=[bb_, bE], writes=[bo1])
                fm_out(Wo1, bo1, 2)
                tr.op("vector", lambda e: e.tensor_tensor(Wo2, Wkd, WE, ALU.mult), reads=[bkd, bE], writes=[bo2])
                fm_out(Wo2, bo2, 3)
                tr.op("vector", lambda e, cumC_bc=cumC_bc: e.tensor_tensor(Wld.rearrange("p (n t) -> p n t", t=64), cumC_bc, cum3, ALU.subtract), reads=[bcum], writes=[bld])
                tr.op("scalar", lambda e: e.activation(WE, Wld, AF.Exp), reads=[bld], writes=[bE])
                tr.op("gpsimd", lambda e: e.tensor_tensor(Wo1, Wb, WE, ALU.mult), reads=[bb_, bE], writes=[bo1])
                transposes_out(Wo1, bo1, lambda hf, hh_, d=d: rw_tm[d, hf * 18:(hf + 1) * 18, :, 2 * j + hh_, 1, :].rearrange("n p k -> p n k"), "B")
                tr.op("vector", lambda e: e.tensor_tensor(Wo2, Wkd, WE, ALU.mult), reads=[bkd, bE], writes=[bo2])
                transposes_out(Wo2, bo2, lambda hf, hh_, d=d: rw_tm[d, hf * 18:(hf + 1) * 18, :, 2 * j + hh_, 2, :].rearrange("n p k -> p n k"), "K")
                tr.op("scalar", lambda e, cumC_bc=cumC_bc: e.activation(WE.rearrange("p (n t) -> p n t", t=64), cumC_bc, AF.Exp), reads=[bcum], writes=[bE])
                fm_out(WE, bE, 4)
            for bi_, (t0, tn) in enumerate(TOK_BLOCKS):
                if t0 + tn <= CTX:
                    continue
                tr.op("tensor", lambda e, t0=t0, tn=tn: e.matmul(PS[3][:, 0:tn], bones, Wbs[:, t0:t0 + tn], start=True, stop=True),
                      reads=[bW[4], b_bones], writes=[PSB[3]])
                tr.op("vector", lambda e, t0=t0, tn=tn: e.tensor_tensor(W[11][:, t0:t0 + tn], PS[3][:, 0:tn], Wv[:, t0:t0 + tn], ALU.mult),
                      reads=[PSB[3], bW[2]], writes=[bW[11]])
            tr.dma("gpsimd", bonus_s[128 * j:128 * (j + 1), :], W[11][:, CTX:T], reads=[bW[11]])
        tr.barrier()
        A.reset(rw_mark)
        if stop_after in ("F", "F1"):
            tr.run()
            return nc

        rmk = A.alloc(3 * 128); b_rmk = Buf()
        tr.dma("sync", rmk, rmask_in.rearrange("p a b -> p (a b)"), writes=[b_rmk])
        rmkv = rmk.rearrange("p (a b) -> p a b", a=3)
        RL = []
        for sl in range(4):
            L = {}
            L["in"] = []
            for par in range(2):
                L["in"].append(dict(fm=A.alloc(4 * 5 * 64, parts=64), At=A.alloc(256, parts=64), Kt=A.alloc(256), X=A.alloc(256),
                                    Vt=A.alloc(256, parts=64), bfm=Buf(), bAt=Buf(), bKt=Buf(), bX=Buf(), bVt=Buf()))
            L["Mt"] = A.alloc(512); L["b_Mt"] = Buf()
            for nm in ("osb", "AakT"):
                L[nm] = A.alloc(256, parts=64); L["b_" + nm] = Buf()
            for nm in ("P0", "P1", "PT0", "PT1", "Y"):
                L[nm] = neu_alloc(A, 256); L["b_" + nm] = Buf()
            for nm in ("M1", "Atb", "Yb"):
                L[nm] = A.alloc(128, parts=64, dt=BF16); L["b_" + nm] = Buf()
            L["Wk"] = A.alloc(256, parts=64, dt=BF16); L["b_Wk"] = Buf()
            L["vn"] = A.alloc(128, dt=BF16); L["b_vn"] = Buf()
            L["Rb"] = A.alloc(128, parts=64, dt=BF16); L["b_Rb"] = Buf()
            L["ATb"] = A.alloc(128, dt=BF16); L["b_ATb"] = Buf()
            L["Ktb"] = A.alloc(128, dt=BF16); L["b_Ktb"] = Buf()
            L["S"] = A.alloc(512, parts=64); L["b_S"] = [Buf() for _ in range(2)]
            L["Sb"] = A.alloc(256, parts=64, dt=BF16); L["b_Sb"] = [Buf() for _ in range(2)]
            RL.append(L)
            tr.op("vector", lambda e, L=L: e.memset(L["S"], 0.0), writes=L["b_S"])
            tr.op("vector", lambda e, L=L: e.memset(L["Sb"], 0.0), writes=L["b_Sb"])
            tr.op("vector", lambda e, L=L: e.memset(L["Wk"], 0.0), writes=[L["b_Wk"]])
        rpass_ctr = [0, 0, 0, 0]

        def rwkv_pass(s, d, hq, sl):
            L = RL[sl]
            HB = HBB[sl]
            n = gorder[d][s]
            lat = n >= NCH_CTX
            h0 = 4 * hq
            par = rpass_ctr[sl] % 2
            rpass_ctr[sl] += 1
            I = L["in"][par]
            fm = I["fm"].rearrange("p (h v t) -> p h v t", h=4, v=5)
            At3, Kt3, X3 = v3(I["At"], 4), v3(I["Kt"], 4), v3(I["X"], 4)
            tr.dma("sync", fm, rw_fm[d, n, :, h0:h0 + 4, :, :], writes=[I["bfm"]])
            tr.dma("sync", At3, rw_tm[d, n, :, h0:h0 + 4, 0, :], writes=[I["bAt"]])
            tr.dma("sync", Kt3[0:64], rw_tm[d, n, :, h0:h0 + 4, 1, :], writes=[I["bKt"]])
            tr.dma("sync", Kt3[64:128], rw_tm[d, n, :, h0:h0 + 4, 2, :], writes=[I["bKt"]])
            tr.dma("sync", X3[64:128], v_tm[n, :, h0:h0 + 4, :], writes=[I["bX"]])
            Vt3 = v3(I["Vt"], 4)
            tr.dma("sync", Vt3, v_tm[n, :, h0:h0 + 4, :], writes=[I["bVt"]])
            for h in range(4):
                tr.op("tensor", lambda e, h=h: e.matmul(hbank(sl, 0)[:, h * 128:(h + 1) * 128],
                                                        fm[:, h, 2:4, :].rearrange("p v t -> p (v t)"),
                                                        fm[:, h, 0:2, :].rearrange("p v t -> p (v t)"), start=True, stop=True),
                      reads=[I["bfm"]], writes=[HB[0], HB[1]])
            for h in range(4):
                tr.op("tensor", lambda e, h=h: e.matmul(hb(sl, 2)[0:64, h * 64:(h + 1) * 64], fm[:, h, 0, :], fm[:, h, 2, :], start=True, stop=True),
                      reads=[I["bfm"]], writes=[HB[2]])
            for h in range(4):
                tr.op("tensor", lambda e, h=h: e.matmul(hb(sl, 3)[0:64, h * 64:(h + 1) * 64], fm[:, h, 3, :], fm[:, h, 0, :], start=True, stop=True),
                      reads=[I["bfm"]], writes=[HB[3]])
            yield
            Mt4 = v3(L["Mt"], 4)
            tr.op("vector", lambda e: e.tensor_tensor(v3(L["AakT"], 4), v3(hb(sl, 3)[0:64, :], 4),
                                                      rmkv[0:64, d, 0:64].unsqueeze(1).broadcast_to([64, 4, 64]), ALU.mult),
                  reads=[HB[3], b_rmk], writes=[L["b_AakT"]])
            tr.op("vector", lambda e: e.tensor_tensor(Mt4, v3(hbank(sl, 0), 4), rmkv[:, d, :].unsqueeze(1).broadcast_to([128, 4, 128]), ALU.mult),
                  reads=[HB[0], HB[1], b_rmk], writes=[L["b_Mt"]])
            tr.op("vector", lambda e: e.tensor_tensor(mv(v3(L["P0"], 4)), v3(hb(sl, 2)[0:64, :], 4),
                                                      rmkv[0:64, 2, 64 * d:64 * d + 64].unsqueeze(1).broadcast_to([64, 4, 64]), ALU.mult),
                  reads=[HB[2], b_rmk], writes=[L["b_P0"]])
            tr.op(POOL_ENG, lambda e: e.tensor_copy(mv(v3(L["PT0"], 4)), Mt4[0:64, :, 0:64]), reads=[L["b_Mt"]], writes=[L["b_PT0"]])
            tr.op(POOL_ENG, lambda e: e.tensor_tensor(mv(v3(L["Y"], 4)), Mt4[0:64, :, 0:64], ident64_bc, ALU.add),
                  reads=[L["b_Mt"], b_ident], writes=[L["b_Y"]])
            Y3 = v3(L["Y"], 4)
            yield from neumann(sl, L, HB, 0, 1, 2)
            Atb3 = v3(L["Atb"], 4)
            tr.op("scalar", lambda e: e.copy(mv(L["Atb"]), I["At"]), reads=[I["bAt"]], writes=[L["b_Atb"]])
            tr.op("scalar", lambda e: e.copy(L["Yb"], L["Y"]), reads=[L["b_Y"]], writes=[L["b_Yb"]])
            Yb3 = v3(L["Yb"], 4)
            for h in range(4):
                tr.op("tensor", lambda e, h=h: e.matmul(hb(sl, 3)[0:64, h * 64:(h + 1) * 64], Atb3[:, h, :], Yb3[:, h, :], start=True, stop=True),
                      reads=[L["b_Atb"], L["b_Yb"]], writes=[HB[3]])
            for h in range(4):
                tr.op("tensor", lambda e, h=h: e.matmul(hb(sl, 0)[0:64, h * 64:(h + 1) * 64], v3(L["AakT"], 4)[:, h, :], Vt3[:, h, :], start=True, stop=True),
                      reads=[L["b_AakT"], I["bVt"]], writes=[HB[0]])
            yield
            Wk4 = L["Wk"].rearrange("p (h c) -> p h c", h=4)
            tr.op("vector", lambda e: e.tensor_copy(Wk4[:, :, 0:64], v3(hb(sl, 3)[0:64, :], 4)),
                  reads=[HB[3]], writes=[L["b_Wk"]])
            tr.op("vector", lambda e: e.tensor_copy(mv(L["M1"]), hb(sl, 0)[0:64, :]), reads=[HB[0]], writes=[L["b_M1"]])
            M13 = v3(L["M1"], 4)
            for h in range(4):
                tr.op("tensor", lambda e, h=h: e.matmul(hb(sl, 1)[0:64, h * 64:(h + 1) * 64], Yb3[:, h, :], M13[:, h, :], start=True, stop=True),
                      reads=[L["b_Yb"], L["b_M1"]], writes=[HB[1]])
            yield
            tr.op("vector", lambda e: e.tensor_copy(I["X"][0:64, :], hb(sl, 1)[0:64, :]), reads=[HB[1]], writes=[I["bX"]])
            S = L["S"][:, (hq // 2) * 256:(hq // 2 + 1) * 256]
            bS = L["b_S"][hq // 2]
            S3 = v3(S, 4)
            Sb = L["Sb"][:, (hq // 2) * 256:(hq // 2 + 1) * 256]
            bSb = L["b_Sb"][hq // 2]
            Sb3 = v3(Sb, 4)
            vn3 = v3(L["vn"], 4)
            Rb3, ATb3, Ktb3 = v3(L["Rb"], 4), v3(L["ATb"], 4), v3(L["Ktb"], 4)
            tr.op(POOL_ENG, lambda e: e.tensor_copy(Rb3, fm[:, :, 1, :]), reads=[I["bfm"]], writes=[L["b_Rb"]])
            tr.op(POOL_ENG, lambda e: e.tensor_copy(ATb3, Mt4[:, :, 64:128]), reads=[L["b_Mt"]], writes=[L["b_ATb"]])
            tr.op(POOL_ENG, lambda e: e.tensor_copy(L["Ktb"], I["Kt"]), reads=[I["bKt"]], writes=[L["b_Ktb"]])
            for h in range(4):
                tr.op("tensor", lambda e, h=h: e.matmul(hb(sl, 2)[:, h * 64:(h + 1) * 64], Wk4[:, h, :], Sb3[:, h, :], start=True, stop=True),
                      reads=[L["b_Wk"], bSb], writes=[HB[2]])
            yield
            tr.op("vector", lambda e: e.tensor_tensor(L["vn"], hb(sl, 2), I["X"], ALU.add), reads=[HB[2], I["bX"]], writes=[L["b_vn"]])
            if lat:
                for h in range(4):
                    tr.op("tensor", lambda e, h=h: e.matmul(hb(sl, 3)[0:64, h * 64:(h + 1) * 64], Rb3[:, h, :], Sb3[:, h, :], start=True, stop=False),
                          reads=[L["b_Rb"], bSb], writes=[HB[3]])
                    tr.op("tensor", lambda e, h=h: e.matmul(hb(sl, 3)[0:64, h * 64:(h + 1) * 64], ATb3[:, h, :], vn3[:, h, :], start=False, stop=True),
                          reads=[L["b_ATb"], L["b_vn"]], writes=[HB[3]])
            for h in range(4):
                tr.op("tensor", lambda e, h=h: e.matmul(hb(sl, 0)[0:64, h * 64:(h + 1) * 64], Ktb3[:, h, :], vn3[:, h, :], start=True, stop=True),
                      reads=[L["b_Ktb"], L["b_vn"]], writes=[HB[0]])
            yield
            tr.op(POOL_ENG, lambda e: e.tensor_tensor(S3, S3, fm[:, :, 4, :], ALU.mult), reads=[bS, I["bfm"]], writes=[bS])
            tr.op("vector", lambda e: e.tensor_tensor(S, S, hb(sl, 0)[0:64, :], ALU.add), reads=[bS, HB[0]], writes=[bS])
            tr.op("scalar", lambda e: e.copy(Sb, S), reads=[bS], writes=[bSb])
            if lat:
                tr.op("vector", lambda e: e.tensor_copy(L["osb"], hb(sl, 3)[0:64, :]), reads=[HB[3]], writes=[L["b_osb"]])
                tok0 = (n - NCH_CTX) * 64
                tr.dma("gpsimd", o_s[d, tok0:tok0 + 64, 1024 + h0 * 64:1024 + (h0 + 4) * 64], L["osb"], reads=[L["b_osb"]])
            yield

        def rwkv_stream(d, par_, nsteps):
            for s in range(nsteps):
                for hq in (par_, par_ + 2)[:int(_os0.environ.get("K_HQ", "2"))]:
                    yield from rwkv_pass(s, d, hq, 2 * d + par_)

        nsteps_r = NCH if stop_after != "G1" else 6
        run_slots([rwkv_stream(0, 0, nsteps_r), rwkv_stream(1, 0, nsteps_r), rwkv_stream(0, 1, nsteps_r), rwkv_stream(1, 1, nsteps_r)])
        tr.barrier()
        A.reset(gdn_mark)
        if stop_after in ("G", "G1"):
            tr.run()
            return nc

        wout_bf = A.alloc(16 * D // 2, dt=BF16); b_wout = Buf()
        woutv = wout_bf.rearrange("p (k c) -> p k c", k=16)
        wstg = [A.alloc(2 * D), A.alloc(2 * D)]; b_wstg = [Buf(), Buf()]
        for g in range(8):
            p = g % 2
            tr.dma("sync", wstg[p].rearrange("p (k c) -> p k c", k=2),
                   w_out_in[g * 256:(g + 1) * 256, :].rearrange("(k p) c -> p k c", p=128), writes=[b_wstg[p]])
            eng = "vector" if g % 2 == 0 else POOL_ENG
            tr.op(eng, lambda e, g=g, p=p: e.tensor_copy(woutv[:, 2 * g:2 * g + 2, :], wstg[p].rearrange("p (k c) -> p k c", k=2)),
                  reads=[b_wstg[p]], writes=[b_wout])
        gng_bc = A.alloc(128); b_gng = Buf()
        gn_g_bc = A.alloc(1024); gn_b_bc = A.alloc(1024); fg_bc = A.alloc(D); b_gnc = Buf()
        tr.dma("sync", gng_bc, gng_in.partition_broadcast(128), writes=[b_gng])
        tr.dma("sync", gn_g_bc, gn_g_in.partition_broadcast(128), writes=[b_gnc])
        tr.dma("sync", gn_b_bc, gn_b_in.partition_broadcast(128), writes=[b_gnc])
        tr.dma("sync", fg_bc, fg_in.partition_broadcast(128), writes=[b_gnc])
        of_ = A.alloc(D); ob_ = A.alloc(D); tmpH = A.alloc(D); zT = A.alloc(D); bon = A.alloc(1024)
        xt_ = A.alloc(D); res = A.alloc(D)
        b_of, b_ob, b_tmpH, b_zT, b_bon, b_xt, b_res = Buf(), Buf(), Buf(), Buf(), Buf(), Buf(), Buf()
        YT = A.alloc(16 * 128 // 2, dt=BF16); b_YT = Buf()
        YTv = YT.rearrange("p (k t) -> p k t", k=16)
        st = A.alloc(64); b_st = Buf()
        n_tt = 16 if stop_after != "H1" else 1
        for tt in range(n_tt):
            tok0 = tt * 128
            tr.dma("sync", of_, o_s[0, tok0:tok0 + 128, :], writes=[b_of])
            tr.dma("sync", ob_, o_s[1, tok0:tok0 + 128, :], writes=[b_ob])
            tr.dma("sync", zT[:, 0:1024].rearrange("p (k t) -> p k t", k=8),
                   uT[3072:4096, CTX + tok0:CTX + tok0 + 128].rearrange("(k p) t -> p k t", p=128), writes=[b_zT])
            tr.dma("sync", zT[:, 1024:2048].rearrange("p (k t) -> p k t", k=8),
                   uT[ZR0:ZR0 + 1024, CTX + tok0:CTX + tok0 + 128].rearrange("(k p) t -> p k t", p=128), writes=[b_zT])
            tr.dma("sync", bon.rearrange("p (k t) -> p k t", k=8),
                   bonus_s[:, tok0:tok0 + 128].rearrange("(k p) t -> p k t", p=128), writes=[b_bon])
            tr.dma("sync", xt_, x_res_in[tok0:tok0 + 128, :], writes=[b_xt])
            tr.op(POOL_ENG, lambda e: e.tensor_tensor(of_, of_, ob_, ALU.add), reads=[b_of, b_ob], writes=[b_of])
            tr.op("scalar", lambda e: e.activation(zT, zT, AF.Silu), reads=[b_zT], writes=[b_zT])
            og = of_[:, 0:1024].rearrange("p (h c) -> p h c", h=8)
            tg = tmpH[:, 0:1024].rearrange("p (h c) -> p h c", h=8)
            tr.op("vector", lambda e: e.tensor_tensor(tmpH[:, 0:1024], of_[:, 0:1024], of_[:, 0:1024], ALU.mult), reads=[b_of], writes=[b_tmpH])
            tr.op("vector", lambda e: e.tensor_reduce(st[:, 0:8], tg, AX.X, ALU.add), reads=[b_tmpH], writes=[b_st])
            tr.op("scalar", lambda e: e.activation(st[:, 0:8], st[:, 0:8], AF.Sqrt, bias=epsc[:, 0:1], scale=1.0 / 128), reads=[b_st, b_epsc], writes=[b_st])
            tr.op("vector", lambda e: e.reciprocal(st[:, 0:8], st[:, 0:8]), reads=[b_st], writes=[b_st])
            tr.op("vector", lambda e: e.tensor_tensor(tg, og, st[:, 0:8].unsqueeze(2).broadcast_to([128, 8, 128]), ALU.mult), reads=[b_of, b_st], writes=[b_tmpH])
            tr.op(POOL_ENG, lambda e: e.tensor_tensor(tg, tg, gng_bc.unsqueeze(1).broadcast_to([128, 8, 128]), ALU.mult), reads=[b_tmpH, b_gng], writes=[b_tmpH])
            orr = of_[:, 1024:2048].rearrange("p (h c) -> p h c", h=16)
            trr = tmpH[:, 1024:2048].rearrange("p (h c) -> p h c", h=16)
            tr.op("vector", lambda e: e.tensor_reduce(st[:, 16:32], orr, AX.X, ALU.add), reads=[b_of], writes=[b_st])
            tr.op("vector", lambda e: e.tensor_scalar(st[:, 16:32], st[:, 16:32], -1.0 / 64, None, ALU.mult), reads=[b_st], writes=[b_st])
            tr.op("vector", lambda e: e.tensor_tensor(orr, orr, st[:, 16:32].unsqueeze(2).broadcast_to([128, 16, 64]), ALU.add), reads=[b_of, b_st], writes=[b_of])
            tr.op("vector", lambda e: e.tensor_tensor(trr, orr, orr, ALU.mult), reads=[b_of], writes=[b_tmpH])
            tr.op("vector", lambda e: e.tensor_reduce(st[:, 32:48], trr, AX.X, ALU.add), reads=[b_tmpH], writes=[b_st])
            tr.op("scalar", lambda e: e.activation(st[:, 32:48], st[:, 32:48], AF.Sqrt, bias=epsc[:, 1:2], scale=1.0 / 64), reads=[b_st, b_epsc], writes=[b_st])
            tr.op("vector", lambda e: e.reciprocal(st[:, 32:48], st[:, 32:48]), reads=[b_st], writes=[b_st])
            tr.op("vector", lambda e: e.tensor_tensor(trr, orr, st[:, 32:48].unsqueeze(2).broadcast_to([128, 16, 64]), ALU.mult), reads=[b_of, b_st], writes=[b_tmpH])
            tr.op(POOL_ENG, lambda e: e.tensor_tensor(tmpH[:, 1024:2048], tmpH[:, 1024:2048], gn_g_bc, ALU.mult), reads=[b_tmpH, b_gnc], writes=[b_tmpH])
            tr.op(POOL_ENG, lambda e: e.tensor_tensor(tmpH[:, 1024:2048], tmpH[:, 1024:2048], gn_b_bc, ALU.add), reads=[b_tmpH, b_gnc], writes=[b_tmpH])
            for k in range(16):
                bank = k // 4
                q = k % 4
                tr.op("tensor", lambda e, bank=bank, q=q, k=k: e.transpose(PS[bank][:, q * 128:(q + 1) * 128], tmpH[:, k * 128:(k + 1) * 128], ident),
                      reads=[b_tmpH, b_ident], writes=[PSB[bank]])
            for bank in range(4):
                zsl = zT[:, bank * 512:(bank + 1) * 512]
                dst = YT[:, bank * 512:(bank + 1) * 512]
                if bank < 2:
                    tr.op("vector", lambda e, bank=bank, zsl=zsl, dst=dst: e.tensor_tensor(dst, PS[bank][:, :], zsl, ALU.mult),
                          reads=[PSB[bank], b_zT], writes=[b_YT])
                else:
                    bsl = bon[:, (bank - 2) * 512:(bank - 1) * 512]
                    tr.op("vector", lambda e, bank=bank, bsl=bsl: e.tensor_tensor(res[:, 0:512], PS[bank][:, :], bsl, ALU.add),
                          reads=[PSB[bank], b_bon], writes=[b_res])
                    tr.op(POOL_ENG, lambda e, zsl=zsl, dst=dst: e.tensor_tensor(dst, res[:, 0:512], zsl, ALU.mult),
                          reads=[b_res, b_zT], writes=[b_YT])
            for cb in range(4):
                bank = 4 + cb
                for kt in range(16):
                    tr.op("tensor", lambda e, bank=bank, kt=kt, cb=cb: e.matmul(PS[bank][:, :], YTv[:, kt, :], woutv[:, kt, cb * 512:(cb + 1) * 512],
                                                                              start=(kt == 0), stop=(kt == 15)),
                          reads=[b_YT, b_wout], writes=[PSB[bank]])
                tr.op("vector", lambda e, bank=bank, cb=cb: e.tensor_tensor(res[:, cb * 512:(cb + 1) * 512], PS[bank][:, :],
                                                                           gate_bc[:, cb * 512:(cb + 1) * 512], ALU.mult),
                      reads=[PSB[bank], b_gate], writes=[b_res])
            tr.op(POOL_ENG, lambda e: e.tensor_tensor(res, res, xt_, ALU.add), reads=[b_res, b_xt], writes=[b_res])
            tr.op("vector", lambda e: e.memset(st[:, 48:49], 0.0), writes=[b_st])
            tr.op("scalar", lambda e: e.activation(xt_, res, AF.Square, accum_out=st[:, 48:49]), reads=[b_res, b_st], writes=[b_xt, b_st])
            tr.op("scalar", lambda e: e.activation(st[:, 48:49], st[:, 48:49], AF.Sqrt, bias=epsc[:, 0:1], scale=1.0 / D), reads=[b_st, b_epsc], writes=[b_st])
            tr.op("vector", lambda e: e.reciprocal(st[:, 48:49], st[:, 48:49]), reads=[b_st], writes=[b_st])
            tr.op("vector", lambda e: e.scalar_tensor_tensor(res, res, st[:, 48:49], fg_bc, ALU.mult, ALU.mult), reads=[b_res, b_st, b_gnc], writes=[b_res])
            tr.dma("gpsimd", out_ap[tok0:tok0 + 128, :], res, reads=[b_res])
        tr.barrier()
        tr.run()
    return nc


def prep_inputs(inputs, b):
    f = lambda a: np.ascontiguousarray(a, dtype=np.float32)
    c = inputs["c"][b]
    cc = np.stack([c.reshape(16, 128).T, inputs["c_ctx"].reshape(16, 128).T], axis=-1)
    m = {
        "x": f(inputs["x"][b]),
        "ctx": f(inputs["ctx"][b]),
        "cc": f(cc),
        "w_ada": f(inputs["w_ada"][0]),
        "b_ada2": f(np.stack([inputs["b_ada"][0], inputs["b_ada"][0]], 0)),
        "normg_T": f(inputs["norm_g"][0].reshape(16, 128).T),
        "w_in": f(inputs["w_in"][0]),
        "ident": np.eye(128, dtype=np.float32),
    }
    cw = inputs["gdn_conv_w"][0]
    m["conv_wT"] = f(cw.reshape(5, 24, 128).transpose(2, 1, 0))
    gpar = np.zeros((32, 8), np.float32)
    gpar[0:16, 0] = inputs["gdn_dt_bias"][0].reshape(16)
    gpar[0:16, 1] = inputs["gdn_a_log"][0].reshape(16)
    gpar[0:16, 2] = 1.0
    gpar[16:32, 3] = 1.0
    gpar[:, 4] = 1.0
    m["gpar"] = gpar
    idx = np.arange(64)
    P = idx[:, None]; Fq = idx[None, :]
    NEG = -30000.0
    gm = np.zeros((64, 6, 64), np.float32)
    gm[:, 0, :] = np.where(P > Fq, 0.0, NEG)
    gm[:, 1, :] = np.where(Fq >= P, 0.0, NEG)
    gm[:, 2, :] = np.where(P < Fq, 0.0, NEG)
    gm[:, 3, :] = np.where(Fq <= P, 0.0, NEG)
    gm[:, 4, :] = (P <= Fq).astype(np.float32)
    gm[:, 5, :] = (P >= Fq).astype(np.float32)
    m["gmask"] = gm
    mu = inputs["rwkv_mu"][0]
    muT = np.zeros((128, 28), np.float32)
    muT[:, 0:24] = mu[0:3072].reshape(24, 128).T
    for g4 in range(4):
        muT[0:96, 24 + g4] = mu[3072 + 96 * g4:3072 + 96 * (g4 + 1)]
    m["muT"] = muT
    ch = np.zeros((128, 28), np.int64) - 1
    ch[:, 0:24] = np.arange(3072).reshape(24, 128).T
    for g4 in range(4):
        ch[0:96, 24 + g4] = 3072 + 96 * g4 + np.arange(96)
    sm = np.zeros((128, 28, 6), np.float32)
    qq = ch // 864
    hh = ch // 1728
    for ci in range(4):
        sm[:, :, ci] = ((qq == ci) & (ch >= 0)).astype(np.float32)
    sm[:, :, 4] = ((hh == 0) & (ch >= 0)).astype(np.float32)
    sm[:, :, 5] = ((hh == 1) & (ch >= 0)).astype(np.float32)
    m["smask"] = sm
    rp = np.zeros((128, 56), np.float32)
    rp[:, 0:16] = inputs["rwkv_w0"][0].reshape(2, 8, 128).transpose(2, 0, 1).reshape(128, 16)
    rp[:, 16:32] = inputs["rwkv_a0"][0].reshape(2, 8, 128).transpose(2, 0, 1).reshape(128, 16)
    rp[:, 32:40] = inputs["rwkv_k_k"][0].reshape(8, 128).T
    rp[:, 40:48] = inputs["rwkv_k_a"][0].reshape(8, 128).T
    rp[:, 48:56] = inputs["rwkv_r_k"][0].reshape(8, 128).T
    m["rpar"] = rp
    blk = np.arange(128) // 64
    m["bones"] = (blk[:, None] == blk[None, :]).astype(np.float32)
    m["w_up"] = f(inputs["rwkv_w_up"][0])
    m["a_up"] = f(inputs["rwkv_a_up"][0])
    s_ = np.arange(128)[:, None] % 64
    t_ = np.arange(128)[None, :] % 64
    is_incl = (np.arange(128)[None, :] >= 64)
    rm = np.zeros((128, 3, 128), np.float32)
    rm[:, 0, :] = np.where(is_incl, t_ >= s_, t_ > s_)
    rm[:, 1, :] = np.where(is_incl, t_ <= s_, t_ < s_)
    tt_ = np.arange(64)[:, None]; ss_ = np.arange(64)[None, :]
    rm[0:64, 2, 0:64] = (tt_ > ss_)
    rm[0:64, 2, 64:128] = (tt_ < ss_)
    m["rmask"] = rm
    m["w_out"] = f(inputs["w_out"][0])
    m["gng"] = f(inputs["gdn_norm_g"][0])
    m["gn_g"] = f(inputs["rwkv_gn_g"][0])
    m["gn_b"] = f(inputs["rwkv_gn_b"][0])
    m["fg"] = f(inputs["final_norm_g"])
    return m


def kernel(**inputs):
    inputs = {k: np.asarray(v) for k, v in inputs.items()}
    nc = build()
    in_maps = [prep_inputs(inputs, b) for b in range(8)]
    res = run_bass_kernel_spmd(nc, in_maps, core_ids=list(range(8)))
    out = np.stack([np.asarray(r["out"]) for r in res.results], axis=0)
    return out.astype(np.float32)
```
